# Optimizing a Trainium2 kernel written in Bass

```python
import math
import jax
import jax.numpy as jnp
from jax import lax
import numpy as np

D_MODEL = 1024
BATCH = 8
SEQ = 4096
DEPTH = 2

MEM_LEN = 256
N_HEADS_GROUP = 4
HEAD_DIM = 64
GROUP_WIDTH = N_HEADS_GROUP * HEAD_DIM
N_GROUPS = 5
MIX_WIDTH = N_GROUPS * GROUP_WIDTH
MLA_Q_RANK = 192
MLA_KV_RANK = 128
MLA_NOPE_DIM = 64
MLA_ROPE_DIM = 32
MLA_QK_DIM = MLA_NOPE_DIM + MLA_ROPE_DIM
DIFF_QK_DIM = HEAD_DIM // 2
ROPE_THETA = 500000.0
ROT_MOBA = HEAD_DIM // 4
ROT_DIFF = DIFF_QK_DIM // 4
POS_OFFSET_MAX = 8192
MOBA_BLOCK = 256
MOBA_TOPK = 3
MOBA_Q_CHUNK = 64
Q_BLOCK = 128
D_FF = ((8 * D_MODEL // 3 + 127) // 128) * 128
CONV_WIDTH = 3
EPS = 1e-6
NEG_INF = -1e30
IN_SIZES = (MLA_Q_RANK, MLA_KV_RANK, MLA_ROPE_DIM,
            3 * GROUP_WIDTH, N_HEADS_GROUP,
            3 * GROUP_WIDTH,
            3 * GROUP_WIDTH,
            GROUP_WIDTH)
D_IN = sum(IN_SIZES)

kernel_name = "hybrid_parallel_heads_mla_fox_moba_diff"


def _rms_norm(x, g):
    xf = x.astype(jnp.float32)
    y = xf * lax.rsqrt(jnp.mean(xf * xf, axis=-1, keepdims=True) + EPS)
    return (y * g.astype(jnp.float32)).astype(x.dtype)


def _heads(t, n):
    b, s, w = t.shape
    return t.reshape(b, s, n, w // n).transpose(0, 2, 1, 3)


def _merge(t):
    b, h, s, d = t.shape
    return t.transpose(0, 2, 1, 3).reshape(b, s, h * d)


def _split_pairs(t):
    b, s, _ = t.shape
    return t.reshape(b, s, N_HEADS_GROUP, 2, DIFF_QK_DIM).transpose(3, 0, 2, 1, 4)


def _rope_tables(positions, rot):
    inv = ROPE_THETA ** (-jnp.arange(0, rot, 2, dtype=jnp.float32) / rot)
    ang = positions.astype(jnp.float32)[:, None, :, None] * inv
    return jnp.cos(ang), jnp.sin(ang)


def _rope(x, cos, sin):
    rot = 2 * cos.shape[-1]
    xr = x[..., :rot].astype(jnp.float32)
    x1, x2 = xr[..., :rot // 2], xr[..., rot // 2:]
    r = jnp.concatenate([x1 * cos - x2 * sin, x1 * sin + x2 * cos], axis=-1).astype(x.dtype)
    return jnp.concatenate([r, x[..., rot:]], axis=-1)


def _causal_mask(start, n_q, n_k):
    return jnp.arange(n_k)[None, :] <= (start + jnp.arange(n_q))[:, None]


def _sweep(block_fn, seq, block):
    out = lax.map(block_fn, jnp.arange(seq // block))
    nb, b, h, q, d = out.shape
    return out.transpose(1, 2, 0, 3, 4).reshape(b, h, nb * q, d)


def _dense_causal_attention(q, k, v, scale, log_decay=None):
    seq = q.shape[2]

    def block(i):
        start = i * Q_BLOCK
        qb = lax.dynamic_slice_in_dim(q, start, Q_BLOCK, axis=2)
        s = jnp.einsum('bhqd,bhkd->bhqk', qb, k).astype(jnp.float32) * scale
        if log_decay is not None:
            cq = lax.dynamic_slice_in_dim(log_decay, start, Q_BLOCK, axis=2)
            s = s + (cq[..., :, None] - log_decay[..., None, :])
        s = jnp.where(_causal_mask(start, Q_BLOCK, seq), s, NEG_INF)
        p = jax.nn.softmax(s, axis=-1).astype(v.dtype)
        return jnp.einsum('bhqk,bhkd->bhqd', p, v)

    return _sweep(block, seq, Q_BLOCK)


def _diff_causal_attention(q1, q2, k1, k2, v, lam, scale):
    seq = q1.shape[2]

    def block(i):
        start = i * Q_BLOCK
        mask = _causal_mask(start, Q_BLOCK, seq)

        def probs(q, k):
            qb = lax.dynamic_slice_in_dim(q, start, Q_BLOCK, axis=2)
            s = jnp.einsum('bhqd,bhkd->bhqk', qb, k).astype(jnp.float32) * scale
            return jax.nn.softmax(jnp.where(mask, s, NEG_INF), axis=-1)

        p = (probs(q1, k1) - lam * probs(q2, k2)).astype(v.dtype)
        return jnp.einsum('bhqk,bhkd->bhqd', p, v)

    return _sweep(block, seq, Q_BLOCK)


def _moba_attention(q, k, v, scale):
    b, h, seq, d = q.shape
    nb = -(-seq // MOBA_BLOCK)
    pad = nb * MOBA_BLOCK - seq
    kp = jnp.pad(k, ((0, 0), (0, 0), (0, pad), (0, 0)))
    vp = jnp.pad(v, ((0, 0), (0, 0), (0, pad), (0, 0)))
    kb = kp.reshape(b, h, nb, MOBA_BLOCK, d)
    vb = vp.reshape(b, h, nb, MOBA_BLOCK, d)
    k_mean = jnp.mean(kb, axis=3)
    gate = jnp.einsum('bhsd,bhnd->bhsn', q, k_mean).astype(jnp.float32)
    q_blk = jnp.arange(seq) // MOBA_BLOCK
    past = jnp.arange(nb)[None, :] < q_blk[:, None]
    gate = jnp.where(past, gate, NEG_INF)
    topk = min(MOBA_TOPK, nb)
    _, sel = lax.top_k(gate, topk)
    sel_ok = sel < q_blk[:, None]
    bi = jnp.arange(b)[:, None, None, None]
    hi = jnp.arange(h)[None, :, None, None]
    n_sel = topk * MOBA_BLOCK

    def chunk(i):
        start = i * MOBA_Q_CHUNK
        qc = lax.dynamic_slice_in_dim(q, start, MOBA_Q_CHUNK, axis=2)
        idx = lax.dynamic_slice_in_dim(sel, start, MOBA_Q_CHUNK, axis=2)
        ok = lax.dynamic_slice_in_dim(sel_ok, start, MOBA_Q_CHUNK, axis=2)
        kg = kb[bi, hi, idx]
        vg = vb[bi, hi, idx]
        s_past = jnp.einsum('bhqd,bhqnkd->bhqnk', qc, kg).astype(jnp.float32) * scale
        s_past = jnp.where(ok[..., None], s_past, NEG_INF).reshape(b, h, MOBA_Q_CHUNK, n_sel)
        own = (start // MOBA_BLOCK) * MOBA_BLOCK
        k_own = lax.dynamic_slice_in_dim(kp, own, MOBA_BLOCK, axis=2)
        v_own = lax.dynamic_slice_in_dim(vp, own, MOBA_BLOCK, axis=2)
        s_own = jnp.einsum('bhqd,bhkd->bhqk', qc, k_own).astype(jnp.float32) * scale
        causal = (own + jnp.arange(MOBA_BLOCK))[None, :] <= (start + jnp.arange(MOBA_Q_CHUNK))[:, None]
        s_own = jnp.where(causal, s_own, NEG_INF)
        p = jax.nn.softmax(jnp.concatenate([s_past, s_own], axis=-1), axis=-1).astype(v.dtype)
        p_past = p[..., :n_sel].reshape(b, h, MOBA_Q_CHUNK, topk, MOBA_BLOCK)
        return (jnp.einsum('bhqnk,bhqnkd->bhqd', p_past, vg)
                + jnp.einsum('bhqk,bhkd->bhqd', p[..., n_sel:], v_own))

    return _sweep(chunk, seq, MOBA_Q_CHUNK)


def _memory_attention(q, k, v, scale):
    s = jnp.einsum('bhqd,bhmd->bhqm', q, k).astype(jnp.float32) * scale
    p = jax.nn.softmax(s, axis=-1).astype(v.dtype)
    return jnp.einsum('bhqm,bhmd->bhqd', p, v)


def _causal_dwconv(h, w, bias):
    seq = h.shape[1]
    hp = jnp.pad(h, ((0, 0), (CONV_WIDTH - 1, 0), (0, 0)))
    y = bias
    for j in range(CONV_WIDTH):
        y = y + w[j] * hp[:, j:j + seq]
    return y


def setup_inputs(seed: int = 0) -> dict:
    key = jax.random.key(seed)
    ks = iter(jax.random.split(key, 40))
    L = DEPTH

    def nrm(shape, scale):
        return scale * jax.random.normal(next(ks), shape, jnp.float32)

    def gain(shape):
        return 1.0 + 0.05 * jax.random.normal(next(ks), shape, jnp.float32)

    x = nrm((BATCH, SEQ, D_MODEL), 1.0)
    mem = nrm((BATCH, MEM_LEN, D_MODEL), 1.0)
    offset = jax.random.randint(next(ks), (BATCH, 1), 0, POS_OFFSET_MAX, dtype=jnp.int32)
    positions = offset + jnp.arange(SEQ, dtype=jnp.int32)[None, :]
    return {
        "x": x,
        "mem": mem,
        "positions": positions,
        "attn_norm": gain((L, D_MODEL)),
        "ffn_norm": gain((L, D_MODEL)),
        "mem_norm": gain((L, D_MODEL)),
        "w_in": nrm((L, D_MODEL, D_IN), D_MODEL ** -0.5),
        "mla_cq_norm": gain((L, MLA_Q_RANK)),
        "mla_ckv_norm": gain((L, MLA_KV_RANK)),
        "mla_w_uq": nrm((L, MLA_Q_RANK, N_HEADS_GROUP * MLA_QK_DIM), MLA_Q_RANK ** -0.5),
        "mla_w_ukv": nrm((L, MLA_KV_RANK, N_HEADS_GROUP * (MLA_NOPE_DIM + HEAD_DIM)), MLA_KV_RANK ** -0.5),
        "mla_q_norm": gain((L, MLA_QK_DIM)),
        "mla_k_norm": gain((L, MLA_QK_DIM)),
        "fox_b_f": 3.0 + 0.5 * jax.random.normal(next(ks), (L, N_HEADS_GROUP), jnp.float32),
        "fox_q_norm": gain((L, HEAD_DIM)),
        "fox_k_norm": gain((L, HEAD_DIM)),
        "moba_q_norm": gain((L, HEAD_DIM)),
        "moba_k_norm": gain((L, HEAD_DIM)),
        "diff_lambda": nrm((L, 4, DIFF_QK_DIM), 0.1),
        "diff_q_norm": gain((L, DIFF_QK_DIM)),
        "diff_k_norm": gain((L, DIFF_QK_DIM)),
        "diff_sub_norm": gain((L, HEAD_DIM)),
        "mem_w_kv": nrm((L, D_MODEL, 2 * GROUP_WIDTH), D_MODEL ** -0.5),
        "mem_q_norm": gain((L, HEAD_DIM)),
        "mem_k_norm": gain((L, HEAD_DIM)),
        "w_o": nrm((L, MIX_WIDTH, D_MODEL), MIX_WIDTH ** -0.5),
        "ffn_w_gate": nrm((L, D_MODEL, D_FF), D_MODEL ** -0.5),
        "ffn_w_up": nrm((L, D_MODEL, D_FF), D_MODEL ** -0.5),
        "ffn_conv_w": nrm((L, CONV_WIDTH, D_FF), CONV_WIDTH ** -0.5),
        "ffn_conv_b": nrm((L, D_FF), 0.02),
        "ffn_w_down": nrm((L, D_FF, D_MODEL), D_FF ** -0.5),
    }


def reference(x, mem, positions, attn_norm, ffn_norm, mem_norm, w_in,
              mla_cq_norm, mla_ckv_norm, mla_w_uq, mla_w_ukv, mla_q_norm, mla_k_norm,
              fox_b_f, fox_q_norm, fox_k_norm,
              moba_q_norm, moba_k_norm,
              diff_lambda, diff_q_norm, diff_k_norm, diff_sub_norm,
              mem_w_kv, mem_q_norm, mem_k_norm,
              w_o, ffn_w_gate, ffn_w_up, ffn_conv_w, ffn_conv_b, ffn_w_down):
    bsz, seq, _ = x.shape
    R = MLA_ROPE_DIM
    cos_a, sin_a = _rope_tables(positions, MLA_ROPE_DIM)
    cos_c, sin_c = _rope_tables(positions, ROT_MOBA)
    cos_d, sin_d = _rope_tables(positions, ROT_DIFF)
    split_at = np.cumsum(IN_SIZES)[:-1].tolist()

    for l in range(DEPTH):
        xn = _rms_norm(x, attn_norm[l])
        h = xn @ w_in[l]
        cq, ckv, kr, fox_qkv, fox_f, moba_qkv, diff_qkv, mem_q = jnp.split(h, split_at, axis=-1)

        q_a = _heads(_rms_norm(cq, mla_cq_norm[l]) @ mla_w_uq[l], N_HEADS_GROUP)
        kv_a = _heads(_rms_norm(ckv, mla_ckv_norm[l]) @ mla_w_ukv[l], N_HEADS_GROUP)
        gq, gk = mla_q_norm[l], mla_k_norm[l]
        q_a = jnp.concatenate([_rope(_rms_norm(q_a[..., :R], gq[:R]), cos_a, sin_a),
                               _rms_norm(q_a[..., R:], gq[R:])], axis=-1)
        k_nope = _rms_norm(kv_a[..., :MLA_NOPE_DIM], gk[R:])
        k_rope = _rope(_rms_norm(kr[:, None], gk[:R]), cos_a, sin_a)
        k_a = jnp.concatenate([jnp.broadcast_to(k_rope, k_nope.shape[:-1] + (R,)), k_nope], axis=-1)
        o_a = _dense_causal_attention(q_a, k_a, kv_a[..., MLA_NOPE_DIM:], MLA_QK_DIM ** -0.5)

        q_b, k_b, v_b = [_heads(t, N_HEADS_GROUP) for t in jnp.split(fox_qkv, 3, axis=-1)]
        log_f = jax.nn.log_sigmoid(fox_f.astype(jnp.float32) + fox_b_f[l].astype(jnp.float32))
        decay = jnp.cumsum(log_f, axis=1).transpose(0, 2, 1)
        o_b = _dense_causal_attention(_rms_norm(q_b, fox_q_norm[l]), _rms_norm(k_b, fox_k_norm[l]),
                                      v_b, HEAD_DIM ** -0.5, decay)

        q_c, k_c, v_c = [_heads(t, N_HEADS_GROUP) for t in jnp.split(moba_qkv, 3, axis=-1)]
        q_c = _rope(_rms_norm(q_c, moba_q_norm[l]), cos_c, sin_c)
        k_c = _rope(_rms_norm(k_c, moba_k_norm[l]), cos_c, sin_c)
        o_c = _moba_attention(q_c, k_c, v_c, HEAD_DIM ** -0.5)

        dq, dk, dv = jnp.split(diff_qkv, 3, axis=-1)
        q_d = _rope(_rms_norm(_split_pairs(dq), diff_q_norm[l]), cos_d, sin_d)
        k_d = _rope(_rms_norm(_split_pairs(dk), diff_k_norm[l]), cos_d, sin_d)
        lam_vec = diff_lambda[l].astype(jnp.float32)
        lam_init = 0.8 - 0.6 * math.exp(-0.3 * l)
        lam = (jnp.exp(jnp.sum(lam_vec[0] * lam_vec[1]))
               - jnp.exp(jnp.sum(lam_vec[2] * lam_vec[3])) + lam_init)
        o_d = _diff_causal_attention(q_d[0], q_d[1], k_d[0], k_d[1], _heads(dv, N_HEADS_GROUP),
                                     lam, DIFF_QK_DIM ** -0.5)
        o_d = _rms_norm(o_d, diff_sub_norm[l]) * (1.0 - lam_init)

        q_e = _rms_norm(_heads(mem_q, N_HEADS_GROUP), mem_q_norm[l])
        mem_kv = _rms_norm(mem, mem_norm[l]) @ mem_w_kv[l]
        k_e, v_e = [_heads(t, N_HEADS_GROUP) for t in jnp.split(mem_kv, 2, axis=-1)]
        o_e = _memory_attention(q_e, _rms_norm(k_e, mem_k_norm[l]), v_e, HEAD_DIM ** -0.5)

        mixed = jnp.concatenate([_merge(o_a), _merge(o_b), _merge(o_c), _merge(o_d), _merge(o_e)], axis=-1)
        x = x + mixed @ w_o[l]

        xn = _rms_norm(x, ffn_norm[l])
        gate = _causal_dwconv(xn @ ffn_w_gate[l], ffn_conv_w[l], ffn_conv_b[l])
        x = x + (jax.nn.silu(gate) * (xn @ ffn_w_up[l])) @ ffn_w_down[l]
    return x
```

```python
import math
from contextlib import ExitStack
import numpy as np
import concourse.bass as bass
import concourse.mybir as mybir
from concourse.bass_utils import run_bass_kernel_spmd

F32 = mybir.dt.float32
BF16 = mybir.dt.bfloat16
I32 = mybir.dt.int32
AF = mybir.ActivationFunctionType
ALU = mybir.AluOpType
AX = mybir.AxisListType

S = 4096
D = 1024
NT = 32
DIN = 2916
DFF = 2816
NCH = 22
L = 2
EPS = 1e-6
NEG = -30000.0
C_CQ, C_CKV, C_KR, C_FQ, C_FK, C_FV, C_FF = 0, 192, 320, 352, 608, 864, 1120
C_MQ, C_MK, C_MV, C_DQ, C_DK, C_DV, C_EQ = 1124, 1380, 1636, 1892, 2148, 2404, 2660
NBLK = 30
NFREQ = 28
DBG_UNITS = None


class Prog:
    ENGS = ('pe', 'act', 'dve', 'pool', 'sp')

    def __init__(self, nc, es, n_dma=56, epoch=10**9):
        self.nc = nc
        self.es = es
        self.eng = {'pe': nc.tensor, 'act': nc.scalar, 'dve': nc.vector,
                    'pool': nc.gpsimd, 'sp': nc.sync}
        self.EPOCH = epoch
        self.nsem = 0
        self.sem = {e: self._newsem() for e in self.ENGS}
        self.ep = {e: 0 for e in self.ENGS}
        self.cnt = {e: 0 for e in self.ENGS}
        self.known = {e: {} for e in self.ENGS}
        self.known_ep = {e: {} for e in self.ENGS}
        self.semof = {(e, 0): self.sem[e] for e in self.ENGS}
        self.dsem = [self._newsem() for _ in range(n_dma)]
        self.dval = [0] * n_dma
        self.dnext = 0
        self.dnext_sw = 0
        self.NHW = n_dma - 16
        self.known_d = {e: [0] * n_dma for e in self.ENGS}
        self.last_w = {}
        self.readers = {}
        self.nwaits = 0
        self.nops = 0

    def _newsem(self):
        self.nsem += 1
        return self.es.enter_context(self.nc.semaphore(f"s{self.nsem}"))

    def _wait(self, E, tok):
        if tok[0] == 'd':
            _, k, v = tok
            if self.known_d[E][k] >= v:
                return
            self.eng[E].wait_ge(self.dsem[k], v)
            self.known_d[E][k] = v
            self.nwaits += 1
        else:
            _, X, ep, c = tok
            if self.known_ep[E].get(X, -1) > ep:
                return
            if self.known[E].get((X, ep), 0) >= c:
                return
            self.eng[E].wait_ge(self.semof[(X, ep)], c)
            self.known[E][(X, ep)] = c
            if self.known_ep[E].get(X, -1) < ep:
                self.known_ep[E][X] = ep
            self.nwaits += 1

    @staticmethod
    def _k(k):
        if isinstance(k, str):
            return k
        if isinstance(k, tuple):
            return tuple(x if isinstance(x, (int, str)) else x.name for x in k)
        return k.name

    def _deps(self, E, reads, writes, defer_last=False):
        deps = []
        for r in reads:
            w = self.last_w.get(r)
            if w is not None:
                deps.append(w)
        for wk in writes:
            lw = self.last_w.get(wk)
            if lw is not None and not (lw[0] == 'e' and lw[1] == E and E == 'pe'):
                deps.append(lw)
            for rd in self.readers.get(wk, {}).values():
                if not (rd[0] == 'e' and rd[1] == E and E == 'pe'):
                    deps.append(rd)
        if not defer_last:
            for d in deps:
                self._wait(E, d)
            return None
        need = [d for d in deps if self._needed(E, d)]
        for d in need[:-1]:
            self._wait(E, d)
        if need and self._needed(E, need[-1]):
            return need[-1]
        return None

    def _needed(self, E, tok):
        if tok[0] == 'd':
            return self.known_d[E][tok[1]] < tok[2]
        _, X, ep, c = tok
        if self.known_ep[E].get(X, -1) > ep:
            return False
        return self.known[E].get((X, ep), 0) < c

    def _embed(self, E, ins, tok):
        if tok[0] == 'd':
            _, k, v = tok
            ins._wait_ge(self.dsem[k], v)
            self.known_d[E][k] = v
        else:
            _, X, ep, c = tok
            ins._wait_ge(self.semof[(X, ep)], c)
            self.known[E][(X, ep)] = c
            if self.known_ep[E].get(X, -1) < ep:
                self.known_ep[E][X] = ep

    def _record(self, tok, E, reads, writes):
        for r in reads:
            self.readers.setdefault(r, {})[(E, tok[0])] = tok
        for wk in writes:
            self.last_w[wk] = tok
            self.readers[wk] = {}

    def op(self, E, fn, reads=(), writes=(), embed=True):
        reads = [self._k(r) for r in reads]
        writes = [self._k(w) for w in writes]
        last = self._deps(E, reads, writes, defer_last=(embed and E in ('act', 'dve', 'pool')))
        ins = fn(self.eng[E])
        if last is not None:
            self._embed(E, ins, last)
        self.cnt[E] += 1
        ins.then_inc(self.sem[E], 1)
        tok = ('e', E, self.ep[E], self.cnt[E])
        self._record(tok, E, reads, writes)
        self.nops += 1
        if self.cnt[E] >= self.EPOCH:
            self.ep[E] += 1
            self.cnt[E] = 0
            self.sem[E] = self._newsem()
            self.semof[(E, self.ep[E])] = self.sem[E]
        return tok

    def dma(self, E, out, in_, reads=(), writes=(), **kw):
        reads = [self._k(r) for r in reads]
        writes = [self._k(w) for w in writes]
        if E == 'pool':
            k = self.NHW + self.dnext_sw
            self.dnext_sw = (self.dnext_sw + 1) % (len(self.dsem) - self.NHW)
        else:
            k = self.dnext
            self.dnext = (self.dnext + 1) % self.NHW
        if self.dval[k] > 0:
            self._wait(E, ('d', k, self.dval[k]))
        self._deps(E, reads, writes)
        self.eng[E].dma_start(out=out, in_=in_, **kw).then_inc(self.dsem[k], 16)
        self.dval[k] += 16
        tok = ('d', k, self.dval[k])
        self._record(tok, E, reads, writes)
        return tok

    def barrier(self):
        toks = []
        for X in self.ENGS:
            if self.cnt[X] > 0:
                toks.append(('e', X, self.ep[X], self.cnt[X]))
            elif self.ep[X] > 0:
                toks.append(('e', X, self.ep[X] - 1, self.EPOCH))
        for k, v in enumerate(self.dval):
            if v > 0:
                toks.append(('d', k, v))
        for E in self.ENGS:
            for t in toks:
                self._wait(E, t)
        self.last_w = {}
        self.readers = {}


def _param_specs():
    return {
        "x": ([S, D], F32), "mem": ([256, D], F32), "pos": ([128, NT], I32),
        "ident": ([128, 128], F32), "tri": ([128, 128], F32), "invf": ([1, NFREQ], F32),
        "w_in": ([L, D, DIN], F32), "w_uq": ([L, 192, 384], F32), "w_ukv": ([L, 128, 512], F32),
        "w_memkv": ([L, D, 512], F32), "w_o": ([L, 1280, D], F32),
        "w_gate": ([L, D, DFF], F32), "w_up": ([L, D, DFF], F32), "w_down": ([L, DFF, D], F32),
        "g_attn": ([L, 128, 8], F32), "g_ffn": ([L, 128, 8], F32), "g_mem": ([L, 128, 8], F32),
        "g_cq": ([L, 128, 2], F32), "g_ckv": ([L, 128, 1], F32),
        "vecs": ([L, 1, 1024], F32),
        "convw": ([L, 128, NCH, 3], F32), "convb": ([L, 128, NCH], F32),
    }


VOFF = {}
_o = 0
for _n, _w in [("mla_q", 96), ("mla_k", 96), ("fox_b", 4), ("fox_q", 64), ("fox_k", 64), ("moba_q", 64),
               ("moba_k", 64), ("lam", 128), ("diff_q", 32), ("diff_k", 32), ("diff_sub", 64),
               ("mem_q", 64), ("mem_k", 64)]:
    VOFF[_n] = (_o, _w)
    _o += _w
assert _o <= 1024


def build(dbg=None, n_layers=L, phases=(1, 2, 3)):
    nc = bass.Bass("TRN2", target_bir_lowering=False)
    din = {}
    for name, (shape, dt) in _param_specs().items():
        din[name] = nc.dram_tensor(name, shape, dt, kind="ExternalInput").ap()
    out_d = nc.dram_tensor("out", [S, D], F32, kind="ExternalOutput").ap()

    def scratch(name, shape, dt):
        kind = "ExternalOutput" if (dbg and name in dbg) else "Internal"
        return nc.dram_tensor(name, shape, dt, kind=kind).ap()
    FTD = scratch("FTD", [NBLK, 128, S], BF16)
    VD = scratch("VD", [16, 128, NT, 65], BF16)
    MIXT = scratch("MIXT", [1280, S], BF16)
    XS = [scratch("XS0", [S, D], F32), scratch("XS1", [S, D], F32)]
    X1 = scratch("X1", [S, D], F32)

    with ExitStack() as es:
        P = Prog(nc, es)

        def sbuf(st, name, shape, dt):
            return st.enter_context(nc.sbuf_tensor(name, shape, dt))

        def psum(st, name, shape, dt):
            return st.enter_context(nc.psum_tensor(name, shape, dt))

        idf = sbuf(es, "idf", [128, 128], F32)
        idb = sbuf(es, "idb", [128, 128], BF16)
        trif = sbuf(es, "trif", [128, 128], F32)
        trib = sbuf(es, "trib", [128, 128], BF16)
        ones32 = sbuf(es, "ones32", [128, 128], F32)
        c256 = sbuf(es, "c256", [128, 1], F32)
        mhalf = sbuf(es, "mhalf", [128, 64], F32)
        epsb = sbuf(es, "epsb", [128, 1], F32)
        cosT = sbuf(es, "cosT", [128, NT, NFREQ], F32)
        sinT = sbuf(es, "sinT", [128, NT, NFREQ], F32)
        cpos = sbuf(es, "cpos", [128, NT, 4], F32)
        rtot = sbuf(es, "rtot", [128, NT + 1, 4], F32)
        fbias = sbuf(es, "fbias", [128, 4, 8, NT], F32)
        vecs = sbuf(es, "vecs_sb", [128, 1024], F32)
        gsub = sbuf(es, "gsub", [128, 64], F32)
        lamn = sbuf(es, "lamn", [128, 1], F32)
        ktm = sbuf(es, "ktm", [128, 2, 256], BF16)
        vmem = sbuf(es, "vmem", [128, 2, 4, 65], BF16)

        P.dma('sp', idf[:], din["ident"][:, :], writes=[idf])
        P.dma('sp', trif[:], din["tri"][:, :], writes=[trif])
        P.op('dve', lambda e: e.tensor_copy(idb[:], idf[:]), reads=[idf], writes=[idb])
        P.op('dve', lambda e: e.tensor_copy(trib[:], trif[:]), reads=[trif], writes=[trib])
        P.op('pool', lambda e: e.memset(ones32[:], 1.0), writes=[ones32])
        P.op('pool', lambda e: e.memset(c256[:], 1.0 / 256), writes=[c256])
        P.op('pool', lambda e: e.memset(mhalf[:], -0.5), writes=[mhalf])
        P.op('pool', lambda e: e.memset(epsb[:], EPS), writes=[epsb])
        P.op('pool', lambda e: e.memset(rtot[:], 0.0), writes=[rtot])
        P.op('pool', lambda e: e.memset(fbias[:], 0.0), writes=[fbias])

        with ExitStack() as st:
            posi = sbuf(st, "posi", [128, NT], I32)
            posf = sbuf(st, "posf", [128, NT], F32)
            invf = sbuf(st, "invf_sb", [128, NFREQ], F32)
            tt = sbuf(st, "tt", [128, NT, NFREQ], F32)
            ti = sbuf(st, "ti", [128, NT, NFREQ], I32)
            tf = sbuf(st, "tf", [128, NT, NFREQ], F32)
            mk = sbuf(st, "mk", [128, NT, NFREQ], F32)
            P.dma('sp', posi[:], din["pos"][:, :], writes=[posi])
            P.dma('sp', invf[:], din["invf"].partition_broadcast(128), writes=[invf])
            P.op('dve', lambda e: e.tensor_copy(posf[:], posi[:]), reads=[posi], writes=[posf])
            P.op('dve', lambda e: e.tensor_tensor(tt[:], posf[:].unsqueeze(2).to_broadcast([128, NT, NFREQ]),
                                                  invf[:].unsqueeze(1).to_broadcast([128, NT, NFREQ]), ALU.mult),
                 reads=[posf, invf], writes=[tt])
            for which, dst in ((0, sinT), (1, cosT)):
                sh = 0.25 * which
                P.op('dve', lambda e: e.tensor_scalar(tf[:], tt[:], 1.0 / (2 * math.pi), sh, ALU.mult, ALU.add),
                     reads=[tt], writes=[tf])
                P.op('dve', lambda e: e.tensor_copy(ti[:], tf[:]), reads=[tf], writes=[ti])
                P.op('dve', lambda e: e.tensor_copy(mk[:], ti[:]), reads=[ti], writes=[mk])
                P.op('dve', lambda e: e.tensor_tensor(tf[:], tf[:], mk[:], ALU.subtract), reads=[tf, mk], writes=[tf])
                P.op('dve', lambda e: e.tensor_scalar(mk[:], tf[:], 0.5, None, ALU.is_gt), reads=[tf], writes=[mk])
                P.op('dve', lambda e: e.tensor_tensor(tf[:], tf[:], mk[:], ALU.subtract), reads=[tf, mk], writes=[tf])
                P.op('dve', lambda e: e.tensor_scalar(mk[:], tf[:], -0.5, None, ALU.is_lt), reads=[tf], writes=[mk])
                P.op('dve', lambda e: e.tensor_tensor(tf[:], tf[:], mk[:], ALU.add), reads=[tf, mk], writes=[tf])
                P.op('act', lambda e: e.activation(dst[:], tf[:], AF.Sin, scale=2 * math.pi), reads=[tf], writes=[dst])
            P.barrier()

        xin = din["x"]
        for l in range(n_layers):
            xout = out_d if l == n_layers - 1 else XS[l % 2]
            if 1 in phases:
                phase1(nc, P, din, l, xin, FTD, VD, locals())
            with ExitStack() as lst:
                wg = lst.enter_context(nc.sbuf_tensor(f"wg_l{l}", [128, 8, DFF], BF16))
                wu = lst.enter_context(nc.sbuf_tensor(f"wu_l{l}", [128, 8, DFF], BF16))
                gff = lst.enter_context(nc.sbuf_tensor(f"gff_l{l}", [128, 8], F32))
                P.dma('sp', gff[:], din["g_ffn"][l], writes=[gff])
                wgv = din["w_gate"][l].rearrange("(k p) n -> p k n", p=128)
                wuv = din["w_up"][l].rearrange("(k p) n -> p k n", p=128)
                for k in range(8):
                    P.dma('pool', wg[:, k, :], wgv[:, k, :], writes=[(wg, k)])
                    P.dma('pool', wu[:, k, :], wuv[:, k, :], writes=[(wu, k)])

                def fold_gains():
                    for k in range(8):
                        P.op('dve', lambda e: e.tensor_scalar(wg[:, k, :], wg[:, k, :], gff[:, k:k + 1], None, ALU.mult), reads=[(wg, k), gff], writes=[(wg, k)])
                        P.op('pool', lambda e: e.tensor_scalar(wu[:, k, :], wu[:, k, :], gff[:, k:k + 1], None, ALU.mult), reads=[(wu, k), gff], writes=[(wu, k)])
                if 2 in phases:
                    phase2(nc, P, din, l, FTD, VD, MIXT, locals())
                if 3 in phases:
                    wd = lst.enter_context(nc.sbuf_tensor(f"wd_l{l}", [128, NCH, D], BF16))
                    wdv = din["w_down"][l].rearrange("(k p) n -> p k n", p=128)
                    for k0 in range(0, NCH, 11):
                        P.dma('pool', wd[:, k0:k0 + 11, :], wdv[:, k0:k0 + 11, :], writes=[(wd, k0)])
                    phase3(nc, P, din, l, xin, MIXT, X1, xout, locals(), wg, wu, wd, gff)
                    P.barrier()
            xin = xout
        P.barrier()
        print("program: ops", P.nops, "waits", P.nwaits, "sems", P.nsem)
    return nc


def _v(vecs, name, lo=0, hi=None):
    o, w = VOFF[name]
    hi = w if hi is None else hi
    return vecs[:, o + lo:o + hi]


def phase1(nc, P, din, l, xin, FTD, VD, C):
    idb, idf, trif, ones32, c256, mhalf = C["idb"], C["idf"], C["trif"], C["ones32"], C["c256"], C["mhalf"]
    cosT, sinT, cpos, rtot, fbias, vecs = C["cosT"], C["sinT"], C["cpos"], C["rtot"], C["fbias"], C["vecs"]
    gsub, lamn, ktm, vmem = C["gsub"], C["lamn"], C["ktm"], C["vmem"]
    epsb = C["epsb"]
    with ExitStack() as st:
        def sbuf(name, shape, dt):
            return st.enter_context(nc.sbuf_tensor(f"{name}_l{l}", shape, dt))

        def psum(name, shape, dt):
            return st.enter_context(nc.psum_tensor(f"{name}_l{l}", shape, dt))
        win = sbuf("win", [128, 8, DIN], BF16)
        wmk = sbuf("wmk", [128, 8, 512], BF16)
        wuq = sbuf("wuq", [128, 2, 384], BF16)
        wukv = sbuf("wukv", [128, 512], BF16)
        gat = sbuf("gat", [128, 8], F32)
        gme = sbuf("gme", [128, 8], F32)
        gcq = sbuf("gcq", [128, 2], F32)
        gckv = sbuf("gckv", [128, 1], F32)
        xt = [sbuf(f"xt{i}", [128, D], F32) for i in range(2)]
        junk = sbuf("junk", [128, DIN], BF16)
        xs = [sbuf(f"xs{i}", [128, D], BF16) for i in range(2)]
        xnT = [sbuf(f"xnT{i}", [128, 8, 128], BF16) for i in range(2)]
        hsb = [sbuf(f"hsb{i}", [128, DIN], F32) for i in range(2)]
        sqh = sbuf("sqh", [128, DIN], F32)
        ssx = sbuf("ssx", [128, 1], F32)
        rsx = sbuf("rsx", [128, 1], F32)
        ssA = sbuf("ssA", [128, 39], F32)
        rsA = sbuf("rsA", [128, 39], F32)
        rsAs = [sbuf(f"rsAs{i}", [128, 39], F32) for i in range(2)]
        invGA = sbuf("invGA", [128, 39], F32)
        ssB = sbuf("ssB", [128, 12], F32)
        rsB = sbuf("rsB", [128, 12], F32)
        invGB = sbuf("invGB", [128, 12], F32)
        cqn = sbuf("cqn", [128, 3, 128], BF16)
        cT = sbuf("cT", [128, 3, 128], BF16)
        qa = sbuf("qa", [128, 384], F32)
        kva = sbuf("kva", [128, 512], F32)
        sqb = sbuf("sqb", [128, 512], F32)
        sqk = sbuf("sqk", [128, 512], F32)
        t512 = sbuf("t512", [128, 512], F32)
        mqr = sbuf("mqr", [128, 512], F32)
        dqr = sbuf("dqr", [128, 512], F32)
        qar = sbuf("qar", [128, 4, 32], F32)
        krr = sbuf("krr", [128, 32], F32)
        r1 = sbuf("r1", [128, 256], F32)
        r2 = sbuf("r2", [128, 256], F32)
        r3 = sbuf("r3", [128, 256], F32)
        r4 = sbuf("r4", [128, 256], F32)
        zf = sbuf("zf", [128, 4], F32)
        ef = sbuf("ef", [128, 4], F32)
        spf = sbuf("spf", [128, 4], F32)
        qT32 = sbuf("qT32", [128, 2, 128], F32)
        kacc = sbuf("kacc", [128, 2], F32)
        kmeanT = sbuf("kmeanT", [128, 2, 16], F32)
        gate = sbuf("gate", [128, 4, 16], F32)
        max8 = sbuf("max8", [128, 4, 8], F32)
        YT = [sbuf(f"YT{i}", [128, NBLK, 128], BF16) for i in range(2)]
        FTS = [sbuf(f"FTS{i}", [128, NBLK, 128], BF16) for i in range(2)]
        VS = [sbuf(f"VS{i}", [128, 16, 65], BF16) for i in range(2)]
        lamt = sbuf("lamt", [128, 2], F32)
        fqd = sbuf("fqd", [128, 4], F32)
        fqh = sbuf("fqh", [128, 4], F32)

        pT = [psum(f"pT{i}", [128, 8, 128], BF16) for i in range(2)]
        pH = [psum(f"pH{i}", [128, 512], F32) for i in range(2)]
        pQA = psum("pQA", [128, 512], F32)
        pKVA = psum("pKVA", [128, 512], F32)
        pM = psum("pM", [128, 512], F32)
        pQ32 = psum("pQ32", [128, 2, 128], F32)

        P.dma('sp', vecs[:], din["vecs"][l].partition_broadcast(128), writes=[vecs])
        P.dma('sp', gat[:], din["g_attn"][l], writes=[gat])
        P.dma('sp', gme[:], din["g_mem"][l], writes=[gme])
        P.dma('sp', gcq[:], din["g_cq"][l], writes=[gcq])
        P.dma('sp', gckv[:], din["g_ckv"][l], writes=[gckv])
        wv = din["w_in"][l].rearrange("(k p) n -> p k n", p=128)
        for k in range(8):
            P.dma('pool', win[:, k, :], wv[:, k, :], writes=[(win, k)])
        P.dma('pool', wmk[:], din["w_memkv"][l].rearrange("(k p) n -> p k n", p=128), writes=[wmk])
        P.dma('pool', wuq[:, 0, :], din["w_uq"][l][0:128, :], writes=[wuq])
        P.dma('pool', wuq[0:64, 1, :], din["w_uq"][l][128:192, :], writes=[wuq])
        P.dma('pool', wukv[:], din["w_ukv"][l], writes=[wukv])
        P.op('dve', lambda e: e.tensor_scalar(wuq[:, 0, :], wuq[:, 0, :], gcq[:, 0:1], None, ALU.mult), reads=[wuq, gcq], writes=[wuq])
        P.op('dve', lambda e: e.tensor_scalar(wuq[0:64, 1, :], wuq[0:64, 1, :], gcq[0:64, 1:2], None, ALU.mult), reads=[wuq, gcq], writes=[wuq])
        P.op('dve', lambda e: e.tensor_scalar(wukv[:], wukv[:], gckv[:, 0:1], None, ALU.mult), reads=[wukv, gckv], writes=[wukv])

        for (a, b, G) in ((0, 1, 192), (1, 2, 128), (2, 3, 32), (3, 19, 64), (19, 35, 32), (35, 39, 64)):
            P.op('pool', lambda e: e.memset(invGA[:, a:b], 1.0 / G), writes=[invGA])
        for (a, b, G) in ((0, 4, 32), (4, 12, 64)):
            P.op('pool', lambda e: e.memset(invGB[:, a:b], 1.0 / G), writes=[invGB])
        for i in range(2):
            P.op('pool', lambda e: e.memset(YT[i][:], 0.0), writes=[YT[i]])
            P.op('pool', lambda e: e.memset(VS[i][:], 1.0), writes=[VS[i]])
            P.op('pool', lambda e: e.memset(YT[i][:, 26:30, 64:66], 1.0), writes=[YT[i]])
        P.op('pool', lambda e: e.memset(gate[:], -1e30), writes=[gate])
        P.op('pool', lambda e: e.memset(kmeanT[:], 0.0), writes=[kmeanT])
        P.op('pool', lambda e: e.memset(cqn[:], 0.0), writes=[cqn])
        P.op('pool', lambda e: e.memset(vmem[:], 1.0), writes=[vmem])

        lam_init = 0.8 - 0.6 * math.exp(-0.3 * l)
        lv = _v(vecs, "lam")
        P.op('dve', lambda e: e.tensor_tensor(r1[:, 0:32], lv[:, 0:32], lv[:, 32:64], ALU.mult), reads=[vecs], writes=[r1])
        P.op('dve', lambda e: e.tensor_tensor(r1[:, 32:64], lv[:, 64:96], lv[:, 96:128], ALU.mult), reads=[vecs, r1], writes=[r1])
        P.op('dve', lambda e: e.tensor_reduce(lamt[:], r1[:, 0:64].rearrange("p (a b) -> p a b", b=32), AX.X, ALU.add), reads=[r1], writes=[lamt])
        P.op('act', lambda e: e.activation(lamt[:], lamt[:], AF.Exp), reads=[lamt], writes=[lamt])
        P.op('dve', lambda e: e.tensor_tensor(lamn[:], lamt[:, 1:2], lamt[:, 0:1], ALU.subtract), reads=[lamt], writes=[lamn])
        P.op('dve', lambda e: e.tensor_scalar(lamn[:], lamn[:], -lam_init, None, ALU.add), reads=[lamn], writes=[lamn])
        P.op('dve', lambda e: e.tensor_scalar(gsub[:], _v(vecs, "diff_sub"), 1.0 - lam_init, None, ALU.mult), reads=[vecs], writes=[gsub])

        def rstd_from(ss, invG, rs, n):
            P.op('dve', lambda e: e.tensor_tensor(rs[:, 0:n], ss[:, 0:n], invG[:, 0:n], ALU.mult), reads=[ss, invG], writes=[rs])
            P.op('act', lambda e: e.activation(rs[:, 0:n], rs[:, 0:n], AF.Ln, bias=epsb[:, 0:1]), reads=[rs, epsb], writes=[rs])
            P.op('act', lambda e: e.activation(rs[:, 0:n], rs[:, 0:n], AF.Exp, scale=-0.5), reads=[rs], writes=[rs])

        def rope(buf, key, nvec, G, half, cs_lo, t, tmp_keys):
            v = buf.rearrange("p (n g) -> p n g", g=G)
            y1 = v[:, :, 0:half]
            y2 = v[:, :, half:2 * half]
            cb = cosT[:, t, cs_lo:cs_lo + half].unsqueeze(1).to_broadcast([128, nvec, half])
            sb_ = sinT[:, t, cs_lo:cs_lo + half].unsqueeze(1).to_broadcast([128, nvec, half])
            tv = [x[:, 0:nvec * half].rearrange("p (n g) -> p n g", g=half) for x in (r1, r2, r3, r4)]
            P.op('dve', lambda e: e.tensor_tensor(tv[0], y1, cb, ALU.mult), reads=[key, cosT], writes=[r1])
            P.op('dve', lambda e: e.tensor_tensor(tv[1], y2, sb_, ALU.mult), reads=[key, sinT], writes=[r2])
            P.op('dve', lambda e: e.tensor_tensor(tv[2], y1, sb_, ALU.mult), reads=[key, sinT], writes=[r3])
            P.op('dve', lambda e: e.tensor_tensor(tv[3], y2, cb, ALU.mult), reads=[key, cosT], writes=[r4])
            P.op('dve', lambda e: e.tensor_tensor(y1, tv[0], tv[1], ALU.subtract), reads=[r1, r2], writes=[key])
            P.op('dve', lambda e: e.tensor_tensor(y2, tv[2], tv[3], ALU.add), reads=[r3, r4], writes=[key])

        def front_chain(src_ap, i, wt, ncols, mem_mode=False, with_stats=False):
            ch = []
            ch.append(lambda: P.op('act', lambda e: e.activation(junk[:, 0:D], xt[i][:], AF.Square, accum_out=ssx[:]), reads=[xt[i]], writes=[junk, ssx], embed=False))
            ch.append(lambda: P.op('act', lambda e: e.activation(rsx[:], ssx[:], AF.Ln, bias=epsb[:, 0:1], scale=1.0 / D), reads=[ssx, epsb], writes=[rsx]))
            ch.append(lambda: P.op('act', lambda e: e.activation(rsx[:], rsx[:], AF.Exp, scale=-0.5), reads=[rsx], writes=[rsx]))
            ch.append(lambda: P.op('act', lambda e: e.activation(xs[i][:], xt[i][:], AF.Copy, scale=rsx[:, 0:1]), reads=[xt[i], rsx], writes=[xs[i]]))

            def tr():
                for k in range(8):
                    P.op('pe', lambda e: e.transpose(pT[0][:, k, :], xs[i][:, k * 128:(k + 1) * 128], idb[:]), reads=[xs[i], idb], writes=[pT[0]])
            ch.append(tr)
            gn = gme if mem_mode else gat
            for k_ in range(8):
                def ev(k=k_):
                    P.op('act', lambda e: e.activation(xnT[i][:, k, :], pT[0][:, k, :], AF.Copy, scale=gn[:, k:k + 1]), reads=[pT[0], gn], writes=[xnT[i]])
                ch.append(ev)
            nchunk = (ncols + 511) // 512
            for c_ in range(nchunk):
                def mm(c=c_):
                    c0, c1 = c * 512, min(ncols, (c + 1) * 512)
                    ph = pH[c % 2]
                    for k in range(8):
                        P.op('pe', lambda e: e.matmul(ph[:, 0:c1 - c0], xnT[i][:, k, :], wt[:, k, c0:c1], start=(k == 0), stop=(k == 7)),
                             reads=[xnT[i], (wt, k)] if not mem_mode else [xnT[i], wt], writes=[ph])

                def evh(c=c_):
                    c0, c1 = c * 512, min(ncols, (c + 1) * 512)
                    ph = pH[c % 2]
                    P.op('act', lambda e: e.copy(hsb[i][:, c0:c1], ph[:, 0:c1 - c0]), reads=[ph], writes=[hsb[i]])
                ch.append(mm)
                ch.append(evh)
            if with_stats:
                h = hsb[i]
                ch.append(lambda: P.op('act', lambda e: e.activation(sqh[:], h[:], AF.Square), reads=[h], writes=[sqh]))
                for (c0_, c1_, G_, a__) in ((C_CQ, C_CKV, 192, 0), (C_CKV, C_KR, 128, 1), (C_KR, C_FQ, 32, 2), (C_FQ, C_FV, 64, 3),
                                            (C_MQ, C_MV, 64, 11), (C_DQ, C_DV, 32, 19), (C_EQ, DIN, 64, 35)):
                    def red(c0=c0_, c1=c1_, G=G_, a_=a__):
                        n = (c1 - c0) // G
                        P.op('dve', lambda e: e.tensor_reduce(ssA[:, a_:a_ + n], sqh[:, c0:c1].rearrange("p (n g) -> p n g", g=G), AX.X, ALU.add),
                             reads=[sqh], writes=[ssA])
                    ch.append(red)
                rs_i = rsAs[i]
                ch.append(lambda: P.op('dve', lambda e: e.tensor_tensor(rs_i[:, 0:39], ssA[:, 0:39], invGA[:, 0:39], ALU.mult), reads=[ssA, invGA], writes=[rs_i]))
                ch.append(lambda: P.op('act', lambda e: e.activation(rs_i[:, 0:39], rs_i[:, 0:39], AF.Ln, bias=epsb[:, 0:1]), reads=[rs_i, epsb], writes=[rs_i]))
                ch.append(lambda: P.op('act', lambda e: e.activation(rs_i[:, 0:39], rs_i[:, 0:39], AF.Exp, scale=-0.5), reads=[rs_i], writes=[rs_i]))
            return ch

        def tile_front(src_ap, i, wt, ncols, mem_mode=False):
            P.dma('act', xt[i][:], src_ap, writes=[xt[i]])
            for f_ in front_chain(src_ap, i, wt, ncols, mem_mode):
                f_()

        for mt in range(2):
            i = mt % 2
            tile_front(din["mem"][mt * 128:(mt + 1) * 128, :], i, wmk, 512, mem_mode=True)
            h = hsb[i]
            P.op('act', lambda e: e.activation(sqh[:, 0:256], h[:, 0:256], AF.Square), reads=[h], writes=[sqh])
            P.op('dve', lambda e: e.tensor_reduce(ssA[:, 0:4], sqh[:, 0:256].rearrange("p (n g) -> p n g", g=64), AX.X, ALU.add), reads=[sqh], writes=[ssA])
            P.op('act', lambda e: e.activation(rsA[:, 0:4], ssA[:, 0:4], AF.Ln, bias=epsb[:, 0:1], scale=1.0 / 64), reads=[ssA, epsb], writes=[rsA])
            P.op('act', lambda e: e.activation(rsA[:, 0:4], rsA[:, 0:4], AF.Exp, scale=-0.5), reads=[rsA], writes=[rsA])
            kv3 = h[:, 0:256].rearrange("p (n g) -> p n g", g=64)
            t3 = t512[:, 0:256].rearrange("p (n g) -> p n g", g=64)
            P.op('dve', lambda e: e.tensor_tensor(t3, kv3, rsA[:, 0:4].unsqueeze(2).to_broadcast([128, 4, 64]), ALU.mult), reads=[h, rsA], writes=[t512])
            y3 = YT[0][:, 0:2, :].rearrange("p b (n g) -> p (b n) g", g=64)
            P.op('dve', lambda e: e.tensor_tensor(y3, t3, _v(vecs, "mem_k").unsqueeze(1).to_broadcast([128, 4, 64]), ALU.mult), reads=[t512, vecs], writes=[YT[0]])
            for b in range(2):
                P.op('pe', lambda e: e.transpose(pT[1][:, b, :], YT[0][:, b, :], idb[:]), reads=[YT[0], idb], writes=[pT[1]])
            P.op('act', lambda e: e.copy(ktm[:, :, mt * 128:(mt + 1) * 128], pT[1][:, 0:2, :]), reads=[pT[1]], writes=[ktm])
            P.op('pool', lambda e: e.tensor_copy(vmem[:, mt, :, 0:64], h[:, 256:512].rearrange("p (n g) -> p n g", g=64)), reads=[h], writes=[vmem])

        rt = {nm: [sbuf(f"rt_{nm}{q}", [128, 64], F32) for q in range(4)] for nm in ("kr", "mo", "df", "qa")}
        t512f = sbuf("t512f", [128, 512], F32)
        t512m = sbuf("t512m", [128, 512], F32)
        t512d = sbuf("t512d", [128, 512], F32)
        t512e = sbuf("t512e", [128, 256], F32)

        def rope_ops(E, buf, key, nvec, G, half, cs_lo, t, tmps):
            v = buf.rearrange("p (n g) -> p n g", g=G)
            y1 = v[:, :, 0:half]
            y2 = v[:, :, half:2 * half]
            cb = cosT[:, t, cs_lo:cs_lo + half].unsqueeze(1).to_broadcast([128, nvec, half])
            sb_ = sinT[:, t, cs_lo:cs_lo + half].unsqueeze(1).to_broadcast([128, nvec, half])
            tv = [x[:, 0:nvec * half].rearrange("p (n g) -> p n g", g=half) for x in tmps]
            return [
                lambda: P.op(E, lambda e: e.tensor_tensor(tv[0], y1, cb, ALU.mult), reads=[key, cosT], writes=[tmps[0]]),
                lambda: P.op(E, lambda e: e.tensor_tensor(tv[1], y2, sb_, ALU.mult), reads=[key, sinT], writes=[tmps[1]]),
                lambda: P.op(E, lambda e: e.tensor_tensor(tv[2], y1, sb_, ALU.mult), reads=[key, sinT], writes=[tmps[2]]),
                lambda: P.op(E, lambda e: e.tensor_tensor(tv[3], y2, cb, ALU.mult), reads=[key, cosT], writes=[tmps[3]]),
                lambda: P.op(E, lambda e: e.tensor_tensor(y1, tv[0], tv[1], ALU.subtract), reads=[tmps[0], tmps[1]], writes=[key]),
                lambda: P.op(E, lambda e: e.tensor_tensor(y2, tv[2], tv[3], ALU.add), reads=[tmps[2], tmps[3]], writes=[key]),
            ]

        def post(t, extra_chain):
            i = t % 2
            h = hsb[i]
            yt = YT[i]
            vs = VS[i]
            nb = t // 2
            rsA = rsAs[i]

            qa3 = qa[:].rearrange("p (n g) -> p n g", g=96)
            kv3 = kva[:].rearrange("p (n g) -> p n g", g=128)
            q3 = sqb[:, 0:384].rearrange("p (n g) -> p n g", g=96)
            k3 = sqk[:].rearrange("p (n g) -> p n g", g=128)
            t3q = t512[:, 0:256].rearrange("p (n g) -> p n g", g=64)
            t3k = t512[:, 256:512].rearrange("p (n g) -> p n g", g=64)

            def mla_pe():
                for b_ in range(3):
                    P.op('pe', lambda e: e.transpose(pT[1][:, b_, :], cqn[:, b_, :], idb[:]), reads=[cqn, idb], writes=[pT[1]])
                P.op('act', lambda e: e.copy(cT[:], pT[1][:, 0:3, :]), reads=[pT[1]], writes=[cT])
                P.op('pe', lambda e: e.matmul(pQA[:, 0:384], cT[:, 0, :], wuq[:, 0, :], start=True, stop=False), reads=[cT, wuq], writes=[pQA])
                P.op('pe', lambda e: e.matmul(pQA[:, 0:384], cT[0:64, 1, :], wuq[0:64, 1, :], start=False, stop=True), reads=[cT, wuq], writes=[pQA])
                P.op('pe', lambda e: e.matmul(pKVA[:], cT[:, 2, :], wukv[:], start=True, stop=True), reads=[cT, wukv], writes=[pKVA])
                P.op('act', lambda e: e.copy(qa[:], pQA[:, 0:384]), reads=[pQA], writes=[qa])
                P.op('act', lambda e: e.copy(kva[:], pKVA[:]), reads=[pKVA], writes=[kva])
                P.op('act', lambda e: e.activation(sqb[:, 0:384], qa[:], AF.Square), reads=[qa], writes=[sqb])
                P.op('act', lambda e: e.activation(sqk[:], kva[:], AF.Square), reads=[kva], writes=[sqk])
                P.op('pool', lambda e: e.tensor_copy(vs[:, 0:4, 0:64], kv3[:, :, 64:128]), reads=[kva], writes=[(vs, 0)])
            ch_mla = [
                lambda: P.op('dve', lambda e: e.tensor_scalar(cqn[:, 0, :], h[:, 0:128], rsA[:, 0:1], None, ALU.mult), reads=[h, rsA], writes=[cqn]),
                lambda: P.op('dve', lambda e: e.tensor_scalar(cqn[:, 1, 0:64], h[:, 128:192], rsA[:, 0:1], None, ALU.mult), reads=[h, rsA], writes=[cqn]),
                lambda: P.op('dve', lambda e: e.tensor_scalar(cqn[:, 2, :], h[:, C_CKV:C_KR], rsA[:, 1:2], None, ALU.mult), reads=[h, rsA], writes=[cqn]),
                mla_pe,
            ]
            ch_mla2 = [
                lambda: P.op('dve', lambda e: e.tensor_reduce(ssB[:, 0:4], q3[:, :, 0:32], AX.X, ALU.add), reads=[sqb], writes=[ssB]),
                lambda: P.op('dve', lambda e: e.tensor_reduce(ssB[:, 4:8], q3[:, :, 32:96], AX.X, ALU.add), reads=[sqb], writes=[ssB]),
                lambda: P.op('dve', lambda e: e.tensor_reduce(ssB[:, 8:12], k3[:, :, 0:64], AX.X, ALU.add), reads=[sqk], writes=[ssB]),
                lambda: rstd_from(ssB, invGB, rsB, 12),
                lambda: P.op('dve', lambda e: e.tensor_tensor(qar[:], qa3[:, :, 0:32], rsB[:, 0:4].unsqueeze(2).to_broadcast([128, 4, 32]), ALU.mult), reads=[qa, rsB], writes=[qar]),
                lambda: P.op('dve', lambda e: e.tensor_tensor(t3q, qa3[:, :, 32:96], rsB[:, 4:8].unsqueeze(2).to_broadcast([128, 4, 64]), ALU.mult), reads=[qa, rsB], writes=[(t512, 0)]),
                lambda: P.op('dve', lambda e: e.tensor_tensor(t3k, kv3[:, :, 0:64], rsB[:, 8:12].unsqueeze(2).to_broadcast([128, 4, 64]), ALU.mult), reads=[kva, rsB], writes=[(t512, 1)]),
                lambda: P.op('dve', lambda e: e.tensor_tensor(qar[:], qar[:], _v(vecs, "mla_q", 0, 32).unsqueeze(1).to_broadcast([128, 4, 32]), ALU.mult), reads=[qar, vecs], writes=[qar]),
                lambda: P.op('dve', lambda e: e.tensor_tensor(yt[:, 0:4, 32:96], t3q, _v(vecs, "mla_q", 32, 96).unsqueeze(1).to_broadcast([128, 4, 64]), ALU.mult), reads=[(t512, 0), vecs], writes=[(yt, "mq")]),
                lambda: P.op('dve', lambda e: e.tensor_tensor(yt[:, 4:8, 32:96], t3k, _v(vecs, "mla_k", 32, 96).unsqueeze(1).to_broadcast([128, 4, 64]), ALU.mult), reads=[(t512, 1), vecs], writes=[(yt, "mk")]),
            ] + rope_ops('dve', qar[:].rearrange("p n g -> p (n g)"), qar, 4, 32, 16, 0, t, rt["qa"]) + [
                lambda: P.op('dve', lambda e: e.tensor_copy(yt[:, 0:4, 0:32], qar[:]), reads=[qar], writes=[(yt, "mqr")]),
            ]
            ch_kr = [
                lambda: P.op('dve', lambda e: e.scalar_tensor_tensor(krr[:], h[:, C_KR:C_FQ], rsA[:, 2:3], _v(vecs, "mla_k", 0, 32), ALU.mult, ALU.mult),
                             reads=[h, rsA, vecs], writes=[krr]),
            ] + rope_ops('dve', krr[:], krr, 1, 32, 16, 0, t, rt["kr"]) + [
                lambda: P.op('dve', lambda e: e.tensor_copy(yt[:, 4:8, 0:32], krr[:].unsqueeze(1).to_broadcast([128, 4, 32])), reads=[krr], writes=[(yt, "kr")]),
            ]
            t3f = t512f[:].rearrange("p (n g) -> p n g", g=64)
            q0 = (t // 4) * 4

            def fox_gate_mid():
                P.op('act', lambda e: e.activation(ef[:], zf[:], AF.Exp, scale=-1.0), reads=[zf], writes=[ef])
                P.op('act', lambda e: e.activation(spf[:], ef[:], AF.Ln, bias=1.0), reads=[ef], writes=[spf])
                P.op('pe', lambda e: e.matmul(pM[:, 0:4], trif[:], spf[:], start=True, stop=True), reads=[trif, spf], writes=[pM])
                P.op('pe', lambda e: e.matmul(pM[:, 8:12], ones32[:], spf[:], start=True, stop=True), reads=[ones32, spf], writes=[pM])
            ch_fox = [
                lambda: P.op('dve', lambda e: e.tensor_tensor(zf[:], h[:, C_FF:C_FF + 4], _v(vecs, "fox_b"), ALU.add), reads=[h, vecs], writes=[zf]),
                fox_gate_mid,
                lambda: P.op('dve', lambda e: e.tensor_tensor(t3f, h[:, C_FQ:C_FV].rearrange("p (n g) -> p n g", g=64), rsA[:, 3:11].unsqueeze(2).to_broadcast([128, 8, 64]), ALU.mult), reads=[h, rsA], writes=[t512f]),
                lambda: P.op('dve', lambda e: e.tensor_tensor(yt[:, 8:12, 0:64], t3f[:, 0:4, :], _v(vecs, "fox_q").unsqueeze(1).to_broadcast([128, 4, 64]), ALU.mult), reads=[t512f, vecs], writes=[(yt, "fq")]),
                lambda: P.op('dve', lambda e: e.tensor_tensor(yt[:, 26:30, 0:64], t3f[:, 4:8, :], _v(vecs, "fox_k").unsqueeze(1).to_broadcast([128, 4, 64]), ALU.mult), reads=[t512f, vecs], writes=[(yt, "fk")]),
                lambda: P.op('dve', lambda e: e.tensor_tensor(cpos[:, t, :], pM[:, 0:4], rtot[:, t, :], ALU.add), reads=[pM, rtot], writes=[cpos]),
                lambda: P.op('dve', lambda e: e.tensor_tensor(rtot[:, t + 1, :], pM[:, 8:12], rtot[:, t, :], ALU.add), reads=[pM, rtot], writes=[rtot]),
                lambda: P.op('dve', lambda e: e.tensor_tensor(fqd[:], rtot[:, q0, :], cpos[:, t, :], ALU.subtract), reads=[rtot, cpos], writes=[fqd]),
                lambda: P.op('dve', lambda e: e.tensor_scalar(yt[:, 8:12, 64], fqd[:], 8.0, None, ALU.mult), reads=[fqd], writes=[(yt, "fh")]),
                lambda: P.op('dve', lambda e: e.tensor_copy(fqh[:], yt[:, 8:12, 64]), reads=[(yt, "fh")], writes=[fqh]),
                lambda: P.op('dve', lambda e: e.scalar_tensor_tensor(yt[:, 8:12, 65], fqd[:], 8.0, fqh[:], ALU.mult, ALU.subtract), reads=[fqd, fqh], writes=[(yt, "fl")]),
            ]
            t3m = t512m[:].rearrange("p (n g) -> p n g", g=64)
            mo, _ = VOFF["moba_q"]
            gm2 = vecs[:, mo:mo + 128].rearrange("p (a g) -> p a g", g=64).unsqueeze(2).to_broadcast([128, 2, 4, 64])

            def moba_gate_pe():
                for pr in range(2):
                    P.op('pe', lambda e: e.matmul(pM[:, 16 + pr:17 + pr], mqr[:, 256 + pr * 128:256 + (pr + 1) * 128], c256[:], start=True, stop=True),
                         reads=[mqr, c256], writes=[pM])
                if nb >= 4:
                    for pr in range(2):
                        P.op('pe', lambda e: e.transpose(pQ32[:, pr, :], mqr[:, pr * 128:(pr + 1) * 128], idf[:]), reads=[mqr, idf], writes=[pQ32])
                    P.op('act', lambda e: e.copy(qT32[:], pQ32[:]), reads=[pQ32], writes=[qT32])
                    for hh in range(4):
                        pr, r0 = hh // 2, (hh % 2) * 64
                        P.op('pe', lambda e: e.matmul(pM[:, 32 + hh * 16:32 + hh * 16 + 16], qT32[r0:r0 + 64, pr, :], kmeanT[r0:r0 + 64, pr, :], start=True, stop=True),
                             reads=[qT32, kmeanT], writes=[pM])
            ch_moba = [
                lambda: P.op('dve', lambda e: e.tensor_tensor(t3m, h[:, C_MQ:C_MV].rearrange("p (n g) -> p n g", g=64), rsA[:, 11:19].unsqueeze(2).to_broadcast([128, 8, 64]), ALU.mult), reads=[h, rsA], writes=[t512m]),
                lambda: P.op('dve', lambda e: e.tensor_tensor(mqr[:].rearrange("p (a n g) -> p a n g", a=2, g=64),
                                                              t512m[:].rearrange("p (a n g) -> p a n g", a=2, g=64), gm2, ALU.mult), reads=[t512m, vecs], writes=[mqr]),
            ] + rope_ops('dve', mqr[:], mqr, 8, 64, 8, 16, t, rt["mo"]) + [
                moba_gate_pe,
                lambda: P.op('dve', lambda e: e.tensor_copy(yt[:, 12:20, 0:64], mqr[:].rearrange("p (n g) -> p n g", g=64)), reads=[mqr], writes=[(yt, "mo")]),
            ]
            if t % 2 == 0:
                ch_moba.append(lambda: P.op('dve', lambda e: e.tensor_copy(kacc[:], pM[:, 16:18]), reads=[pM], writes=[kacc]))
            else:
                ch_moba.append(lambda: P.op('dve', lambda e: e.tensor_tensor(kmeanT[:, :, nb], pM[:, 16:18], kacc[:], ALU.add), reads=[pM, kacc], writes=[kmeanT]))
            if nb >= 4:
                ch_moba.append(lambda: P.op('dve', lambda e: e.tensor_copy(gate[:, :, 0:nb], pM[:, 32:96].rearrange("p (n g) -> p n g", g=16)[:, :, 0:nb]), reads=[pM], writes=[gate]))
                for hh_ in range(4):
                    def gsel(hh=hh_):
                        P.op('dve', lambda e: e.max(max8[:, hh, :], gate[:, hh, :]), reads=[gate], writes=[(max8, hh)])
                        P.op('dve', lambda e: e.tensor_scalar(yt[:, 12 + hh, 64:64 + nb], gate[:, hh, 0:nb], max8[:, hh, 2:3], NEG, ALU.is_lt, ALU.mult),
                             reads=[gate, (max8, hh)], writes=[(yt, "mg")])
                    ch_moba.append(gsel)
            t3d = t512d[:].rearrange("p (n g) -> p n g", g=32)
            do, _ = VOFF["diff_q"]
            gd2 = vecs[:, do:do + 64].rearrange("p (a g) -> p a g", g=32).unsqueeze(2).to_broadcast([128, 2, 8, 32])
            t3e = t512e[:].rearrange("p (n g) -> p n g", g=64)
            ch_pool = [
                lambda: P.op('pool', lambda e: e.tensor_tensor(t3d, h[:, C_DQ:C_DV].rearrange("p (n g) -> p n g", g=32), rsA[:, 19:35].unsqueeze(2).to_broadcast([128, 16, 32]), ALU.mult), reads=[h, rsA], writes=[t512d]),
                lambda: P.op('pool', lambda e: e.tensor_tensor(dqr[:].rearrange("p (a n g) -> p a n g", a=2, g=32),
                                                               t512d[:].rearrange("p (a n g) -> p a n g", a=2, g=32), gd2, ALU.mult), reads=[t512d, vecs], writes=[dqr]),
            ] + rope_ops('pool', dqr[:], dqr, 16, 32, 4, 24, t, rt["df"]) + [
                lambda: P.op('pool', lambda e: e.tensor_copy(yt[:, 20:24, :].rearrange("p b c -> p (b c)"), dqr[:]), reads=[dqr], writes=[(yt, "df")]),
                lambda: P.op('pool', lambda e: e.tensor_tensor(t3e, h[:, C_EQ:DIN].rearrange("p (n g) -> p n g", g=64), rsA[:, 35:39].unsqueeze(2).to_broadcast([128, 4, 64]), ALU.mult), reads=[h, rsA], writes=[t512e]),
                lambda: P.op('pool', lambda e: e.tensor_tensor(yt[:, 24:26, :].rearrange("p b (n g) -> p (b n) g", g=64), t3e,
                                                               _v(vecs, "mem_q").unsqueeze(1).to_broadcast([128, 4, 64]), ALU.mult), reads=[t512e, vecs], writes=[(yt, "eq")]),
                lambda: P.op('pool', lambda e: e.memset(yt[:, 16:20, 64:80], 0.0), writes=[(yt, "oh")]),
                lambda: P.op('pool', lambda e: e.memset(yt[:, 16:20, 64 + nb:65 + nb], 1.0), writes=[(yt, "oh")]),
            ]
            ch_v = []
            for (c0_, hb_) in ((C_FV, 4), (C_MV, 8), (C_DV, 12)):
                def vcopy(c0=c0_, hb=hb_):
                    P.op('act', lambda e: e.copy(vs[:, hb:hb + 4, 0:64], h[:, c0:c0 + 256].rearrange("p (n g) -> p n g", g=64)), reads=[h], writes=[(vs, hb)])
                ch_v.append(vcopy)
            ch_mla = ch_mla + [lambda: None] * 3 + ch_mla2
            chains = [extra_chain, ch_mla, ch_fox, ch_moba, ch_kr, ch_pool, ch_v]
            idx = [0] * len(chains)
            live = True
            while live:
                live = False
                for ci, ch in enumerate(chains):
                    if idx[ci] < len(ch):
                        ch[idx[ci]]()
                        idx[ci] += 1
                        live = True
            ykeys = [(yt, k_) for k_ in ("mq", "mk", "mqr", "kr", "fq", "fk", "fh", "fl", "mo", "mg", "df", "eq", "oh")] + [yt]
            fts = FTS[i]
            for grp in range(4):
                b0, b1 = grp * 8, min(NBLK, grp * 8 + 8)
                pt = pT[grp % 2]
                for b_ in range(b0, b1):
                    P.op('pe', lambda e: e.transpose(pt[:, b_ - b0, :], yt[:, b_, :], idb[:]), reads=ykeys + [idb], writes=[pt])
                if grp % 2 == 0:
                    P.op('act', lambda e: e.copy(fts[:, b0:b1, :], pt[:, 0:b1 - b0, :]), reads=[pt], writes=[fts])
                else:
                    P.op('dve', lambda e: e.tensor_copy(fts[:, b0:b1, :], pt[:, 0:b1 - b0, :]), reads=[pt], writes=[fts])
            P.dma('sp', FTD[:, :, t * 128:(t + 1) * 128].rearrange("b p t -> p b t"), fts[:], reads=[fts], writes=["FTD"])
            vkeys = [(vs, 0), (vs, 4), (vs, 8), (vs, 12), vs]
            P.dma('sp', VD[:, :, t, :].rearrange("h p c -> p h c"), vs[:], reads=vkeys, writes=["VD"])
            return ykeys, vkeys

        P.barrier()
        P.dma('act', xt[0][:], xin[0:128, :], writes=[xt[0]])
        P.dma('act', xt[1][:], xin[128:256, :], writes=[xt[1]])
        for f_ in front_chain(None, 0, win, DIN, with_stats=True):
            f_()
        for t in range(NT):
            nxt = front_chain(None, (t + 1) % 2, win, DIN, with_stats=True) if t + 1 < NT else []
            if t + 2 < NT:
                nxt = [lambda t=t: P.dma('act', xt[t % 2][:], xin[(t + 2) * 128:(t + 3) * 128, :], writes=[xt[t % 2]])] + nxt
            post(t, nxt)
        for hh in range(4):
            for Q in range(8):
                n = 4 * Q + 4
                P.op('dve', lambda e: e.tensor_scalar(fbias[:, hh, Q, 0:n], cpos[:, 0:n, hh], rtot[:, 4 * Q, hh:hh + 1], None, ALU.subtract),
                     reads=[cpos, rtot], writes=[fbias])
        P.barrier()


def phase2(nc, P, din, l, FTD, VD, MIXT, C, mid_hook=None):
    idb, trib, mhalf, fbias, vecs = C["idb"], C["trib"], C["mhalf"], C["fbias"], C["vecs"]
    ones32 = C["ones32"]
    gsub, lamn, ktm, vmem = C["gsub"], C["lamn"], C["ktm"], C["vmem"]
    epsb = C["epsb"]
    with ExitStack() as st:
        def sbuf(name, shape, dt):
            return st.enter_context(nc.sbuf_tensor(f"{name}_a{l}", shape, dt))

        def psum(name, shape, dt):
            return st.enter_context(nc.psum_tensor(f"{name}_a{l}", shape, dt))
        QT = [[sbuf(f"QT{i}{m}", [128, S], BF16) for m in range(2)] for i in range(2)]
        KT = [[sbuf(f"KT{i}{m}", [128, S], BF16) for m in range(2)] for i in range(2)]
        VV = [sbuf(f"VV{i}", [128, NT, 65], BF16) for i in range(2)]
        PT = [sbuf(f"PT{i}", [128, 512], BF16) for i in range(3)]
        rec = sbuf("rec", [128, 8], F32)
        osb = [sbuf(f"osb{i}", [128, 4, 64], BF16) for i in range(2)]
        oT = [sbuf(f"oT{i}", [64, 512], BF16) for i in range(2)]
        da = sbuf("da", [128, 4, 64], F32)
        db = sbuf("db", [128, 4, 64], F32)
        dsq = sbuf("dsq", [128, 4, 64], F32)
        dss = sbuf("dss", [128, 4], F32)
        rrows = [sbuf(f"rrow{i}", [128, 512], F32) for i in range(2)]
        oraws = [sbuf(f"oraw{i}", [64, 512], F32) for i in range(2)]
        pS = [psum(f"pS{i}", [128, 512], F32) for i in range(3)]
        pO = [[psum(f"pO{i}{m}", [128, 4, 128], F32) for m in range(2)] for i in range(2)]
        pTO = psum("pTO", [128, 4, 128], BF16)

        units = []
        for h in range(4):
            units.append(dict(kind="mla", rows=96, q=[(h, 0)], k=[(4 + h, 0)], v=h, scale=96 ** -0.5, out=0 * 256 + h * 64))
        for h in range(4):
            units.append(dict(kind="fox", rows=66, q=[(8 + h, 0)], k=[(26 + h, 0)], v=4 + h, scale=0.125, out=256 + h * 64, h=h))
        for h in range(4):
            units.append(dict(kind="moba", rows=80, q=[(12 + h, 0)], k=[(16 + h, 0)], v=8 + h, scale=0.125, out=512 + h * 64))
        for h in range(4):
            r0 = (h % 2) * 64
            units.append(dict(kind="diff", rows=32, q=[(20 + h // 2, r0), (20 + h // 2, r0 + 32)], k=[(22 + h // 2, r0), (22 + h // 2, r0 + 32)],
                              v=12 + h, scale=32 ** -0.5, out=768 + h * 64))
        for h in range(4):
            units.append(dict(kind="mem", rows=64, q=[(24 + h // 2, (h % 2) * 64)], k=None, v=None, scale=0.125, out=1024 + h * 64, h=h))

        def load_unit(u, i):
            rows = u["rows"]
            base = (u["h"] % 2) * 64 if u["kind"] == "mem" else 0
            for m, (blk, r0) in enumerate(u["q"]):
                P.dma('sp', QT[i][m][base:base + rows, :], FTD[blk, r0:r0 + rows, :], reads=["FTD"], writes=[QT[i][m]])
            if u["k"] is not None:
                for m, (blk, r0) in enumerate(u["k"]):
                    P.dma('sp', KT[i][m][0:rows, :], FTD[blk, r0:r0 + rows, :], reads=["FTD"], writes=[KT[i][m]])
                P.dma('sp', VV[i][:], VD[u["v"]], reads=["VD"], writes=[VV[i]])

        if DBG_UNITS is not None:
            units = [units[k] for k in DBG_UNITS]
        PT4 = PT + [sbuf("PT3", [128, 512], BF16)]
        gstep = [0]
        qglob = [0]
        load_unit(units[0], 0)
        for ui, u in enumerate(units):
            i = ui % 2
            if ui + 1 < len(units):
                load_unit(units[ui + 1], (ui + 1) % 2)
            if mid_hook is not None and ui == min(6, len(units) - 1):
                mid_hook()
            rows, kind, scale = u["rows"], u["kind"], u["scale"]
            nmap = len(u["q"])
            base = (u["h"] % 2) * 64 if kind == "mem" else 0
            steps = []
            for Q in range(8):
                nkt = 2 if kind == "mem" else 4 * Q + 4
                for j in range(nkt):
                    for m in range(nmap):
                        steps.append((Q, j, m, nkt))
            pobuf = {}
            for Q in range(8):
                pobuf[Q] = pO[qglob[0] % 2]
                qglob[0] += 1
            first = {}
            bufs = {}
            deferred = []

            def front(s):
                Q, j, m, nkt = steps[s]
                g = j - 4 * Q if (kind != "mem" and j >= 4 * Q) else None
                c0 = g * 128 if g is not None else 0
                ps = pS[gstep[0] % 3]
                pt = PT4[gstep[0] % 4]
                gstep[0] += 1
                bufs[s] = pt
                if kind == "mem":
                    lhsT = ktm[base:base + 64, u["h"] // 2, j * 128:(j + 1) * 128]
                    kkey = ktm
                else:
                    lhsT = KT[i][m][0:rows, j * 128:(j + 1) * 128]
                    kkey = KT[i][m]
                rhs = QT[i][m][base:base + rows, Q * 512 + c0:(Q + 1) * 512]
                P.op('pe', lambda e: e.matmul(ps[:, c0:512], lhsT, rhs, start=True, stop=True), reads=[kkey, QT[i][m]], writes=[ps])
                bias = fbias[:, u["h"], Q, j:j + 1] if kind == "fox" else 0.0
                P.op('act', lambda e: e.activation(pt[:, c0:512], ps[:, c0:512], AF.Exp, bias=bias, scale=scale),
                     reads=[ps, fbias] if kind == "fox" else [ps], writes=[pt])
                if g is not None:
                    P.op('pool', lambda e: e.tensor_tensor(pt[:, c0:c0 + 128], pt[:, c0:c0 + 128], trib[:], ALU.mult), reads=[pt, trib], writes=[pt])

            def back(s):
                Q, j, m, nkt = steps[s]
                g = j - 4 * Q if (kind != "mem" and j >= 4 * Q) else None
                pt = bufs.pop(s)
                po = pobuf[Q]
                if kind == "mem":
                    vap = vmem[:, j, u["h"], :]
                    vkey = vmem
                else:
                    vap = VV[i][:, j, :]
                    vkey = VV[i]
                if kind != "diff":
                    c0 = g * 128 if g is not None else 0
                    pof = po[0][:].rearrange("p a b -> p (a b)")
                    st_ = first.get((Q, m), True)
                    P.op('pe', lambda e: e.matmul(pof[0:65, c0:512], vap, pt[:, c0:512], start=st_, stop=(j == nkt - 1), skip_group_check=True),
                         reads=[pt, vkey], writes=[po[0]])
                    first[(Q, m)] = False
                    if j == nkt - 1:
                        evac_t(Q, s)
                    return
                for gp in range(g if g is not None else 0, 4):
                    st_ = first.get((Q, m), True)
                    P.op('pe', lambda e: e.matmul(po[m][:, gp, 0:65], pt[:, gp * 128:(gp + 1) * 128], vap, start=st_, stop=(j == nkt - 1 and gp == 3),
                                                  skip_group_check=True),
                         reads=[pt, vkey], writes=[po[m]])
                    first[(Q, m)] = False
                if j == nkt - 1 and m == nmap - 1:
                    evac_a(Q, s)

            def evac_t(Q, s):
                po = pobuf[Q]
                pof = po[0][:].rearrange("p a b -> p (a b)")
                rrow, oraw = rrows[Q % 2], oraws[Q % 2]
                P.op('dve', lambda e: e.tensor_copy(oraw[:], pof[0:64, :]), reads=[po[0]], writes=[oraw])
                P.op('dve', lambda e: e.reciprocal(rrow[64:65, :], pof[64:65, :]), reads=[po[0]], writes=[rrow])
                deferred.append((s + (3 if kind == "mem" else 8), ("t", Q)))

            def evac_tb(Q):
                po = pobuf[Q]
                rrow, oraw = rrows[Q % 2], oraws[Q % 2]
                pbf = po[1][:].rearrange("p a b -> p (a b)")
                P.op('pe', lambda e: e.matmul(pbf[0:64, :], ones32[64:65, 0:64], rrow[64:65, :], start=True, stop=True), reads=[ones32, rrow], writes=[po[1]])
                ot = oT[Q % 2]
                P.op('dve', lambda e: e.tensor_tensor(ot[:], pbf[0:64, :], oraw[:], ALU.mult), reads=[po[1], oraw], writes=[ot])
                P.dma('sp', MIXT[u["out"]:u["out"] + 64, Q * 512:(Q + 1) * 512], ot[:], reads=[ot], writes=["MIXT"])

            def evac_a(Q, s):
                po = pobuf[Q]
                ob = osb[Q % 2]
                if kind != "diff":
                    P.op('dve', lambda e: e.reciprocal(rec[:, 0:4], po[0][:, :, 64]), reads=[po[0]], writes=[rec])
                    P.op('dve', lambda e: e.tensor_tensor(ob[:], po[0][:, :, 0:64], rec[:, 0:4].unsqueeze(2).to_broadcast([128, 4, 64]), ALU.mult),
                         reads=[po[0], rec], writes=[ob])
                    delay = 3
                else:
                    P.op('dve', lambda e: e.reciprocal(rec[:, 0:4], po[0][:, :, 64]), reads=[po[0]], writes=[rec])
                    P.op('dve', lambda e: e.reciprocal(rec[:, 4:8], po[1][:, :, 64]), reads=[po[1]], writes=[rec])
                    P.op('dve', lambda e: e.tensor_scalar(rec[:, 4:8], rec[:, 4:8], lamn[:, 0:1], None, ALU.mult), reads=[rec, lamn], writes=[rec])
                    P.op('dve', lambda e: e.tensor_tensor(da[:], po[0][:, :, 0:64], rec[:, 0:4].unsqueeze(2).to_broadcast([128, 4, 64]), ALU.mult), reads=[po[0], rec], writes=[da])
                    P.op('dve', lambda e: e.tensor_tensor(db[:], po[1][:, :, 0:64], rec[:, 4:8].unsqueeze(2).to_broadcast([128, 4, 64]), ALU.mult), reads=[po[1], rec], writes=[db])
                    P.op('dve', lambda e: e.tensor_tensor(da[:], da[:], db[:], ALU.add), reads=[da, db], writes=[da])
                    P.op('pool', lambda e: e.tensor_tensor(dsq[:], da[:], da[:], ALU.mult), reads=[da], writes=[dsq])
                    P.op('dve', lambda e: e.tensor_reduce(dss[:], dsq[:], AX.X, ALU.add), reads=[dsq], writes=[dss])
                    P.op('act', lambda e: e.activation(dss[:], dss[:], AF.Ln, bias=epsb[:, 0:1], scale=1.0 / 64), reads=[dss, epsb], writes=[dss])
                    P.op('act', lambda e: e.activation(dss[:], dss[:], AF.Exp, scale=-0.5), reads=[dss], writes=[dss])
                    P.op('dve', lambda e: e.tensor_tensor(da[:], da[:], dss[:].unsqueeze(2).to_broadcast([128, 4, 64]), ALU.mult), reads=[da, dss], writes=[da])
                    P.op('dve', lambda e: e.tensor_tensor(ob[:], da[:], gsub[:].unsqueeze(1).to_broadcast([128, 4, 64]), ALU.mult), reads=[da, gsub], writes=[ob])
                    delay = 8
                deferred.append((s + delay, Q))

            def evac_b(Q):
                ob = osb[Q % 2]
                for gp in range(4):
                    P.op('pe', lambda e: e.transpose(pTO[0:64, gp, :], ob[:, gp, :], idb[:]), reads=[ob, idb], writes=[pTO])
                ot = oT[Q % 2]
                P.op('dve', lambda e: e.tensor_copy(ot[:], pTO[0:64, :, :].rearrange("p a b -> p (a b)")), reads=[pTO], writes=[ot])
                P.dma('sp', MIXT[u["out"]:u["out"] + 64, Q * 512:(Q + 1) * 512], ot[:], reads=[ot], writes=["MIXT"])

            def run_deferred(item):
                if isinstance(item, tuple):
                    evac_tb(item[1])
                else:
                    evac_b(item)

            LAG = 2
            ns = len(steps)
            for s in range(ns + LAG):
                if s < ns:
                    front(s)
                if s - LAG >= 0:
                    back(s - LAG)
                while deferred and deferred[0][0] <= s - LAG:
                    run_deferred(deferred.pop(0)[1])
            while deferred:
                run_deferred(deferred.pop(0)[1])
        P.barrier()


def phase3(nc, P, din, l, xin, MIXT, X1, xout, C, wg, wu, wd, gff):
    idb, mhalf, epsb = C["idb"], C["mhalf"], C["epsb"]
    TT = 256
    ntt = S // TT
    with ExitStack() as st:
        def sbuf(name, shape, dt):
            return st.enter_context(nc.sbuf_tensor(f"{name}_o{l}", shape, dt))

        def psum(name, shape, dt):
            return st.enter_context(nc.psum_tensor(f"{name}_o{l}", shape, dt))
        wo = sbuf("wo", [128, 10, D], BF16)
        mixT = [sbuf(f"mixT{i}", [128, 10, TT], BF16) for i in range(2)]
        xr = [sbuf(f"xr{i}", [128, 2, D], F32) for i in range(2)]
        x1 = xr
        pY = [psum(f"pY{i}", [128, 512], F32) for i in range(8)]
        P.dma('pool', wo[:], din["w_o"][l].rearrange("(k p) n -> p k n", p=128), writes=[wo])

        def loads(tt, i):
            P.dma('sp', mixT[i][:], MIXT[:, tt * TT:(tt + 1) * TT].rearrange("(k p) t -> p k t", p=128), reads=["MIXT"], writes=[mixT[i]])
            P.dma('sp', xr[i][:], xin[tt * TT:(tt + 1) * TT, :].rearrange("(g p) d -> p g d", p=128), reads=["xin"], writes=[xr[i]])
        loads(0, 0)
        for tt in range(ntt):
            i = tt % 2
            if tt + 1 < ntt:
                loads(tt + 1, (tt + 1) % 2)
            for g in range(2):
                for hf in range(2):
                    py = pY[i * 4 + g * 2 + hf]
                    for k in range(10):
                        P.op('pe', lambda e: e.matmul(py[:], mixT[i][:, k, g * 128:(g + 1) * 128], wo[:, k, hf * 512:(hf + 1) * 512], start=(k == 0), stop=(k == 9)),
                             reads=[mixT[i], wo], writes=[py])
                    P.op('dve', lambda e: e.tensor_tensor(x1[i][:, g, hf * 512:(hf + 1) * 512], py[:], xr[i][:, g, hf * 512:(hf + 1) * 512], ALU.add),
                         reads=[py, xr[i]], writes=[x1[i]])
            P.dma('sp', X1[tt * TT:(tt + 1) * TT, :].rearrange("(g p) d -> p g d", p=128), x1[i][:], reads=[x1[i]], writes=["X1"])
        P.barrier()

    with ExitStack() as st:
        def sbuf(name, shape, dt):
            return st.enter_context(nc.sbuf_tensor(f"{name}_f{l}", shape, dt))

        def psum(name, shape, dt):
            return st.enter_context(nc.psum_tensor(f"{name}_f{l}", shape, dt))
        cw = sbuf("cw", [128, NCH, 3], F32)
        cb = sbuf("cb", [128, NCH], F32)
        halo = sbuf("halo", [128, NCH, 2], F32)
        x1 = [sbuf(f"x1{i}", [128, 2, D], F32) for i in range(2)]
        junk = sbuf("junk", [128, D], BF16)
        xs2 = [sbuf(f"xs{i}", [128, 2, D], BF16) for i in range(2)]
        xn2T2 = [sbuf(f"xn2T{i}", [128, 8, TT], BF16) for i in range(2)]
        ss2 = [sbuf(f"ss{i}", [128, 2], F32) for i in range(2)]
        rs2 = [sbuf(f"rs{i}", [128, 2], F32) for i in range(2)]
        gsb = [sbuf(f"gsb{i}", [128, TT + 2], F32) for i in range(2)]
        acc = [sbuf(f"acc{i}", [128, TT], F32) for i in range(2)]
        sg = [sbuf(f"sg{i}", [128, TT], F32) for i in range(2)]
        hT = sbuf("hT", [128, NCH, TT], BF16)
        pY = [psum(f"pY{i}", [128, 512], F32) for i in range(2)]
        pT = [psum(f"pT{i}", [128, 8, 128], BF16) for i in range(2)]
        pG = [psum(f"pG{i}", [128, 512], F32) for i in range(2)]
        pU = [psum(f"pU{i}", [128, 512], F32) for i in range(2)]

        P.dma('sp', cw[:], din["convw"][l], writes=[cw])
        P.dma('sp', cb[:], din["convb"][l], writes=[cb])
        P.op('pool', lambda e: e.memset(halo[:], 0.0), writes=[halo])
        wdk = [(wd, 0), (wd, 11)]

        def loads2(tt, i):
            P.dma('sp', x1[i][:], X1[tt * TT:(tt + 1) * TT, :].rearrange("(g p) d -> p g d", p=128), reads=["X1"], writes=[x1[i]])
        def norm_part(i):
            xs, ss, rs = xs2[i], ss2[i], rs2[i]
            for g in range(2):
                P.op('act', lambda e: e.activation(junk[:], x1[i][:, g, :], AF.Square, accum_out=ss[:, g:g + 1]), reads=[x1[i]], writes=[junk, (ss, g)], embed=False)
                P.op('act', lambda e: e.activation(rs[:, g:g + 1], ss[:, g:g + 1], AF.Ln, bias=epsb[:, 0:1], scale=1.0 / D), reads=[(ss, g), epsb], writes=[(rs, g)])
                P.op('act', lambda e: e.activation(rs[:, g:g + 1], rs[:, g:g + 1], AF.Exp, scale=-0.5), reads=[(rs, g)], writes=[(rs, g)])
                P.op('dve', lambda e: e.tensor_scalar(xs[:, g, :], x1[i][:, g, :], rs[:, g:g + 1], None, ALU.mult), reads=[x1[i], (rs, g)], writes=[(xs, g)])

        def transpose_part(i):
            xs, xn2T = xs2[i], xn2T2[i]
            for g in range(2):
                for k in range(8):
                    P.op('pe', lambda e: e.transpose(pT[g][:, k, :], xs[:, g, k * 128:(k + 1) * 128], idb[:]), reads=[(xs, g), idb], writes=[pT[g]])
                for k in range(8):
                    P.op('act', lambda e: e.activation(xn2T[:, k, g * 128:(g + 1) * 128], pT[g][:, k, :], AF.Copy, scale=gff[:, k:k + 1]), reads=[pT[g], gff], writes=[xn2T])

        loads2(0, 0)
        norm_part(0)
        transpose_part(0)
        for tt in range(ntt):
            i = tt % 2
            xn2T = xn2T2[i]
            if tt + 1 < ntt:
                loads2(tt + 1, (tt + 1) % 2)
            for c in range(NCH):
                j = c % 2
                for k in range(8):
                    P.op('pe', lambda e: e.matmul(pG[j][:, 0:TT], wg[:, k, c * 128:(c + 1) * 128], xn2T[:, k, :], start=(k == 0), stop=(k == 7)),
                         reads=[(wg, k), xn2T], writes=[pG[j]])
                for k in range(8):
                    P.op('pe', lambda e: e.matmul(pU[j][:, 0:TT], wu[:, k, c * 128:(c + 1) * 128], xn2T[:, k, :], start=(k == 0), stop=(k == 7)),
                         reads=[(wu, k), xn2T], writes=[pU[j]])
                P.op('pool', lambda e: e.tensor_copy(gsb[j][:, 0:2], halo[:, c, :]), reads=[(halo, c)], writes=[gsb[j]])
                P.op('act', lambda e: e.copy(gsb[j][:, 2:TT + 2], pG[j][:, 0:TT]), reads=[pG[j]], writes=[gsb[j]])
                P.op('pool', lambda e: e.tensor_copy(halo[:, c, :], gsb[j][:, TT:TT + 2]), reads=[gsb[j]], writes=[(halo, c)])
                P.op('dve', lambda e: e.tensor_scalar(acc[j][:], gsb[j][:, 2:TT + 2], cw[:, c, 2:3], cb[:, c:c + 1], ALU.mult, ALU.add), reads=[gsb[j], cw, cb], writes=[acc[j]])
                P.op('dve', lambda e: e.scalar_tensor_tensor(acc[j][:], gsb[j][:, 1:TT + 1], cw[:, c, 1:2], acc[j][:], ALU.mult, ALU.add), reads=[gsb[j], cw, acc[j]], writes=[acc[j]])
                P.op('dve', lambda e: e.scalar_tensor_tensor(acc[j][:], gsb[j][:, 0:TT], cw[:, c, 0:1], acc[j][:], ALU.mult, ALU.add), reads=[gsb[j], cw, acc[j]], writes=[acc[j]])
                P.op('act', lambda e: e.activation(sg[j][:], acc[j][:], AF.Silu), reads=[acc[j]], writes=[sg[j]])
                P.op('dve', lambda e: e.tensor_tensor(hT[:, c, :], pU[j][:, 0:TT], sg[j][:], ALU.mult), reads=[pU[j], sg[j]], writes=[(hT, c)])
                if c == 10 and tt + 1 < ntt:
                    norm_part((tt + 1) % 2)
            if tt + 1 < ntt:
                transpose_part((tt + 1) % 2)
            for g in range(2):
                for hf in range(2):
                    py = pY[hf]
                    for c in range(NCH):
                        P.op('pe', lambda e: e.matmul(py[:], hT[:, c, g * 128:(g + 1) * 128], wd[:, c, hf * 512:(hf + 1) * 512], start=(c == 0), stop=(c == NCH - 1)),
                             reads=[(hT, c), wdk[0 if c < 11 else 1]], writes=[py])
                    P.op('dve', lambda e: e.tensor_tensor(x1[i][:, g, hf * 512:(hf + 1) * 512], py[:], x1[i][:, g, hf * 512:(hf + 1) * 512], ALU.add),
                         reads=[py, x1[i]], writes=[x1[i]])
            P.dma('sp', xout[tt * TT:(tt + 1) * TT, :].rearrange("(g p) d -> p g d", p=128), x1[i][:], reads=[x1[i]], writes=["xout"])
        P.barrier()


def _host_inputs(inputs):
    f = lambda a: np.ascontiguousarray(np.asarray(a, dtype=np.float32))
    ident = np.eye(128, dtype=np.float32)
    tri = np.triu(np.ones((128, 128), dtype=np.float32))
    theta = 500000.0
    invs = []
    for rot in (32, 16, 8):
        invs.append(theta ** (-np.arange(0, rot, 2, dtype=np.float32) / rot))
    invf = np.concatenate(invs).astype(np.float32).reshape(1, NFREQ)

    def pk(v, nk):
        v = f(v)
        return np.ascontiguousarray(v.reshape(L, nk, 128).transpose(0, 2, 1))
    g_cq = np.zeros((L, 256), np.float32)
    g_cq[:, :192] = f(inputs["mla_cq_norm"])
    vecs = np.zeros((L, 1, 1024), np.float32)

    def put(name, arr):
        o, w = VOFF[name]
        vecs[:, 0, o:o + w] = f(arr).reshape(L, w)
    put("mla_q", inputs["mla_q_norm"]); put("mla_k", inputs["mla_k_norm"]); put("fox_b", inputs["fox_b_f"])
    put("fox_q", inputs["fox_q_norm"]); put("fox_k", inputs["fox_k_norm"]); put("moba_q", inputs["moba_q_norm"])
    put("moba_k", inputs["moba_k_norm"]); put("lam", inputs["diff_lambda"]); put("diff_q", inputs["diff_q_norm"])
    put("diff_k", inputs["diff_k_norm"]); put("diff_sub", inputs["diff_sub_norm"]); put("mem_q", inputs["mem_q_norm"])
    put("mem_k", inputs["mem_k_norm"])
    convw = np.ascontiguousarray(f(inputs["ffn_conv_w"]).reshape(L, 3, NCH, 128).transpose(0, 3, 2, 1))
    convb = np.ascontiguousarray(f(inputs["ffn_conv_b"]).reshape(L, NCH, 128).transpose(0, 2, 1))
    shared = {
        "ident": ident, "tri": tri, "invf": invf,
        "w_in": f(inputs["w_in"]), "w_uq": f(inputs["mla_w_uq"]), "w_ukv": f(inputs["mla_w_ukv"]),
        "w_memkv": f(inputs["mem_w_kv"]), "w_o": f(inputs["w_o"]), "w_gate": f(inputs["ffn_w_gate"]),
        "w_up": f(inputs["ffn_w_up"]), "w_down": f(inputs["ffn_w_down"]),
        "g_attn": pk(inputs["attn_norm"], 8), "g_ffn": pk(inputs["ffn_norm"], 8), "g_mem": pk(inputs["mem_norm"], 8),
        "g_cq": pk(g_cq, 2), "g_ckv": pk(inputs["mla_ckv_norm"], 1),
        "vecs": vecs, "convw": convw, "convb": convb,
    }
    x = f(inputs["x"]); mem = f(inputs["mem"]); pos = np.asarray(inputs["positions"]).astype(np.int32)
    maps = []
    for b in range(x.shape[0]):
        m = dict(shared)
        m["x"] = x[b]
        m["mem"] = mem[b]
        m["pos"] = np.ascontiguousarray(pos[b].reshape(NT, 128).T)
        maps.append(m)
    return maps


_NC_CACHE = {}


def kernel(**inputs):
    maps = _host_inputs(inputs)
    if "nc" not in _NC_CACHE:
        _NC_CACHE["nc"] = build()
    nc = _NC_CACHE["nc"]
    res = run_bass_kernel_spmd(nc, maps, core_ids=list(range(8)))
    return np.stack([np.asarray(r["out"], dtype=np.float32) for r in res.results], axis=0)
```

```python
import math
from contextlib import ExitStack
import numpy as np
import concourse.bass as bass
import concourse.mybir as mybir
from concourse.bass_utils import run_bass_kernel_spmd

F32 = mybir.dt.float32
BF16 = mybir.dt.bfloat16
I32 = mybir.dt.int32
AF = mybir.ActivationFunctionType
ALU = mybir.AluOpType
AX = mybir.AxisListType

S = 4096
D = 1024
NT = 32
DIN = 2916
DFF = 2816
NCH = 22
L = 2
EPS = 1e-6
NEG = -30000.0
C_CQ, C_CKV, C_KR, C_FQ, C_FK, C_FV, C_FF = 0, 192, 320, 352, 608, 864, 1120
C_MQ, C_MK, C_MV, C_DQ, C_DK, C_DV, C_EQ = 1124, 1380, 1636, 1892, 2148, 2404, 2660
NBLK = 30
NFREQ = 28
DBG_UNITS = None


class Prog:
    ENGS = ('pe', 'act', 'dve', 'pool', 'sp')

    def __init__(self, nc, es, n_dma=56, epoch=10**9):
        self.nc = nc
        self.es = es
        self.eng = {'pe': nc.tensor, 'act': nc.scalar, 'dve': nc.vector,
                    'pool': nc.gpsimd, 'sp': nc.sync}
        self.EPOCH = epoch
        self.nsem = 0
        self.sem = {e: self._newsem() for e in self.ENGS}
        self.ep = {e: 0 for e in self.ENGS}
        self.cnt = {e: 0 for e in self.ENGS}
        self.known = {e: {} for e in self.ENGS}
        self.known_ep = {e: {} for e in self.ENGS}
        self.semof = {(e, 0): self.sem[e] for e in self.ENGS}
        self.dsem = [self._newsem() for _ in range(n_dma)]
        self.dval = [0] * n_dma
        self.dnext = 0
        self.dnext_sw = 0
        self.NHW = n_dma - 16
        self.known_d = {e: [0] * n_dma for e in self.ENGS}
        self.last_w = {}
        self.readers = {}
        self.nwaits = 0
        self.nops = 0

    def _newsem(self):
        self.nsem += 1
        return self.es.enter_context(self.nc.semaphore(f"s{self.nsem}"))

    def _wait(self, E, tok):
        if tok[0] == 'd':
            _, k, v = tok
            if self.known_d[E][k] >= v:
                return
            self.eng[E].wait_ge(self.dsem[k], v)
            self.known_d[E][k] = v
            self.nwaits += 1
        else:
            _, X, ep, c = tok
            if self.known_ep[E].get(X, -1) > ep:
                return
            if self.known[E].get((X, ep), 0) >= c:
                return
            self.eng[E].wait_ge(self.semof[(X, ep)], c)
            self.known[E][(X, ep)] = c
            if self.known_ep[E].get(X, -1) < ep:
                self.known_ep[E][X] = ep
            self.nwaits += 1

    @staticmethod
    def _k(k):
        if isinstance(k, str):
            return k
        if isinstance(k, tuple):
            return tuple(x if isinstance(x, (int, str)) else x.name for x in k)
        return k.name

    def _deps(self, E, reads, writes, defer_last=False):
        deps = []
        for r in reads:
            w = self.last_w.get(r)
            if w is not None:
                deps.append(w)
        for wk in writes:
            lw = self.last_w.get(wk)
            if lw is not None and not (lw[0] == 'e' and lw[1] == E and E == 'pe'):
                deps.append(lw)
            for rd in self.readers.get(wk, {}).values():
                if not (rd[0] == 'e' and rd[1] == E and E == 'pe'):
                    deps.append(rd)
        if not defer_last:
            for d in deps:
                self._wait(E, d)
            return None
        need = [d for d in deps if self._needed(E, d)]
        for d in need[:-1]:
            self._wait(E, d)
        if need and self._needed(E, need[-1]):
            return need[-1]
        return None

    def _needed(self, E, tok):
        if tok[0] == 'd':
            return self.known_d[E][tok[1]] < tok[2]
        _, X, ep, c = tok
        if self.known_ep[E].get(X, -1) > ep:
            return False
        return self.known[E].get((X, ep), 0) < c

    def _embed(self, E, ins, tok):
        if tok[0] == 'd':
            _, k, v = tok
            ins._wait_ge(self.dsem[k], v)
            self.known_d[E][k] = v
        else:
            _, X, ep, c = tok
            ins._wait_ge(self.semof[(X, ep)], c)
            self.known[E][(X, ep)] = c
            if self.known_ep[E].get(X, -1) < ep:
                self.known_ep[E][X] = ep

    def _record(self, tok, E, reads, writes):
        for r in reads:
            self.readers.setdefault(r, {})[(E, tok[0])] = tok
        for wk in writes:
            self.last_w[wk] = tok
            self.readers[wk] = {}

    def op(self, E, fn, reads=(), writes=(), embed=True):
        reads = [self._k(r) for r in reads]
        writes = [self._k(w) for w in writes]
        last = self._deps(E, reads, writes, defer_last=(embed and E in ('act', 'dve', 'pool')))
        ins = fn(self.eng[E])
        if last is not None:
            self._embed(E, ins, last)
        self.cnt[E] += 1
        ins.then_inc(self.sem[E], 1)
        tok = ('e', E, self.ep[E], self.cnt[E])
        self._record(tok, E, reads, writes)
        self.nops += 1
        if self.cnt[E] >= self.EPOCH:
            self.ep[E] += 1
            self.cnt[E] = 0
            self.sem[E] = self._newsem()
            self.semof[(E, self.ep[E])] = self.sem[E]
        return tok

    def dma(self, E, out, in_, reads=(), writes=(), **kw):
        reads = [self._k(r) for r in reads]
        writes = [self._k(w) for w in writes]
        if E == 'pool':
            k = self.NHW + self.dnext_sw
            self.dnext_sw = (self.dnext_sw + 1) % (len(self.dsem) - self.NHW)
        else:
            k = self.dnext
            self.dnext = (self.dnext + 1) % self.NHW
        if self.dval[k] > 0:
            self._wait(E, ('d', k, self.dval[k]))
        self._deps(E, reads, writes)
        self.eng[E].dma_start(out=out, in_=in_, **kw).then_inc(self.dsem[k], 16)
        self.dval[k] += 16
        tok = ('d', k, self.dval[k])
        self._record(tok, E, reads, writes)
        return tok

    def barrier(self):
        toks = []
        for X in self.ENGS:
            if self.cnt[X] > 0:
                toks.append(('e', X, self.ep[X], self.cnt[X]))
            elif self.ep[X] > 0:
                toks.append(('e', X, self.ep[X] - 1, self.EPOCH))
        for k, v in enumerate(self.dval):
            if v > 0:
                toks.append(('d', k, v))
        for E in self.ENGS:
            for t in toks:
                self._wait(E, t)
        self.last_w = {}
        self.readers = {}


def _param_specs():
    return {
        "x": ([S, D], F32), "mem": ([256, D], F32), "pos": ([128, NT], I32),
        "ident": ([128, 128], F32), "tri": ([128, 128], F32), "invf": ([1, NFREQ], F32),
        "w_in": ([L, D, DIN], F32), "w_uq": ([L, 192, 384], F32), "w_ukv": ([L, 128, 512], F32),
        "w_memkv": ([L, D, 512], F32), "w_o": ([L, 1280, D], F32),
        "w_gate": ([L, D, DFF], F32), "w_up": ([L, D, DFF], F32), "w_down": ([L, DFF, D], F32),
        "g_attn": ([L, 128, 8], F32), "g_ffn": ([L, 128, 8], F32), "g_mem": ([L, 128, 8], F32),
        "g_cq": ([L, 128, 2], F32), "g_ckv": ([L, 128, 1], F32),
        "vecs": ([L, 1, 1024], F32),
        "convw": ([L, 128, NCH, 3], F32), "convb": ([L, 128, NCH], F32),
    }


VOFF = {}
_o = 0
for _n, _w in [("mla_q", 96), ("mla_k", 96), ("fox_b", 4), ("fox_q", 64), ("fox_k", 64), ("moba_q", 64),
               ("moba_k", 64), ("lam", 128), ("diff_q", 32), ("diff_k", 32), ("diff_sub", 64),
               ("mem_q", 64), ("mem_k", 64)]:
    VOFF[_n] = (_o, _w)
    _o += _w
assert _o <= 1024


def build(dbg=None, n_layers=L, phases=(1, 2, 3)):
    nc = bass.Bass("TRN2", target_bir_lowering=False)
    din = {}
    for name, (shape, dt) in _param_specs().items():
        din[name] = nc.dram_tensor(name, shape, dt, kind="ExternalInput").ap()
    out_d = nc.dram_tensor("out", [S, D], F32, kind="ExternalOutput").ap()

    def scratch(name, shape, dt):
        kind = "ExternalOutput" if (dbg and name in dbg) else "Internal"
        return nc.dram_tensor(name, shape, dt, kind=kind).ap()
    FTD = scratch("FTD", [NBLK, 128, S], BF16)
    VD = scratch("VD", [16, 128, NT, 65], BF16)
    MIXT = scratch("MIXT", [1280, S], BF16)
    XS = [scratch("XS0", [S, D], F32), scratch("XS1", [S, D], F32)]
    X1 = scratch("X1", [S, D], F32)

    with ExitStack() as es:
        P = Prog(nc, es)

        def sbuf(st, name, shape, dt):
            return st.enter_context(nc.sbuf_tensor(name, shape, dt))

        def psum(st, name, shape, dt):
            return st.enter_context(nc.psum_tensor(name, shape, dt))

        idf = sbuf(es, "idf", [128, 128], F32)
        idb = sbuf(es, "idb", [128, 128], BF16)
        trif = sbuf(es, "trif", [128, 128], F32)
        trib = sbuf(es, "trib", [128, 128], BF16)
        ones32 = sbuf(es, "ones32", [128, 128], F32)
        c256 = sbuf(es, "c256", [128, 1], F32)
        mhalf = sbuf(es, "mhalf", [128, 64], F32)
        epsb = sbuf(es, "epsb", [128, 1], F32)
        cosT = sbuf(es, "cosT", [128, NT, NFREQ], F32)
        sinT = sbuf(es, "sinT", [128, NT, NFREQ], F32)
        cpos = sbuf(es, "cpos", [128, NT, 4], F32)
        rtot = sbuf(es, "rtot", [128, NT + 1, 4], F32)
        fbias = sbuf(es, "fbias", [128, 4, 8, NT], F32)
        vecs = sbuf(es, "vecs_sb", [128, 1024], F32)
        gsub = sbuf(es, "gsub", [128, 64], F32)
        lamn = sbuf(es, "lamn", [128, 1], F32)
        ktm = sbuf(es, "ktm", [128, 2, 256], BF16)
        vmem = sbuf(es, "vmem", [128, 2, 4, 65], BF16)

        P.dma('sp', idf[:], din["ident"][:, :], writes=[idf])
        P.dma('sp', trif[:], din["tri"][:, :], writes=[trif])
        P.op('dve', lambda e: e.tensor_copy(idb[:], idf[:]), reads=[idf], writes=[idb])
        P.op('dve', lambda e: e.tensor_copy(trib[:], trif[:]), reads=[trif], writes=[trib])
        P.op('pool', lambda e: e.memset(ones32[:], 1.0), writes=[ones32])
        P.op('pool', lambda e: e.memset(c256[:], 1.0 / 256), writes=[c256])
        P.op('pool', lambda e: e.memset(mhalf[:], -0.5), writes=[mhalf])
        P.op('pool', lambda e: e.memset(epsb[:], EPS), writes=[epsb])
        P.op('pool', lambda e: e.memset(rtot[:], 0.0), writes=[rtot])
        P.op('pool', lambda e: e.memset(fbias[:], 0.0), writes=[fbias])

        with ExitStack() as st:
            posi = sbuf(st, "posi", [128, NT], I32)
            posf = sbuf(st, "posf", [128, NT], F32)
            invf = sbuf(st, "invf_sb", [128, NFREQ], F32)
            tt = sbuf(st, "tt", [128, NT, NFREQ], F32)
            ti = sbuf(st, "ti", [128, NT, NFREQ], I32)
            tf = sbuf(st, "tf", [128, NT, NFREQ], F32)
            mk = sbuf(st, "mk", [128, NT, NFREQ], F32)
            P.dma('sp', posi[:], din["pos"][:, :], writes=[posi])
            P.dma('sp', invf[:], din["invf"].partition_broadcast(128), writes=[invf])
            P.op('dve', lambda e: e.tensor_copy(posf[:], posi[:]), reads=[posi], writes=[posf])
            P.op('dve', lambda e: e.tensor_tensor(tt[:], posf[:].unsqueeze(2).to_broadcast([128, NT, NFREQ]),
                                                  invf[:].unsqueeze(1).to_broadcast([128, NT, NFREQ]), ALU.mult),
                 reads=[posf, invf], writes=[tt])
            for which, dst in ((0, sinT), (1, cosT)):
                sh = 0.25 * which
                P.op('dve', lambda e: e.tensor_scalar(tf[:], tt[:], 1.0 / (2 * math.pi), sh, ALU.mult, ALU.add),
                     reads=[tt], writes=[tf])
                P.op('dve', lambda e: e.tensor_copy(ti[:], tf[:]), reads=[tf], writes=[ti])
                P.op('dve', lambda e: e.tensor_copy(mk[:], ti[:]), reads=[ti], writes=[mk])
                P.op('dve', lambda e: e.tensor_tensor(tf[:], tf[:], mk[:], ALU.subtract), reads=[tf, mk], writes=[tf])
                P.op('dve', lambda e: e.tensor_scalar(mk[:], tf[:], 0.5, None, ALU.is_gt), reads=[tf], writes=[mk])
                P.op('dve', lambda e: e.tensor_tensor(tf[:], tf[:], mk[:], ALU.subtract), reads=[tf, mk], writes=[tf])
                P.op('dve', lambda e: e.tensor_scalar(mk[:], tf[:], -0.5, None, ALU.is_lt), reads=[tf], writes=[mk])
                P.op('dve', lambda e: e.tensor_tensor(tf[:], tf[:], mk[:], ALU.add), reads=[tf, mk], writes=[tf])
                P.op('act', lambda e: e.activation(dst[:], tf[:], AF.Sin, scale=2 * math.pi), reads=[tf], writes=[dst])
            P.barrier()

        xin = din["x"]
        for l in range(n_layers):
            xout = out_d if l == n_layers - 1 else XS[l % 2]
            if 1 in phases:
                phase1(nc, P, din, l, xin, FTD, VD, locals())
            with ExitStack() as lst:
                wg = lst.enter_context(nc.sbuf_tensor(f"wg_l{l}", [128, 8, DFF], BF16))
                wu = lst.enter_context(nc.sbuf_tensor(f"wu_l{l}", [128, 8, DFF], BF16))
                gff = lst.enter_context(nc.sbuf_tensor(f"gff_l{l}", [128, 8], F32))
                P.dma('sp', gff[:], din["g_ffn"][l], writes=[gff])
                wgv = din["w_gate"][l].rearrange("(k p) n -> p k n", p=128)
                wuv = din["w_up"][l].rearrange("(k p) n -> p k n", p=128)
                for k in range(8):
                    P.dma('pool', wg[:, k, :], wgv[:, k, :], writes=[(wg, k)])
                    P.dma('pool', wu[:, k, :], wuv[:, k, :], writes=[(wu, k)])

                def fold_gains():
                    for k in range(8):
                        P.op('dve', lambda e: e.tensor_scalar(wg[:, k, :], wg[:, k, :], gff[:, k:k + 1], None, ALU.mult), reads=[(wg, k), gff], writes=[(wg, k)])
                        P.op('pool', lambda e: e.tensor_scalar(wu[:, k, :], wu[:, k, :], gff[:, k:k + 1], None, ALU.mult), reads=[(wu, k), gff], writes=[(wu, k)])
                if 2 in phases:
                    phase2(nc, P, din, l, FTD, VD, MIXT, locals())
                if 3 in phases:
                    wd = lst.enter_context(nc.sbuf_tensor(f"wd_l{l}", [128, NCH, D], BF16))
                    wdv = din["w_down"][l].rearrange("(k p) n -> p k n", p=128)
                    for k0 in range(0, NCH, 11):
                        P.dma('pool', wd[:, k0:k0 + 11, :], wdv[:, k0:k0 + 11, :], writes=[(wd, k0)])
                    phase3(nc, P, din, l, xin, MIXT, X1, xout, locals(), wg, wu, wd, gff)
                    P.barrier()
            xin = xout
        P.barrier()
        print("program: ops", P.nops, "waits", P.nwaits, "sems", P.nsem)
    return nc


def _v(vecs, name, lo=0, hi=None):
    o, w = VOFF[name]
    hi = w if hi is None else hi
    return vecs[:, o + lo:o + hi]


def phase1(nc, P, din, l, xin, FTD, VD, C):
    idb, idf, trif, ones32, c256, mhalf = C["idb"], C["idf"], C["trif"], C["ones32"], C["c256"], C["mhalf"]
    cosT, sinT, cpos, rtot, fbias, vecs = C["cosT"], C["sinT"], C["cpos"], C["rtot"], C["fbias"], C["vecs"]
    gsub, lamn, ktm, vmem = C["gsub"], C["lamn"], C["ktm"], C["vmem"]
    epsb = C["epsb"]
    with ExitStack() as st:
        def sbuf(name, shape, dt):
            return st.enter_context(nc.sbuf_tensor(f"{name}_l{l}", shape, dt))

        def psum(name, shape, dt):
            return st.enter_context(nc.psum_tensor(f"{name}_l{l}", shape, dt))
        win = sbuf("win", [128, 8, DIN], BF16)
        wmk = sbuf("wmk", [128, 8, 512], BF16)
        wuq = sbuf("wuq", [128, 2, 384], BF16)
        wukv = sbuf("wukv", [128, 512], BF16)
        gat = sbuf("gat", [128, 8], F32)
        gme = sbuf("gme", [128, 8], F32)
        gcq = sbuf("gcq", [128, 2], F32)
        gckv = sbuf("gckv", [128, 1], F32)
        xt = [sbuf(f"xt{i}", [128, D], F32) for i in range(2)]
        junk = sbuf("junk", [128, DIN], BF16)
        xs = [sbuf(f"xs{i}", [128, D], BF16) for i in range(2)]
        xnT = [sbuf(f"xnT{i}", [128, 8, 128], BF16) for i in range(2)]
        hsb = [sbuf(f"hsb{i}", [128, DIN], F32) for i in range(2)]
        sqh = sbuf("sqh", [128, DIN], F32)
        ssx = sbuf("ssx", [128, 1], F32)
        rsx = sbuf("rsx", [128, 1], F32)
        ssA = sbuf("ssA", [128, 39], F32)
        rsA = sbuf("rsA", [128, 39], F32)
        rsAs = [sbuf(f"rsAs{i}", [128, 39], F32) for i in range(2)]
        invGA = sbuf("invGA", [128, 39], F32)
        ssB = sbuf("ssB", [128, 12], F32)
        rsB = sbuf("rsB", [128, 12], F32)
        invGB = sbuf("invGB", [128, 12], F32)
        cqn = sbuf("cqn", [128, 3, 128], BF16)
        cT = sbuf("cT", [128, 3, 128], BF16)
        qa = sbuf("qa", [128, 384], F32)
        kva = sbuf("kva", [128, 512], F32)
        sqb = sbuf("sqb", [128, 512], F32)
        sqk = sbuf("sqk", [128, 512], F32)
        t512 = sbuf("t512", [128, 512], F32)
        mqr = sbuf("mqr", [128, 512], F32)
        dqr = sbuf("dqr", [128, 512], F32)
        qar = sbuf("qar", [128, 4, 32], F32)
        krr = sbuf("krr", [128, 32], F32)
        r1 = sbuf("r1", [128, 256], F32)
        r2 = sbuf("r2", [128, 256], F32)
        r3 = sbuf("r3", [128, 256], F32)
        r4 = sbuf("r4", [128, 256], F32)
        zf = sbuf("zf", [128, 4], F32)
        ef = sbuf("ef", [128, 4], F32)
        spf = sbuf("spf", [128, 4], F32)
        qT32 = sbuf("qT32", [128, 2, 128], F32)
        kacc = sbuf("kacc", [128, 2], F32)
        kmeanT = sbuf("kmeanT", [128, 2, 16], F32)
        gate = sbuf("gate", [128, 4, 16], F32)
        max8 = sbuf("max8", [128, 4, 8], F32)
        YT = [sbuf(f"YT{i}", [128, NBLK, 128], BF16) for i in range(2)]
        FTS = [sbuf(f"FTS{i}", [128, NBLK, 128], BF16) for i in range(2)]
        VS = [sbuf(f"VS{i}", [128, 16, 65], BF16) for i in range(2)]
        lamt = sbuf("lamt", [128, 2], F32)
        fqd = sbuf("fqd", [128, 4], F32)
        fqh = sbuf("fqh", [128, 4], F32)

        pT = [psum(f"pT{i}", [128, 8, 128], BF16) for i in range(2)]
        pH = [psum(f"pH{i}", [128, 512], F32) for i in range(2)]
        pQA = psum("pQA", [128, 512], F32)
        pKVA = psum("pKVA", [128, 512], F32)
        pM = psum("pM", [128, 512], F32)
        pQ32 = psum("pQ32", [128, 2, 128], F32)

        P.dma('sp', vecs[:], din["vecs"][l].partition_broadcast(128), writes=[vecs])
        P.dma('sp', gat[:], din["g_attn"][l], writes=[gat])
        P.dma('sp', gme[:], din["g_mem"][l], writes=[gme])
        P.dma('sp', gcq[:], din["g_cq"][l], writes=[gcq])
        P.dma('sp', gckv[:], din["g_ckv"][l], writes=[gckv])
        wv = din["w_in"][l].rearrange("(k p) n -> p k n", p=128)
        for k in range(8):
            P.dma('pool', win[:, k, :], wv[:, k, :], writes=[(win, k)])
        P.dma('pool', wmk[:], din["w_memkv"][l].rearrange("(k p) n -> p k n", p=128), writes=[wmk])
        P.dma('pool', wuq[:, 0, :], din["w_uq"][l][0:128, :], writes=[wuq])
        P.dma('pool', wuq[0:64, 1, :], din["w_uq"][l][128:192, :], writes=[wuq])
        P.dma('pool', wukv[:], din["w_ukv"][l], writes=[wukv])
        P.op('dve', lambda e: e.tensor_scalar(wuq[:, 0, :], wuq[:, 0, :], gcq[:, 0:1], None, ALU.mult), reads=[wuq, gcq], writes=[wuq])
        P.op('dve', lambda e: e.tensor_scalar(wuq[0:64, 1, :], wuq[0:64, 1, :], gcq[0:64, 1:2], None, ALU.mult), reads=[wuq, gcq], writes=[wuq])
        P.op('dve', lambda e: e.tensor_scalar(wukv[:], wukv[:], gckv[:, 0:1], None, ALU.mult), reads=[wukv, gckv], writes=[wukv])

        for (a, b, G) in ((0, 1, 192), (1, 2, 128), (2, 3, 32), (3, 19, 64), (19, 35, 32), (35, 39, 64)):
            P.op('pool', lambda e: e.memset(invGA[:, a:b], 1.0 / G), writes=[invGA])
        for (a, b, G) in ((0, 4, 32), (4, 12, 64)):
            P.op('pool', lambda e: e.memset(invGB[:, a:b], 1.0 / G), writes=[invGB])
        for i in range(2):
            P.op('pool', lambda e: e.memset(YT[i][:], 0.0), writes=[YT[i]])
            P.op('pool', lambda e: e.memset(VS[i][:], 1.0), writes=[VS[i]])
            P.op('pool', lambda e: e.memset(YT[i][:, 26:30, 64:66], 1.0), writes=[YT[i]])
        P.op('pool', lambda e: e.memset(gate[:], -1e30), writes=[gate])
        P.op('pool', lambda e: e.memset(kmeanT[:], 0.0), writes=[kmeanT])
        P.op('pool', lambda e: e.memset(cqn[:], 0.0), writes=[cqn])
        P.op('pool', lambda e: e.memset(vmem[:], 1.0), writes=[vmem])

        lam_init = 0.8 - 0.6 * math.exp(-0.3 * l)
        lv = _v(vecs, "lam")
        P.op('dve', lambda e: e.tensor_tensor(r1[:, 0:32], lv[:, 0:32], lv[:, 32:64], ALU.mult), reads=[vecs], writes=[r1])
        P.op('dve', lambda e: e.tensor_tensor(r1[:, 32:64], lv[:, 64:96], lv[:, 96:128], ALU.mult), reads=[vecs, r1], writes=[r1])
        P.op('dve', lambda e: e.tensor_reduce(lamt[:], r1[:, 0:64].rearrange("p (a b) -> p a b", b=32), AX.X, ALU.add), reads=[r1], writes=[lamt])
        P.op('act', lambda e: e.activation(lamt[:], lamt[:], AF.Exp), reads=[lamt], writes=[lamt])
        P.op('dve', lambda e: e.tensor_tensor(lamn[:], lamt[:, 1:2], lamt[:, 0:1], ALU.subtract), reads=[lamt], writes=[lamn])
        P.op('dve', lambda e: e.tensor_scalar(lamn[:], lamn[:], -lam_init, None, ALU.add), reads=[lamn], writes=[lamn])
        P.op('dve', lambda e: e.tensor_scalar(gsub[:], _v(vecs, "diff_sub"), 1.0 - lam_init, None, ALU.mult), reads=[vecs], writes=[gsub])

        def rstd_from(ss, invG, rs, n):
            P.op('dve', lambda e: e.tensor_tensor(rs[:, 0:n], ss[:, 0:n], invG[:, 0:n], ALU.mult), reads=[ss, invG], writes=[rs])
            P.op('act', lambda e: e.activation(rs[:, 0:n], rs[:, 0:n], AF.Ln, bias=epsb[:, 0:1]), reads=[rs, epsb], writes=[rs])
            P.op('act', lambda e: e.activation(rs[:, 0:n], rs[:, 0:n], AF.Exp, scale=-0.5), reads=[rs], writes=[rs])

        def rope(buf, key, nvec, G, half, cs_lo, t, tmp_keys):
            v = buf.rearrange("p (n g) -> p n g", g=G)
            y1 = v[:, :, 0:half]
            y2 = v[:, :, half:2 * half]
            cb = cosT[:, t, cs_lo:cs_lo + half].unsqueeze(1).to_broadcast([128, nvec, half])
            sb_ = sinT[:, t, cs_lo:cs_lo + half].unsqueeze(1).to_broadcast([128, nvec, half])
            tv = [x[:, 0:nvec * half].rearrange("p (n g) -> p n g", g=half) for x in (r1, r2, r3, r4)]
            P.op('dve', lambda e: e.tensor_tensor(tv[0], y1, cb, ALU.mult), reads=[key, cosT], writes=[r1])
            P.op('dve', lambda e: e.tensor_tensor(tv[1], y2, sb_, ALU.mult), reads=[key, sinT], writes=[r2])
            P.op('dve', lambda e: e.tensor_tensor(tv[2], y1, sb_, ALU.mult), reads=[key, sinT], writes=[r3])
            P.op('dve', lambda e: e.tensor_tensor(tv[3], y2, cb, ALU.mult), reads=[key, cosT], writes=[r4])
            P.op('dve', lambda e: e.tensor_tensor(y1, tv[0], tv[1], ALU.subtract), reads=[r1, r2], writes=[key])
            P.op('dve', lambda e: e.tensor_tensor(y2, tv[2], tv[3], ALU.add), reads=[r3, r4], writes=[key])

        def front_chain(src_ap, i, wt, ncols, mem_mode=False, with_stats=False):
            ch = []
            ch.append(lambda: P.op('act', lambda e: e.activation(junk[:, 0:D], xt[i][:], AF.Square, accum_out=ssx[:]), reads=[xt[i]], writes=[junk, ssx], embed=False))
            ch.append(lambda: P.op('act', lambda e: e.activation(rsx[:], ssx[:], AF.Ln, bias=epsb[:, 0:1], scale=1.0 / D), reads=[ssx, epsb], writes=[rsx]))
            ch.append(lambda: P.op('act', lambda e: e.activation(rsx[:], rsx[:], AF.Exp, scale=-0.5), reads=[rsx], writes=[rsx]))
            ch.append(lambda: P.op('act', lambda e: e.activation(xs[i][:], xt[i][:], AF.Copy, scale=rsx[:, 0:1]), reads=[xt[i], rsx], writes=[xs[i]]))

            def tr():
                for k in range(8):
                    P.op('pe', lambda e: e.transpose(pT[0][:, k, :], xs[i][:, k * 128:(k + 1) * 128], idb[:]), reads=[xs[i], idb], writes=[pT[0]])
            ch.append(tr)
            gn = gme if mem_mode else gat
            for k_ in range(8):
                def ev(k=k_):
                    P.op('act', lambda e: e.activation(xnT[i][:, k, :], pT[0][:, k, :], AF.Copy, scale=gn[:, k:k + 1]), reads=[pT[0], gn], writes=[xnT[i]])
                ch.append(ev)
            nchunk = (ncols + 511) // 512
            for c_ in range(nchunk):
                def mm(c=c_):
                    c0, c1 = c * 512, min(ncols, (c + 1) * 512)
                    ph = pH[c % 2]
                    for k in range(8):
                        P.op('pe', lambda e: e.matmul(ph[:, 0:c1 - c0], xnT[i][:, k, :], wt[:, k, c0:c1], start=(k == 0), stop=(k == 7)),
                             reads=[xnT[i], (wt, k)] if not mem_mode else [xnT[i], wt], writes=[ph])

                def evh(c=c_):
                    c0, c1 = c * 512, min(ncols, (c + 1) * 512)
                    ph = pH[c % 2]
                    P.op('act', lambda e: e.copy(hsb[i][:, c0:c1], ph[:, 0:c1 - c0]), reads=[ph], writes=[hsb[i]])
                ch.append(mm)
                ch.append(evh)
            if with_stats:
                h = hsb[i]
                ch.append(lambda: P.op('act', lambda e: e.activation(sqh[:], h[:], AF.Square), reads=[h], writes=[sqh]))
                for (c0_, c1_, G_, a__) in ((C_CQ, C_CKV, 192, 0), (C_CKV, C_KR, 128, 1), (C_KR, C_FQ, 32, 2), (C_FQ, C_FV, 64, 3),
                                            (C_MQ, C_MV, 64, 11), (C_DQ, C_DV, 32, 19), (C_EQ, DIN, 64, 35)):
                    def red(c0=c0_, c1=c1_, G=G_, a_=a__):
                        n = (c1 - c0) // G
                        P.op('dve', lambda e: e.tensor_reduce(ssA[:, a_:a_ + n], sqh[:, c0:c1].rearrange("p (n g) -> p n g", g=G), AX.X, ALU.add),
                             reads=[sqh], writes=[ssA])
                    ch.append(red)
                rs_i = rsAs[i]
                ch.append(lambda: P.op('dve', lambda e: e.tensor_tensor(rs_i[:, 0:39], ssA[:, 0:39], invGA[:, 0:39], ALU.mult), reads=[ssA, invGA], writes=[rs_i]))
                ch.append(lambda: P.op('act', lambda e: e.activation(rs_i[:, 0:39], rs_i[:, 0:39], AF.Ln, bias=epsb[:, 0:1]), reads=[rs_i, epsb], writes=[rs_i]))
                ch.append(lambda: P.op('act', lambda e: e.activation(rs_i[:, 0:39], rs_i[:, 0:39], AF.Exp, scale=-0.5), reads=[rs_i], writes=[rs_i]))
            return ch

        def tile_front(src_ap, i, wt, ncols, mem_mode=False):
            P.dma('act', xt[i][:], src_ap, writes=[xt[i]])
            for f_ in front_chain(src_ap, i, wt, ncols, mem_mode):
                f_()

        for mt in range(2):
            i = mt % 2
            tile_front(din["mem"][mt * 128:(mt + 1) * 128, :], i, wmk, 512, mem_mode=True)
            h = hsb[i]
            P.op('act', lambda e: e.activation(sqh[:, 0:256], h[:, 0:256], AF.Square), reads=[h], writes=[sqh])
            P.op('dve', lambda e: e.tensor_reduce(ssA[:, 0:4], sqh[:, 0:256].rearrange("p (n g) -> p n g", g=64), AX.X, ALU.add), reads=[sqh], writes=[ssA])
            P.op('act', lambda e: e.activation(rsA[:, 0:4], ssA[:, 0:4], AF.Ln, bias=epsb[:, 0:1], scale=1.0 / 64), reads=[ssA, epsb], writes=[rsA])
            P.op('act', lambda e: e.activation(rsA[:, 0:4], rsA[:, 0:4], AF.Exp, scale=-0.5), reads=[rsA], writes=[rsA])
            kv3 = h[:, 0:256].rearrange("p (n g) -> p n g", g=64)
            t3 = t512[:, 0:256].rearrange("p (n g) -> p n g", g=64)
            P.op('dve', lambda e: e.tensor_tensor(t3, kv3, rsA[:, 0:4].unsqueeze(2).to_broadcast([128, 4, 64]), ALU.mult), reads=[h, rsA], writes=[t512])
            y3 = YT[0][:, 0:2, :].rearrange("p b (n g) -> p (b n) g", g=64)
            P.op('dve', lambda e: e.tensor_tensor(y3, t3, _v(vecs, "mem_k").unsqueeze(1).to_broadcast([128, 4, 64]), ALU.mult), reads=[t512, vecs], writes=[YT[0]])
            for b in range(2):
                P.op('pe', lambda e: e.transpose(pT[1][:, b, :], YT[0][:, b, :], idb[:]), reads=[YT[0], idb], writes=[pT[1]])
            P.op('act', lambda e: e.copy(ktm[:, :, mt * 128:(mt + 1) * 128], pT[1][:, 0:2, :]), reads=[pT[1]], writes=[ktm])
            P.op('pool', lambda e: e.tensor_copy(vmem[:, mt, :, 0:64], h[:, 256:512].rearrange("p (n g) -> p n g", g=64)), reads=[h], writes=[vmem])

        rt = {nm: [sbuf(f"rt_{nm}{q}", [128, 64], F32) for q in range(4)] for nm in ("kr", "mo", "df", "qa")}
        t512f = sbuf("t512f", [128, 512], F32)
        t512m = sbuf("t512m", [128, 512], F32)
        t512d = sbuf("t512d", [128, 512], F32)
        t512e = sbuf("t512e", [128, 256], F32)

        def rope_ops(E, buf, key, nvec, G, half, cs_lo, t, tmps):
            v = buf.rearrange("p (n g) -> p n g", g=G)
            y1 = v[:, :, 0:half]
            y2 = v[:, :, half:2 * half]
            cb = cosT[:, t, cs_lo:cs_lo + half].unsqueeze(1).to_broadcast([128, nvec, half])
            sb_ = sinT[:, t, cs_lo:cs_lo + half].unsqueeze(1).to_broadcast([128, nvec, half])
            tv = [x[:, 0:nvec * half].rearrange("p (n g) -> p n g", g=half) for x in tmps]
            return [
                lambda: P.op(E, lambda e: e.tensor_tensor(tv[0], y1, cb, ALU.mult), reads=[key, cosT], writes=[tmps[0]]),
                lambda: P.op(E, lambda e: e.tensor_tensor(tv[1], y2, sb_, ALU.mult), reads=[key, sinT], writes=[tmps[1]]),
                lambda: P.op(E, lambda e: e.tensor_tensor(tv[2], y1, sb_, ALU.mult), reads=[key, sinT], writes=[tmps[2]]),
                lambda: P.op(E, lambda e: e.tensor_tensor(tv[3], y2, cb, ALU.mult), reads=[key, cosT], writes=[tmps[3]]),
                lambda: P.op(E, lambda e: e.tensor_tensor(y1, tv[0], tv[1], ALU.subtract), reads=[tmps[0], tmps[1]], writes=[key]),
                lambda: P.op(E, lambda e: e.tensor_tensor(y2, tv[2], tv[3], ALU.add), reads=[tmps[2], tmps[3]], writes=[key]),
            ]

        def post(t, extra_chain):
            i = t % 2
            h = hsb[i]
            yt = YT[i]
            vs = VS[i]
            nb = t // 2
            rsA = rsAs[i]

            qa3 = qa[:].rearrange("p (n g) -> p n g", g=96)
            kv3 = kva[:].rearrange("p (n g) -> p n g", g=128)
            q3 = sqb[:, 0:384].rearrange("p (n g) -> p n g", g=96)
            k3 = sqk[:].rearrange("p (n g) -> p n g", g=128)
            t3q = t512[:, 0:256].rearrange("p (n g) -> p n g", g=64)
            t3k = t512[:, 256:512].rearrange("p (n g) -> p n g", g=64)

            def mla_pe():
                for b_ in range(3):
                    P.op('pe', lambda e: e.transpose(pT[1][:, b_, :], cqn[:, b_, :], idb[:]), reads=[cqn, idb], writes=[pT[1]])
                P.op('act', lambda e: e.copy(cT[:], pT[1][:, 0:3, :]), reads=[pT[1]], writes=[cT])
                P.op('pe', lambda e: e.matmul(pQA[:, 0:384], cT[:, 0, :], wuq[:, 0, :], start=True, stop=False), reads=[cT, wuq], writes=[pQA])
                P.op('pe', lambda e: e.matmul(pQA[:, 0:384], cT[0:64, 1, :], wuq[0:64, 1, :], start=False, stop=True), reads=[cT, wuq], writes=[pQA])
                P.op('pe', lambda e: e.matmul(pKVA[:], cT[:, 2, :], wukv[:], start=True, stop=True), reads=[cT, wukv], writes=[pKVA])
                P.op('act', lambda e: e.copy(qa[:], pQA[:, 0:384]), reads=[pQA], writes=[qa])
                P.op('act', lambda e: e.copy(kva[:], pKVA[:]), reads=[pKVA], writes=[kva])
                P.op('act', lambda e: e.activation(sqb[:, 0:384], qa[:], AF.Square), reads=[qa], writes=[sqb])
                P.op('act', lambda e: e.activation(sqk[:], kva[:], AF.Square), reads=[kva], writes=[sqk])
                P.op('pool', lambda e: e.tensor_copy(vs[:, 0:4, 0:64], kv3[:, :, 64:128]), reads=[kva], writes=[(vs, 0)])
            ch_mla = [
                lambda: P.op('dve', lambda e: e.tensor_scalar(cqn[:, 0, :], h[:, 0:128], rsA[:, 0:1], None, ALU.mult), reads=[h, rsA], writes=[cqn]),
                lambda: P.op('dve', lambda e: e.tensor_scalar(cqn[:, 1, 0:64], h[:, 128:192], rsA[:, 0:1], None, ALU.mult), reads=[h, rsA], writes=[cqn]),
                lambda: P.op('dve', lambda e: e.tensor_scalar(cqn[:, 2, :], h[:, C_CKV:C_KR], rsA[:, 1:2], None, ALU.mult), reads=[h, rsA], writes=[cqn]),
                mla_pe,
            ]
            ch_mla2 = [
                lambda: P.op('dve', lambda e: e.tensor_reduce(ssB[:, 0:4], q3[:, :, 0:32], AX.X, ALU.add), reads=[sqb], writes=[ssB]),
                lambda: P.op('dve', lambda e: e.tensor_reduce(ssB[:, 4:8], q3[:, :, 32:96], AX.X, ALU.add), reads=[sqb], writes=[ssB]),
                lambda: P.op('dve', lambda e: e.tensor_reduce(ssB[:, 8:12], k3[:, :, 0:64], AX.X, ALU.add), reads=[sqk], writes=[ssB]),
                lambda: rstd_from(ssB, invGB, rsB, 12),
                lambda: P.op('dve', lambda e: e.tensor_tensor(qar[:], qa3[:, :, 0:32], rsB[:, 0:4].unsqueeze(2).to_broadcast([128, 4, 32]), ALU.mult), reads=[qa, rsB], writes=[qar]),
                lambda: P.op('dve', lambda e: e.tensor_tensor(t3q, qa3[:, :, 32:96], rsB[:, 4:8].unsqueeze(2).to_broadcast([128, 4, 64]), ALU.mult), reads=[qa, rsB], writes=[(t512, 0)]),
                lambda: P.op('dve', lambda e: e.tensor_tensor(t3k, kv3[:, :, 0:64], rsB[:, 8:12].unsqueeze(2).to_broadcast([128, 4, 64]), ALU.mult), reads=[kva, rsB], writes=[(t512, 1)]),
                lambda: P.op('dve', lambda e: e.tensor_tensor(qar[:], qar[:], _v(vecs, "mla_q", 0, 32).unsqueeze(1).to_broadcast([128, 4, 32]), ALU.mult), reads=[qar, vecs], writes=[qar]),
                lambda: P.op('dve', lambda e: e.tensor_tensor(yt[:, 0:4, 32:96], t3q, _v(vecs, "mla_q", 32, 96).unsqueeze(1).to_broadcast([128, 4, 64]), ALU.mult), reads=[(t512, 0), vecs], writes=[(yt, "mq")]),
                lambda: P.op('dve', lambda e: e.tensor_tensor(yt[:, 4:8, 32:96], t3k, _v(vecs, "mla_k", 32, 96).unsqueeze(1).to_broadcast([128, 4, 64]), ALU.mult), reads=[(t512, 1), vecs], writes=[(yt, "mk")]),
            ] + rope_ops('dve', qar[:].rearrange("p n g -> p (n g)"), qar, 4, 32, 16, 0, t, rt["qa"]) + [
                lambda: P.op('dve', lambda e: e.tensor_copy(yt[:, 0:4, 0:32], qar[:]), reads=[qar], writes=[(yt, "mqr")]),
            ]
            ch_kr = [
                lambda: P.op('dve', lambda e: e.scalar_tensor_tensor(krr[:], h[:, C_KR:C_FQ], rsA[:, 2:3], _v(vecs, "mla_k", 0, 32), ALU.mult, ALU.mult),
                             reads=[h, rsA, vecs], writes=[krr]),
            ] + rope_ops('dve', krr[:], krr, 1, 32, 16, 0, t, rt["kr"]) + [
                lambda: P.op('dve', lambda e: e.tensor_copy(yt[:, 4:8, 0:32], krr[:].unsqueeze(1).to_broadcast([128, 4, 32])), reads=[krr], writes=[(yt, "kr")]),
            ]
            t3f = t512f[:].rearrange("p (n g) -> p n g", g=64)
            q0 = (t // 4) * 4

            def fox_gate_mid():
                P.op('act', lambda e: e.activation(ef[:], zf[:], AF.Exp, scale=-1.0), reads=[zf], writes=[ef])
                P.op('act', lambda e: e.activation(spf[:], ef[:], AF.Ln, bias=1.0), reads=[ef], writes=[spf])
                P.op('pe', lambda e: e.matmul(pM[:, 0:4], trif[:], spf[:], start=True, stop=True), reads=[trif, spf], writes=[pM])
                P.op('pe', lambda e: e.matmul(pM[:, 8:12], ones32[:], spf[:], start=True, stop=True), reads=[ones32, spf], writes=[pM])
            ch_fox = [
                lambda: P.op('dve', lambda e: e.tensor_tensor(zf[:], h[:, C_FF:C_FF + 4], _v(vecs, "fox_b"), ALU.add), reads=[h, vecs], writes=[zf]),
                fox_gate_mid,
                lambda: P.op('dve', lambda e: e.tensor_tensor(t3f, h[:, C_FQ:C_FV].rearrange("p (n g) -> p n g", g=64), rsA[:, 3:11].unsqueeze(2).to_broadcast([128, 8, 64]), ALU.mult), reads=[h, rsA], writes=[t512f]),
                lambda: P.op('dve', lambda e: e.tensor_tensor(yt[:, 8:12, 0:64], t3f[:, 0:4, :], _v(vecs, "fox_q").unsqueeze(1).to_broadcast([128, 4, 64]), ALU.mult), reads=[t512f, vecs], writes=[(yt, "fq")]),
                lambda: P.op('dve', lambda e: e.tensor_tensor(yt[:, 26:30, 0:64], t3f[:, 4:8, :], _v(vecs, "fox_k").unsqueeze(1).to_broadcast([128, 4, 64]), ALU.mult), reads=[t512f, vecs], writes=[(yt, "fk")]),
                lambda: P.op('dve', lambda e: e.tensor_tensor(cpos[:, t, :], pM[:, 0:4], rtot[:, t, :], ALU.add), reads=[pM, rtot], writes=[cpos]),
                lambda: P.op('dve', lambda e: e.tensor_tensor(rtot[:, t + 1, :], pM[:, 8:12], rtot[:, t, :], ALU.add), reads=[pM, rtot], writes=[rtot]),
                lambda: P.op('dve', lambda e: e.tensor_tensor(fqd[:], rtot[:, q0, :], cpos[:, t, :], ALU.subtract), reads=[rtot, cpos], writes=[fqd]),
                lambda: P.op('dve', lambda e: e.tensor_scalar(yt[:, 8:12, 64], fqd[:], 8.0, None, ALU.mult), reads=[fqd], writes=[(yt, "fh")]),
                lambda: P.op('dve', lambda e: e.tensor_copy(fqh[:], yt[:, 8:12, 64]), reads=[(yt, "fh")], writes=[fqh]),
                lambda: P.op('dve', lambda e: e.scalar_tensor_tensor(yt[:, 8:12, 65], fqd[:], 8.0, fqh[:], ALU.mult, ALU.subtract), reads=[fqd, fqh], writes=[(yt, "fl")]),
            ]
            t3m = t512m[:].rearrange("p (n g) -> p n g", g=64)
            mo, _ = VOFF["moba_q"]
            gm2 = vecs[:, mo:mo + 128].rearrange("p (a g) -> p a g", g=64).unsqueeze(2).to_broadcast([128, 2, 4, 64])

            def moba_gate_pe():
                for pr in range(2):
                    P.op('pe', lambda e: e.matmul(pM[:, 16 + pr:17 + pr], mqr[:, 256 + pr * 128:256 + (pr + 1) * 128], c256[:], start=True, stop=True),
                         reads=[mqr, c256], writes=[pM])
                if nb >= 4:
                    for pr in range(2):
                        P.op('pe', lambda e: e.transpose(pQ32[:, pr, :], mqr[:, pr * 128:(pr + 1) * 128], idf[:]), reads=[mqr, idf], writes=[pQ32])
                    P.op('act', lambda e: e.copy(qT32[:], pQ32[:]), reads=[pQ32], writes=[qT32])
                    for hh in range(4):
                        pr, r0 = hh // 2, (hh % 2) * 64
                        P.op('pe', lambda e: e.matmul(pM[:, 32 + hh * 16:32 + hh * 16 + 16], qT32[r0:r0 + 64, pr, :], kmeanT[r0:r0 + 64, pr, :], start=True, stop=True),
                             reads=[qT32, kmeanT], writes=[pM])
            ch_moba = [
                lambda: P.op('dve', lambda e: e.tensor_tensor(t3m, h[:, C_MQ:C_MV].rearrange("p (n g) -> p n g", g=64), rsA[:, 11:19].unsqueeze(2).to_broadcast([128, 8, 64]), ALU.mult), reads=[h, rsA], writes=[t512m]),
                lambda: P.op('dve', lambda e: e.tensor_tensor(mqr[:].rearrange("p (a n g) -> p a n g", a=2, g=64),
                                                              t512m[:].rearrange("p (a n g) -> p a n g", a=2, g=64), gm2, ALU.mult), reads=[t512m, vecs], writes=[mqr]),
            ] + rope_ops('dve', mqr[:], mqr, 8, 64, 8, 16, t, rt["mo"]) + [
                moba_gate_pe,
                lambda: P.op('dve', lambda e: e.tensor_copy(yt[:, 12:20, 0:64], mqr[:].rearrange("p (n g) -> p n g", g=64)), reads=[mqr], writes=[(yt, "mo")]),
            ]
            if t % 2 == 0:
                ch_moba.append(lambda: P.op('dve', lambda e: e.tensor_copy(kacc[:], pM[:, 16:18]), reads=[pM], writes=[kacc]))
            else:
                ch_moba.append(lambda: P.op('dve', lambda e: e.tensor_tensor(kmeanT[:, :, nb], pM[:, 16:18], kacc[:], ALU.add), reads=[pM, kacc], writes=[kmeanT]))
            if nb >= 4:
                ch_moba.append(lambda: P.op('dve', lambda e: e.tensor_copy(gate[:, :, 0:nb], pM[:, 32:96].rearrange("p (n g) -> p n g", g=16)[:, :, 0:nb]), reads=[pM], writes=[gate]))
                for hh_ in range(4):
                    def gsel(hh=hh_):
                        P.op('dve', lambda e: e.max(max8[:, hh, :], gate[:, hh, :]), reads=[gate], writes=[(max8, hh)])
                        P.op('dve', lambda e: e.tensor_scalar(yt[:, 12 + hh, 64:64 + nb], gate[:, hh, 0:nb], max8[:, hh, 2:3], NEG, ALU.is_lt, ALU.mult),
                             reads=[gate, (max8, hh)], writes=[(yt, "mg")])
                    ch_moba.append(gsel)
            t3d = t512d[:].rearrange("p (n g) -> p n g", g=32)
            do, _ = VOFF["diff_q"]
            gd2 = vecs[:, do:do + 64].rearrange("p (a g) -> p a g", g=32).unsqueeze(2).to_broadcast([128, 2, 8, 32])
            t3e = t512e[:].rearrange("p (n g) -> p n g", g=64)
            ch_pool = [
                lambda: P.op('pool', lambda e: e.tensor_tensor(t3d, h[:, C_DQ:C_DV].rearrange("p (n g) -> p n g", g=32), rsA[:, 19:35].unsqueeze(2).to_broadcast([128, 16, 32]), ALU.mult), reads=[h, rsA], writes=[t512d]),
                lambda: P.op('pool', lambda e: e.tensor_tensor(dqr[:].rearrange("p (a n g) -> p a n g", a=2, g=32),
                                                               t512d[:].rearrange("p (a n g) -> p a n g", a=2, g=32), gd2, ALU.mult), reads=[t512d, vecs], writes=[dqr]),
            ] + rope_ops('pool', dqr[:], dqr, 16, 32, 4, 24, t, rt["df"]) + [
                lambda: P.op('pool', lambda e: e.tensor_copy(yt[:, 20:24, :].rearrange("p b c -> p (b c)"), dqr[:]), reads=[dqr], writes=[(yt, "df")]),
                lambda: P.op('pool', lambda e: e.tensor_tensor(t3e, h[:, C_EQ:DIN].rearrange("p (n g) -> p n g", g=64), rsA[:, 35:39].unsqueeze(2).to_broadcast([128, 4, 64]), ALU.mult), reads=[h, rsA], writes=[t512e]),
                lambda: P.op('pool', lambda e: e.tensor_tensor(yt[:, 24:26, :].rearrange("p b (n g) -> p (b n) g", g=64), t3e,
                                                               _v(vecs, "mem_q").unsqueeze(1).to_broadcast([128, 4, 64]), ALU.mult), reads=[t512e, vecs], writes=[(yt, "eq")]),
                lambda: P.op('pool', lambda e: e.memset(yt[:, 16:20, 64:80], 0.0), writes=[(yt, "oh")]),
                lambda: P.op('pool', lambda e: e.memset(yt[:, 16:20, 64 + nb:65 + nb], 1.0), writes=[(yt, "oh")]),
            ]
            ch_v = []
            for (c0_, hb_) in ((C_FV, 4), (C_MV, 8), (C_DV, 12)):
                def vcopy(c0=c0_, hb=hb_):
                    P.op('act', lambda e: e.copy(vs[:, hb:hb + 4, 0:64], h[:, c0:c0 + 256].rearrange("p (n g) -> p n g", g=64)), reads=[h], writes=[(vs, hb)])
                ch_v.append(vcopy)
            ch_mla = ch_mla + [lambda: None] * 3 + ch_mla2
            chains = [extra_chain, ch_mla, ch_fox, ch_moba, ch_kr, ch_pool, ch_v]
            idx = [0] * len(chains)
            live = True
            while live:
                live = False
                for ci, ch in enumerate(chains):
                    if idx[ci] < len(ch):
                        ch[idx[ci]]()
                        idx[ci] += 1
                        live = True
            ykeys = [(yt, k_) for k_ in ("mq", "mk", "mqr", "kr", "fq", "fk", "fh", "fl", "mo", "mg", "df", "eq", "oh")] + [yt]
            fts = FTS[i]
            for grp in range(4):
                b0, b1 = grp * 8, min(NBLK, grp * 8 + 8)
                pt = pT[grp % 2]
                for b_ in range(b0, b1):
                    P.op('pe', lambda e: e.transpose(pt[:, b_ - b0, :], yt[:, b_, :], idb[:]), reads=ykeys + [idb], writes=[pt])
                if grp % 2 == 0:
                    P.op('act', lambda e: e.copy(fts[:, b0:b1, :], pt[:, 0:b1 - b0, :]), reads=[pt], writes=[fts])
                else:
                    P.op('dve', lambda e: e.tensor_copy(fts[:, b0:b1, :], pt[:, 0:b1 - b0, :]), reads=[pt], writes=[fts])
            P.dma('sp', FTD[:, :, t * 128:(t + 1) * 128].rearrange("b p t -> p b t"), fts[:], reads=[fts], writes=["FTD"])
            vkeys = [(vs, 0), (vs, 4), (vs, 8), (vs, 12), vs]
            P.dma('sp', VD[:, :, t, :].rearrange("h p c -> p h c"), vs[:], reads=vkeys, writes=["VD"])
            return ykeys, vkeys

        P.barrier()
        P.dma('act', xt[0][:], xin[0:128, :], writes=[xt[0]])
        P.dma('act', xt[1][:], xin[128:256, :], writes=[xt[1]])
        for f_ in front_chain(None, 0, win, DIN, with_stats=True):
            f_()
        for t in range(NT):
            nxt = front_chain(None, (t + 1) % 2, win, DIN, with_stats=True) if t + 1 < NT else []
            if t + 2 < NT:
                nxt = [lambda t=t: P.dma('act', xt[t % 2][:], xin[(t + 2) * 128:(t + 3) * 128, :], writes=[xt[t % 2]])] + nxt
            post(t, nxt)
        for hh in range(4):
            for Q in range(8):
                n = 4 * Q + 4
                P.op('dve', lambda e: e.tensor_scalar(fbias[:, hh, Q, 0:n], cpos[:, 0:n, hh], rtot[:, 4 * Q, hh:hh + 1], None, ALU.subtract),
                     reads=[cpos, rtot], writes=[fbias])
        P.barrier()


def phase2(nc, P, din, l, FTD, VD, MIXT, C, mid_hook=None):
    idb, trib, mhalf, fbias, vecs = C["idb"], C["trib"], C["mhalf"], C["fbias"], C["vecs"]
    gsub, lamn, ktm, vmem = C["gsub"], C["lamn"], C["ktm"], C["vmem"]
    epsb = C["epsb"]
    with ExitStack() as st:
        def sbuf(name, shape, dt):
            return st.enter_context(nc.sbuf_tensor(f"{name}_a{l}", shape, dt))

        def psum(name, shape, dt):
            return st.enter_context(nc.psum_tensor(f"{name}_a{l}", shape, dt))
        QT = [[sbuf(f"QT{i}{m}", [128, S], BF16) for m in range(2)] for i in range(2)]
        KT = [[sbuf(f"KT{i}{m}", [128, S], BF16) for m in range(2)] for i in range(2)]
        VV = [sbuf(f"VV{i}", [128, NT, 65], BF16) for i in range(2)]
        PT = [sbuf(f"PT{i}", [128, 512], BF16) for i in range(3)]
        rec = sbuf("rec", [128, 8], F32)
        osb = [sbuf(f"osb{i}", [128, 4, 64], BF16) for i in range(2)]
        oT = [sbuf(f"oT{i}", [64, 512], BF16) for i in range(2)]
        da = sbuf("da", [128, 4, 64], F32)
        db = sbuf("db", [128, 4, 64], F32)
        dsq = sbuf("dsq", [128, 4, 64], F32)
        dss = sbuf("dss", [128, 4], F32)
        pS = [psum(f"pS{i}", [128, 512], F32) for i in range(3)]
        pO = [[psum(f"pO{i}{m}", [128, 4, 128], F32) for m in range(2)] for i in range(2)]
        pTO = psum("pTO", [128, 4, 128], BF16)

        units = []
        for h in range(4):
            units.append(dict(kind="mla", rows=96, q=[(h, 0)], k=[(4 + h, 0)], v=h, scale=96 ** -0.5, out=0 * 256 + h * 64))
        for h in range(4):
            units.append(dict(kind="fox", rows=66, q=[(8 + h, 0)], k=[(26 + h, 0)], v=4 + h, scale=0.125, out=256 + h * 64, h=h))
        for h in range(4):
            units.append(dict(kind="moba", rows=80, q=[(12 + h, 0)], k=[(16 + h, 0)], v=8 + h, scale=0.125, out=512 + h * 64))
        for h in range(4):
            r0 = (h % 2) * 64
            units.append(dict(kind="diff", rows=32, q=[(20 + h // 2, r0), (20 + h // 2, r0 + 32)], k=[(22 + h // 2, r0), (22 + h // 2, r0 + 32)],
                              v=12 + h, scale=32 ** -0.5, out=768 + h * 64))
        for h in range(4):
            units.append(dict(kind="mem", rows=64, q=[(24 + h // 2, (h % 2) * 64)], k=None, v=None, scale=0.125, out=1024 + h * 64, h=h))

        def load_unit(u, i):
            rows = u["rows"]
            base = (u["h"] % 2) * 64 if u["kind"] == "mem" else 0
            for m, (blk, r0) in enumerate(u["q"]):
                P.dma('sp', QT[i][m][base:base + rows, :], FTD[blk, r0:r0 + rows, :], reads=["FTD"], writes=[QT[i][m]])
            if u["k"] is not None:
                for m, (blk, r0) in enumerate(u["k"]):
                    P.dma('sp', KT[i][m][0:rows, :], FTD[blk, r0:r0 + rows, :], reads=["FTD"], writes=[KT[i][m]])
                P.dma('sp', VV[i][:], VD[u["v"]], reads=["VD"], writes=[VV[i]])

        if DBG_UNITS is not None:
            units = [units[k] for k in DBG_UNITS]
        PT4 = PT + [sbuf("PT3", [128, 512], BF16)]
        gstep = [0]
        qglob = [0]
        load_unit(units[0], 0)
        for ui, u in enumerate(units):
            i = ui % 2
            if ui + 1 < len(units):
                load_unit(units[ui + 1], (ui + 1) % 2)
            if mid_hook is not None and ui == min(6, len(units) - 1):
                mid_hook()
            rows, kind, scale = u["rows"], u["kind"], u["scale"]
            nmap = len(u["q"])
            base = (u["h"] % 2) * 64 if kind == "mem" else 0
            steps = []
            for Q in range(8):
                nkt = 2 if kind == "mem" else 4 * Q + 4
                for j in range(nkt):
                    for m in range(nmap):
                        steps.append((Q, j, m, nkt))
            pobuf = {}
            for Q in range(8):
                pobuf[Q] = pO[qglob[0] % 2]
                qglob[0] += 1
            first = {}
            bufs = {}
            deferred = []

            def front(s):
                Q, j, m, nkt = steps[s]
                g = j - 4 * Q if (kind != "mem" and j >= 4 * Q) else None
                c0 = g * 128 if g is not None else 0
                ps = pS[gstep[0] % 3]
                pt = PT4[gstep[0] % 4]
                gstep[0] += 1
                bufs[s] = pt
                if kind == "mem":
                    lhsT = ktm[base:base + 64, u["h"] // 2, j * 128:(j + 1) * 128]
                    kkey = ktm
                else:
                    lhsT = KT[i][m][0:rows, j * 128:(j + 1) * 128]
                    kkey = KT[i][m]
                rhs = QT[i][m][base:base + rows, Q * 512 + c0:(Q + 1) * 512]
                P.op('pe', lambda e: e.matmul(ps[:, c0:512], lhsT, rhs, start=True, stop=True), reads=[kkey, QT[i][m]], writes=[ps])
                bias = fbias[:, u["h"], Q, j:j + 1] if kind == "fox" else 0.0
                P.op('act', lambda e: e.activation(pt[:, c0:512], ps[:, c0:512], AF.Exp, bias=bias, scale=scale),
                     reads=[ps, fbias] if kind == "fox" else [ps], writes=[pt])
                if g is not None:
                    P.op('pool', lambda e: e.tensor_tensor(pt[:, c0:c0 + 128], pt[:, c0:c0 + 128], trib[:], ALU.mult), reads=[pt, trib], writes=[pt])

            def back(s):
                Q, j, m, nkt = steps[s]
                g = j - 4 * Q if (kind != "mem" and j >= 4 * Q) else None
                pt = bufs.pop(s)
                po = pobuf[Q]
                if kind == "mem":
                    vap = vmem[:, j, u["h"], :]
                    vkey = vmem
                else:
                    vap = VV[i][:, j, :]
                    vkey = VV[i]
                for gp in range(g if g is not None else 0, 4):
                    st_ = first.get((Q, m), True)
                    P.op('pe', lambda e: e.matmul(po[m][:, gp, 0:65], pt[:, gp * 128:(gp + 1) * 128], vap, start=st_, stop=(j == nkt - 1 and gp == 3),
                                                  skip_group_check=True),
                         reads=[pt, vkey], writes=[po[m]])
                    first[(Q, m)] = False
                if j == nkt - 1 and m == nmap - 1:
                    evac_a(Q, s)

            def evac_a(Q, s):
                po = pobuf[Q]
                ob = osb[Q % 2]
                if kind != "diff":
                    P.op('dve', lambda e: e.reciprocal(rec[:, 0:4], po[0][:, :, 64]), reads=[po[0]], writes=[rec])
                    P.op('dve', lambda e: e.tensor_tensor(ob[:], po[0][:, :, 0:64], rec[:, 0:4].unsqueeze(2).to_broadcast([128, 4, 64]), ALU.mult),
                         reads=[po[0], rec], writes=[ob])
                    delay = 3
                else:
                    P.op('dve', lambda e: e.reciprocal(rec[:, 0:4], po[0][:, :, 64]), reads=[po[0]], writes=[rec])
                    P.op('dve', lambda e: e.reciprocal(rec[:, 4:8], po[1][:, :, 64]), reads=[po[1]], writes=[rec])
                    P.op('dve', lambda e: e.tensor_scalar(rec[:, 4:8], rec[:, 4:8], lamn[:, 0:1], None, ALU.mult), reads=[rec, lamn], writes=[rec])
                    P.op('dve', lambda e: e.tensor_tensor(da[:], po[0][:, :, 0:64], rec[:, 0:4].unsqueeze(2).to_broadcast([128, 4, 64]), ALU.mult), reads=[po[0], rec], writes=[da])
                    P.op('dve', lambda e: e.tensor_tensor(db[:], po[1][:, :, 0:64], rec[:, 4:8].unsqueeze(2).to_broadcast([128, 4, 64]), ALU.mult), reads=[po[1], rec], writes=[db])
                    P.op('dve', lambda e: e.tensor_tensor(da[:], da[:], db[:], ALU.add), reads=[da, db], writes=[da])
                    P.op('pool', lambda e: e.tensor_tensor(dsq[:], da[:], da[:], ALU.mult), reads=[da], writes=[dsq])
                    P.op('dve', lambda e: e.tensor_reduce(dss[:], dsq[:], AX.X, ALU.add), reads=[dsq], writes=[dss])
                    P.op('act', lambda e: e.activation(dss[:], dss[:], AF.Ln, bias=epsb[:, 0:1], scale=1.0 / 64), reads=[dss, epsb], writes=[dss])
                    P.op('act', lambda e: e.activation(dss[:], dss[:], AF.Exp, scale=-0.5), reads=[dss], writes=[dss])
                    P.op('dve', lambda e: e.tensor_tensor(da[:], da[:], dss[:].unsqueeze(2).to_broadcast([128, 4, 64]), ALU.mult), reads=[da, dss], writes=[da])
                    P.op('dve', lambda e: e.tensor_tensor(ob[:], da[:], gsub[:].unsqueeze(1).to_broadcast([128, 4, 64]), ALU.mult), reads=[da, gsub], writes=[ob])
                    delay = 8
                deferred.append((s + delay, Q))

            def evac_b(Q):
                ob = osb[Q % 2]
                for gp in range(4):
                    P.op('pe', lambda e: e.transpose(pTO[0:64, gp, :], ob[:, gp, :], idb[:]), reads=[ob, idb], writes=[pTO])
                ot = oT[Q % 2]
                P.op('dve', lambda e: e.tensor_copy(ot[:], pTO[0:64, :, :].rearrange("p a b -> p (a b)")), reads=[pTO], writes=[ot])
                P.dma('sp', MIXT[u["out"]:u["out"] + 64, Q * 512:(Q + 1) * 512], ot[:], reads=[ot], writes=["MIXT"])

            LAG = 2
            ns = len(steps)
            for s in range(ns + LAG):
                if s < ns:
                    front(s)
                if s - LAG >= 0:
                    back(s - LAG)
                while deferred and deferred[0][0] <= s - LAG:
                    evac_b(deferred.pop(0)[1])
            while deferred:
                evac_b(deferred.pop(0)[1])
        P.barrier()


def phase3(nc, P, din, l, xin, MIXT, X1, xout, C, wg, wu, wd, gff):
    idb, mhalf, epsb = C["idb"], C["mhalf"], C["epsb"]
    TT = 256
    ntt = S // TT
    with ExitStack() as st:
        def sbuf(name, shape, dt):
            return st.enter_context(nc.sbuf_tensor(f"{name}_o{l}", shape, dt))

        def psum(name, shape, dt):
            return st.enter_context(nc.psum_tensor(f"{name}_o{l}", shape, dt))
        wo = sbuf("wo", [128, 10, D], BF16)
        mixT = [sbuf(f"mixT{i}", [128, 10, TT], BF16) for i in range(2)]
        xr = [sbuf(f"xr{i}", [128, 2, D], F32) for i in range(2)]
        x1 = xr
        pY = [psum(f"pY{i}", [128, 512], F32) for i in range(8)]
        P.dma('pool', wo[:], din["w_o"][l].rearrange("(k p) n -> p k n", p=128), writes=[wo])

        def loads(tt, i):
            P.dma('sp', mixT[i][:], MIXT[:, tt * TT:(tt + 1) * TT].rearrange("(k p) t -> p k t", p=128), reads=["MIXT"], writes=[mixT[i]])
            P.dma('sp', xr[i][:], xin[tt * TT:(tt + 1) * TT, :].rearrange("(g p) d -> p g d", p=128), reads=["xin"], writes=[xr[i]])
        loads(0, 0)
        for tt in range(ntt):
            i = tt % 2
            if tt + 1 < ntt:
                loads(tt + 1, (tt + 1) % 2)
            for g in range(2):
                for hf in range(2):
                    py = pY[i * 4 + g * 2 + hf]
                    for k in range(10):
                        P.op('pe', lambda e: e.matmul(py[:], mixT[i][:, k, g * 128:(g + 1) * 128], wo[:, k, hf * 512:(hf + 1) * 512], start=(k == 0), stop=(k == 9)),
                             reads=[mixT[i], wo], writes=[py])
                    P.op('dve', lambda e: e.tensor_tensor(x1[i][:, g, hf * 512:(hf + 1) * 512], py[:], xr[i][:, g, hf * 512:(hf + 1) * 512], ALU.add),
                         reads=[py, xr[i]], writes=[x1[i]])
            P.dma('sp', X1[tt * TT:(tt + 1) * TT, :].rearrange("(g p) d -> p g d", p=128), x1[i][:], reads=[x1[i]], writes=["X1"])
        P.barrier()

    with ExitStack() as st:
        def sbuf(name, shape, dt):
            return st.enter_context(nc.sbuf_tensor(f"{name}_f{l}", shape, dt))

        def psum(name, shape, dt):
            return st.enter_context(nc.psum_tensor(f"{name}_f{l}", shape, dt))
        cw = sbuf("cw", [128, NCH, 3], F32)
        cb = sbuf("cb", [128, NCH], F32)
        halo = sbuf("halo", [128, NCH, 2], F32)
        x1 = [sbuf(f"x1{i}", [128, 2, D], F32) for i in range(2)]
        junk = sbuf("junk", [128, D], BF16)
        xs2 = [sbuf(f"xs{i}", [128, 2, D], BF16) for i in range(2)]
        xn2T2 = [sbuf(f"xn2T{i}", [128, 8, TT], BF16) for i in range(2)]
        ss2 = [sbuf(f"ss{i}", [128, 2], F32) for i in range(2)]
        rs2 = [sbuf(f"rs{i}", [128, 2], F32) for i in range(2)]
        gsb = [sbuf(f"gsb{i}", [128, TT + 2], F32) for i in range(2)]
        acc = [sbuf(f"acc{i}", [128, TT], F32) for i in range(2)]
        sg = [sbuf(f"sg{i}", [128, TT], F32) for i in range(2)]
        hT = sbuf("hT", [128, NCH, TT], BF16)
        pY = [psum(f"pY{i}", [128, 512], F32) for i in range(2)]
        pT = [psum(f"pT{i}", [128, 8, 128], BF16) for i in range(2)]
        pG = [psum(f"pG{i}", [128, 512], F32) for i in range(2)]
        pU = [psum(f"pU{i}", [128, 512], F32) for i in range(2)]

        P.dma('sp', cw[:], din["convw"][l], writes=[cw])
        P.dma('sp', cb[:], din["convb"][l], writes=[cb])
        P.op('pool', lambda e: e.memset(halo[:], 0.0), writes=[halo])
        wdk = [(wd, 0), (wd, 11)]

        def loads2(tt, i):
            P.dma('sp', x1[i][:], X1[tt * TT:(tt + 1) * TT, :].rearrange("(g p) d -> p g d", p=128), reads=["X1"], writes=[x1[i]])
        def norm_part(i):
            xs, ss, rs = xs2[i], ss2[i], rs2[i]
            for g in range(2):
                P.op('act', lambda e: e.activation(junk[:], x1[i][:, g, :], AF.Square, accum_out=ss[:, g:g + 1]), reads=[x1[i]], writes=[junk, ss], embed=False)
                P.op('act', lambda e: e.activation(rs[:, g:g + 1], ss[:, g:g + 1], AF.Ln, bias=epsb[:, 0:1], scale=1.0 / D), reads=[ss, epsb], writes=[rs])
                P.op('act', lambda e: e.activation(rs[:, g:g + 1], rs[:, g:g + 1], AF.Exp, scale=-0.5), reads=[rs], writes=[rs])
                P.op('dve', lambda e: e.tensor_scalar(xs[:, g, :], x1[i][:, g, :], rs[:, g:g + 1], None, ALU.mult), reads=[x1[i], rs], writes=[(xs, g)])

        def transpose_part(i):
            xs, xn2T = xs2[i], xn2T2[i]
            for g in range(2):
                for k in range(8):
                    P.op('pe', lambda e: e.transpose(pT[g][:, k, :], xs[:, g, k * 128:(k + 1) * 128], idb[:]), reads=[(xs, g), idb], writes=[pT[g]])
                for k in range(8):
                    P.op('act', lambda e: e.activation(xn2T[:, k, g * 128:(g + 1) * 128], pT[g][:, k, :], AF.Copy, scale=gff[:, k:k + 1]), reads=[pT[g], gff], writes=[xn2T])

        loads2(0, 0)
        norm_part(0)
        transpose_part(0)
        for tt in range(ntt):
            i = tt % 2
            xn2T = xn2T2[i]
            if tt + 1 < ntt:
                loads2(tt + 1, (tt + 1) % 2)
            for c in range(NCH):
                j = c % 2
                for k in range(8):
                    P.op('pe', lambda e: e.matmul(pG[j][:, 0:TT], wg[:, k, c * 128:(c + 1) * 128], xn2T[:, k, :], start=(k == 0), stop=(k == 7)),
                         reads=[(wg, k), xn2T], writes=[pG[j]])
                for k in range(8):
                    P.op('pe', lambda e: e.matmul(pU[j][:, 0:TT], wu[:, k, c * 128:(c + 1) * 128], xn2T[:, k, :], start=(k == 0), stop=(k == 7)),
                         reads=[(wu, k), xn2T], writes=[pU[j]])
                P.op('pool', lambda e: e.tensor_copy(gsb[j][:, 0:2], halo[:, c, :]), reads=[halo], writes=[gsb[j]])
                P.op('act', lambda e: e.copy(gsb[j][:, 2:TT + 2], pG[j][:, 0:TT]), reads=[pG[j]], writes=[gsb[j]])
                P.op('pool', lambda e: e.tensor_copy(halo[:, c, :], gsb[j][:, TT:TT + 2]), reads=[gsb[j]], writes=[halo])
                P.op('dve', lambda e: e.tensor_scalar(acc[j][:], gsb[j][:, 2:TT + 2], cw[:, c, 2:3], cb[:, c:c + 1], ALU.mult, ALU.add), reads=[gsb[j], cw, cb], writes=[acc[j]])
                P.op('dve', lambda e: e.scalar_tensor_tensor(acc[j][:], gsb[j][:, 1:TT + 1], cw[:, c, 1:2], acc[j][:], ALU.mult, ALU.add), reads=[gsb[j], cw, acc[j]], writes=[acc[j]])
                P.op('dve', lambda e: e.scalar_tensor_tensor(acc[j][:], gsb[j][:, 0:TT], cw[:, c, 0:1], acc[j][:], ALU.mult, ALU.add), reads=[gsb[j], cw, acc[j]], writes=[acc[j]])
                P.op('act', lambda e: e.activation(sg[j][:], acc[j][:], AF.Silu), reads=[acc[j]], writes=[sg[j]])
                P.op('dve', lambda e: e.tensor_tensor(hT[:, c, :], pU[j][:, 0:TT], sg[j][:], ALU.mult), reads=[pU[j], sg[j]], writes=[(hT, c)])
                if c == 10 and tt + 1 < ntt:
                    norm_part((tt + 1) % 2)
            if tt + 1 < ntt:
                transpose_part((tt + 1) % 2)
            for g in range(2):
                for hf in range(2):
                    py = pY[hf]
                    for c in range(NCH):
                        P.op('pe', lambda e: e.matmul(py[:], hT[:, c, g * 128:(g + 1) * 128], wd[:, c, hf * 512:(hf + 1) * 512], start=(c == 0), stop=(c == NCH - 1)),
                             reads=[(hT, c), wdk[0 if c < 11 else 1]], writes=[py])
                    P.op('dve', lambda e: e.tensor_tensor(x1[i][:, g, hf * 512:(hf + 1) * 512], py[:], x1[i][:, g, hf * 512:(hf + 1) * 512], ALU.add),
                         reads=[py, x1[i]], writes=[x1[i]])
            P.dma('sp', xout[tt * TT:(tt + 1) * TT, :].rearrange("(g p) d -> p g d", p=128), x1[i][:], reads=[x1[i]], writes=["xout"])
        P.barrier()


def _host_inputs(inputs):
    f = lambda a: np.ascontiguousarray(np.asarray(a, dtype=np.float32))
    ident = np.eye(128, dtype=np.float32)
    tri = np.triu(np.ones((128, 128), dtype=np.float32))
    theta = 500000.0
    invs = []
    for rot in (32, 16, 8):
        invs.append(theta ** (-np.arange(0, rot, 2, dtype=np.float32) / rot))
    invf = np.concatenate(invs).astype(np.float32).reshape(1, NFREQ)

    def pk(v, nk):
        v = f(v)
        return np.ascontiguousarray(v.reshape(L, nk, 128).transpose(0, 2, 1))
    g_cq = np.zeros((L, 256), np.float32)
    g_cq[:, :192] = f(inputs["mla_cq_norm"])
    vecs = np.zeros((L, 1, 1024), np.float32)

    def put(name, arr):
        o, w = VOFF[name]
        vecs[:, 0, o:o + w] = f(arr).reshape(L, w)
    put("mla_q", inputs["mla_q_norm"]); put("mla_k", inputs["mla_k_norm"]); put("fox_b", inputs["fox_b_f"])
    put("fox_q", inputs["fox_q_norm"]); put("fox_k", inputs["fox_k_norm"]); put("moba_q", inputs["moba_q_norm"])
    put("moba_k", inputs["moba_k_norm"]); put("lam", inputs["diff_lambda"]); put("diff_q", inputs["diff_q_norm"])
    put("diff_k", inputs["diff_k_norm"]); put("diff_sub", inputs["diff_sub_norm"]); put("mem_q", inputs["mem_q_norm"])
    put("mem_k", inputs["mem_k_norm"])
    convw = np.ascontiguousarray(f(inputs["ffn_conv_w"]).reshape(L, 3, NCH, 128).transpose(0, 3, 2, 1))
    convb = np.ascontiguousarray(f(inputs["ffn_conv_b"]).reshape(L, NCH, 128).transpose(0, 2, 1))
    shared = {
        "ident": ident, "tri": tri, "invf": invf,
        "w_in": f(inputs["w_in"]), "w_uq": f(inputs["mla_w_uq"]), "w_ukv": f(inputs["mla_w_ukv"]),
        "w_memkv": f(inputs["mem_w_kv"]), "w_o": f(inputs["w_o"]), "w_gate": f(inputs["ffn_w_gate"]),
        "w_up": f(inputs["ffn_w_up"]), "w_down": f(inputs["ffn_w_down"]),
        "g_attn": pk(inputs["attn_norm"], 8), "g_ffn": pk(inputs["ffn_norm"], 8), "g_mem": pk(inputs["mem_norm"], 8),
        "g_cq": pk(g_cq, 2), "g_ckv": pk(inputs["mla_ckv_norm"], 1),
        "vecs": vecs, "convw": convw, "convb": convb,
    }
    x = f(inputs["x"]); mem = f(inputs["mem"]); pos = np.asarray(inputs["positions"]).astype(np.int32)
    maps = []
    for b in range(x.shape[0]):
        m = dict(shared)
        m["x"] = x[b]
        m["mem"] = mem[b]
        m["pos"] = np.ascontiguousarray(pos[b].reshape(NT, 128).T)
        maps.append(m)
    return maps


_NC_CACHE = {}


def kernel(**inputs):
    maps = _host_inputs(inputs)
    if "nc" not in _NC_CACHE:
        _NC_CACHE["nc"] = build()
    nc = _NC_CACHE["nc"]
    res = run_bass_kernel_spmd(nc, maps, core_ids=list(range(8)))
    return np.stack([np.asarray(r["out"], dtype=np.float32) for r in res.results], axis=0)
```

```python
import math
from contextlib import ExitStack
import numpy as np
import concourse.bass as bass
import concourse.mybir as mybir
from concourse.bass_utils import run_bass_kernel_spmd

F32 = mybir.dt.float32
BF16 = mybir.dt.bfloat16
I32 = mybir.dt.int32
AF = mybir.ActivationFunctionType
ALU = mybir.AluOpType
AX = mybir.AxisListType

S = 4096
D = 1024
NT = 32
DIN = 2916
DFF = 2816
NCH = 22
L = 2
EPS = 1e-6
NEG = -30000.0
C_CQ, C_CKV, C_KR, C_FQ, C_FK, C_FV, C_FF = 0, 192, 320, 352, 608, 864, 1120
C_MQ, C_MK, C_MV, C_DQ, C_DK, C_DV, C_EQ = 1124, 1380, 1636, 1892, 2148, 2404, 2660
NBLK = 30
NFREQ = 28
DBG_UNITS = None


class Prog:
    ENGS = ('pe', 'act', 'dve', 'pool', 'sp')

    def __init__(self, nc, es, n_dma=56, epoch=10**9):
        self.nc = nc
        self.es = es
        self.eng = {'pe': nc.tensor, 'act': nc.scalar, 'dve': nc.vector,
                    'pool': nc.gpsimd, 'sp': nc.sync}
        self.EPOCH = epoch
        self.nsem = 0
        self.sem = {e: self._newsem() for e in self.ENGS}
        self.ep = {e: 0 for e in self.ENGS}
        self.cnt = {e: 0 for e in self.ENGS}
        self.known = {e: {} for e in self.ENGS}
        self.known_ep = {e: {} for e in self.ENGS}
        self.semof = {(e, 0): self.sem[e] for e in self.ENGS}
        self.dsem = [self._newsem() for _ in range(n_dma)]
        self.dval = [0] * n_dma
        self.dnext = 0
        self.dnext_sw = 0
        self.NHW = n_dma - 16
        self.known_d = {e: [0] * n_dma for e in self.ENGS}
        self.last_w = {}
        self.readers = {}
        self.nwaits = 0
        self.nops = 0

    def _newsem(self):
        self.nsem += 1
        return self.es.enter_context(self.nc.semaphore(f"s{self.nsem}"))

    def _wait(self, E, tok):
        if tok[0] == 'd':
            _, k, v = tok
            if self.known_d[E][k] >= v:
                return
            self.eng[E].wait_ge(self.dsem[k], v)
            self.known_d[E][k] = v
            self.nwaits += 1
        else:
            _, X, ep, c = tok
            if self.known_ep[E].get(X, -1) > ep:
                return
            if self.known[E].get((X, ep), 0) >= c:
                return
            self.eng[E].wait_ge(self.semof[(X, ep)], c)
            self.known[E][(X, ep)] = c
            if self.known_ep[E].get(X, -1) < ep:
                self.known_ep[E][X] = ep
            self.nwaits += 1

    @staticmethod
    def _k(k):
        if isinstance(k, str):
            return k
        if isinstance(k, tuple):
            return tuple(x if isinstance(x, (int, str)) else x.name for x in k)
        return k.name

    def _deps(self, E, reads, writes, defer_last=False):
        deps = []
        for r in reads:
            w = self.last_w.get(r)
            if w is not None:
                deps.append(w)
        for wk in writes:
            lw = self.last_w.get(wk)
            if lw is not None and not (lw[0] == 'e' and lw[1] == E and E == 'pe'):
                deps.append(lw)
            for rd in self.readers.get(wk, {}).values():
                if not (rd[0] == 'e' and rd[1] == E and E == 'pe'):
                    deps.append(rd)
        if not defer_last:
            for d in deps:
                self._wait(E, d)
            return None
        need = [d for d in deps if self._needed(E, d)]
        for d in need[:-1]:
            self._wait(E, d)
        if need and self._needed(E, need[-1]):
            return need[-1]
        return None

    def _needed(self, E, tok):
        if tok[0] == 'd':
            return self.known_d[E][tok[1]] < tok[2]
        _, X, ep, c = tok
        if self.known_ep[E].get(X, -1) > ep:
            return False
        return self.known[E].get((X, ep), 0) < c

    def _embed(self, E, ins, tok):
        if tok[0] == 'd':
            _, k, v = tok
            ins._wait_ge(self.dsem[k], v)
            self.known_d[E][k] = v
        else:
            _, X, ep, c = tok
            ins._wait_ge(self.semof[(X, ep)], c)
            self.known[E][(X, ep)] = c
            if self.known_ep[E].get(X, -1) < ep:
                self.known_ep[E][X] = ep

    def _record(self, tok, E, reads, writes):
        for r in reads:
            self.readers.setdefault(r, {})[(E, tok[0])] = tok
        for wk in writes:
            self.last_w[wk] = tok
            self.readers[wk] = {}

    def op(self, E, fn, reads=(), writes=(), embed=True):
        reads = [self._k(r) for r in reads]
        writes = [self._k(w) for w in writes]
        last = self._deps(E, reads, writes, defer_last=(embed and E in ('act', 'dve', 'pool')))
        ins = fn(self.eng[E])
        if last is not None:
            self._embed(E, ins, last)
        self.cnt[E] += 1
        ins.then_inc(self.sem[E], 1)
        tok = ('e', E, self.ep[E], self.cnt[E])
        self._record(tok, E, reads, writes)
        self.nops += 1
        if self.cnt[E] >= self.EPOCH:
            self.ep[E] += 1
            self.cnt[E] = 0
            self.sem[E] = self._newsem()
            self.semof[(E, self.ep[E])] = self.sem[E]
        return tok

    def dma(self, E, out, in_, reads=(), writes=(), **kw):
        reads = [self._k(r) for r in reads]
        writes = [self._k(w) for w in writes]
        if E == 'pool':
            k = self.NHW + self.dnext_sw
            self.dnext_sw = (self.dnext_sw + 1) % (len(self.dsem) - self.NHW)
        else:
            k = self.dnext
            self.dnext = (self.dnext + 1) % self.NHW
        if self.dval[k] > 0:
            self._wait(E, ('d', k, self.dval[k]))
        self._deps(E, reads, writes)
        self.eng[E].dma_start(out=out, in_=in_, **kw).then_inc(self.dsem[k], 16)
        self.dval[k] += 16
        tok = ('d', k, self.dval[k])
        self._record(tok, E, reads, writes)
        return tok

    def barrier(self):
        toks = []
        for X in self.ENGS:
            if self.cnt[X] > 0:
                toks.append(('e', X, self.ep[X], self.cnt[X]))
            elif self.ep[X] > 0:
                toks.append(('e', X, self.ep[X] - 1, self.EPOCH))
        for k, v in enumerate(self.dval):
            if v > 0:
                toks.append(('d', k, v))
        for E in self.ENGS:
            for t in toks:
                self._wait(E, t)
        self.last_w = {}
        self.readers = {}


def _param_specs():
    return {
        "x": ([S, D], F32), "mem": ([256, D], F32), "pos": ([128, NT], I32),
        "ident": ([128, 128], F32), "tri": ([128, 128], F32), "invf": ([1, NFREQ], F32),
        "w_in": ([L, D, DIN], F32), "w_uq": ([L, 192, 384], F32), "w_ukv": ([L, 128, 512], F32),
        "w_memkv": ([L, D, 512], F32), "w_o": ([L, 1280, D], F32),
        "w_gate": ([L, D, DFF], F32), "w_up": ([L, D, DFF], F32), "w_down": ([L, DFF, D], F32),
        "g_attn": ([L, 128, 8], F32), "g_ffn": ([L, 128, 8], F32), "g_mem": ([L, 128, 8], F32),
        "g_cq": ([L, 128, 2], F32), "g_ckv": ([L, 128, 1], F32),
        "vecs": ([L, 1, 1024], F32),
        "convw": ([L, 128, NCH, 3], F32), "convb": ([L, 128, NCH], F32),
    }


VOFF = {}
_o = 0
for _n, _w in [("mla_q", 96), ("mla_k", 96), ("fox_b", 4), ("fox_q", 64), ("fox_k", 64), ("moba_q", 64),
               ("moba_k", 64), ("lam", 128), ("diff_q", 32), ("diff_k", 32), ("diff_sub", 64),
               ("mem_q", 64), ("mem_k", 64)]:
    VOFF[_n] = (_o, _w)
    _o += _w
assert _o <= 1024


def build(dbg=None, n_layers=L, phases=(1, 2, 3)):
    nc = bass.Bass("TRN2", target_bir_lowering=False)
    din = {}
    for name, (shape, dt) in _param_specs().items():
        din[name] = nc.dram_tensor(name, shape, dt, kind="ExternalInput").ap()
    out_d = nc.dram_tensor("out", [S, D], F32, kind="ExternalOutput").ap()

    def scratch(name, shape, dt):
        kind = "ExternalOutput" if (dbg and name in dbg) else "Internal"
        return nc.dram_tensor(name, shape, dt, kind=kind).ap()
    FTD = scratch("FTD", [NBLK, 128, S], BF16)
    VD = scratch("VD", [16, 128, NT, 65], BF16)
    MIXT = scratch("MIXT", [1280, S], BF16)
    XS = [scratch("XS0", [S, D], F32), scratch("XS1", [S, D], F32)]
    X1 = scratch("X1", [S, D], F32)

    with ExitStack() as es:
        P = Prog(nc, es)

        def sbuf(st, name, shape, dt):
            return st.enter_context(nc.sbuf_tensor(name, shape, dt))

        def psum(st, name, shape, dt):
            return st.enter_context(nc.psum_tensor(name, shape, dt))

        idf = sbuf(es, "idf", [128, 128], F32)
        idb = sbuf(es, "idb", [128, 128], BF16)
        trif = sbuf(es, "trif", [128, 128], F32)
        trib = sbuf(es, "trib", [128, 128], BF16)
        ones32 = sbuf(es, "ones32", [128, 128], F32)
        c256 = sbuf(es, "c256", [128, 1], F32)
        mhalf = sbuf(es, "mhalf", [128, 64], F32)
        epsb = sbuf(es, "epsb", [128, 1], F32)
        cosT = sbuf(es, "cosT", [128, NT, NFREQ], F32)
        sinT = sbuf(es, "sinT", [128, NT, NFREQ], F32)
        cpos = sbuf(es, "cpos", [128, NT, 4], F32)
        rtot = sbuf(es, "rtot", [128, NT + 1, 4], F32)
        fbias = sbuf(es, "fbias", [128, 4, 8, NT], F32)
        vecs = sbuf(es, "vecs_sb", [128, 1024], F32)
        gsub = sbuf(es, "gsub", [128, 64], F32)
        lamn = sbuf(es, "lamn", [128, 1], F32)
        ktm = sbuf(es, "ktm", [128, 2, 256], BF16)
        vmem = sbuf(es, "vmem", [128, 2, 4, 65], BF16)

        P.dma('sp', idf[:], din["ident"][:, :], writes=[idf])
        P.dma('sp', trif[:], din["tri"][:, :], writes=[trif])
        P.op('dve', lambda e: e.tensor_copy(idb[:], idf[:]), reads=[idf], writes=[idb])
        P.op('dve', lambda e: e.tensor_copy(trib[:], trif[:]), reads=[trif], writes=[trib])
        P.op('pool', lambda e: e.memset(ones32[:], 1.0), writes=[ones32])
        P.op('pool', lambda e: e.memset(c256[:], 1.0 / 256), writes=[c256])
        P.op('pool', lambda e: e.memset(mhalf[:], -0.5), writes=[mhalf])
        P.op('pool', lambda e: e.memset(epsb[:], EPS), writes=[epsb])
        P.op('pool', lambda e: e.memset(rtot[:], 0.0), writes=[rtot])
        P.op('pool', lambda e: e.memset(fbias[:], 0.0), writes=[fbias])

        with ExitStack() as st:
            posi = sbuf(st, "posi", [128, NT], I32)
            posf = sbuf(st, "posf", [128, NT], F32)
            invf = sbuf(st, "invf_sb", [128, NFREQ], F32)
            tt = sbuf(st, "tt", [128, NT, NFREQ], F32)
            ti = sbuf(st, "ti", [128, NT, NFREQ], I32)
            tf = sbuf(st, "tf", [128, NT, NFREQ], F32)
            mk = sbuf(st, "mk", [128, NT, NFREQ], F32)
            P.dma('sp', posi[:], din["pos"][:, :], writes=[posi])
            P.dma('sp', invf[:], din["invf"].partition_broadcast(128), writes=[invf])
            P.op('dve', lambda e: e.tensor_copy(posf[:], posi[:]), reads=[posi], writes=[posf])
            P.op('dve', lambda e: e.tensor_tensor(tt[:], posf[:].unsqueeze(2).to_broadcast([128, NT, NFREQ]),
                                                  invf[:].unsqueeze(1).to_broadcast([128, NT, NFREQ]), ALU.mult),
                 reads=[posf, invf], writes=[tt])
            for which, dst in ((0, sinT), (1, cosT)):
                sh = 0.25 * which
                P.op('dve', lambda e: e.tensor_scalar(tf[:], tt[:], 1.0 / (2 * math.pi), sh, ALU.mult, ALU.add),
                     reads=[tt], writes=[tf])
                P.op('dve', lambda e: e.tensor_copy(ti[:], tf[:]), reads=[tf], writes=[ti])
                P.op('dve', lambda e: e.tensor_copy(mk[:], ti[:]), reads=[ti], writes=[mk])
                P.op('dve', lambda e: e.tensor_tensor(tf[:], tf[:], mk[:], ALU.subtract), reads=[tf, mk], writes=[tf])
                P.op('dve', lambda e: e.tensor_scalar(mk[:], tf[:], 0.5, None, ALU.is_gt), reads=[tf], writes=[mk])
                P.op('dve', lambda e: e.tensor_tensor(tf[:], tf[:], mk[:], ALU.subtract), reads=[tf, mk], writes=[tf])
                P.op('dve', lambda e: e.tensor_scalar(mk[:], tf[:], -0.5, None, ALU.is_lt), reads=[tf], writes=[mk])
                P.op('dve', lambda e: e.tensor_tensor(tf[:], tf[:], mk[:], ALU.add), reads=[tf, mk], writes=[tf])
                P.op('act', lambda e: e.activation(dst[:], tf[:], AF.Sin, scale=2 * math.pi), reads=[tf], writes=[dst])
            P.barrier()

        xin = din["x"]
        for l in range(n_layers):
            xout = out_d if l == n_layers - 1 else XS[l % 2]
            if 1 in phases:
                phase1(nc, P, din, l, xin, FTD, VD, locals())
            with ExitStack() as lst:
                wg = lst.enter_context(nc.sbuf_tensor(f"wg_l{l}", [128, 8, DFF], BF16))
                wu = lst.enter_context(nc.sbuf_tensor(f"wu_l{l}", [128, 8, DFF], BF16))
                gff = lst.enter_context(nc.sbuf_tensor(f"gff_l{l}", [128, 8], F32))
                P.dma('sp', gff[:], din["g_ffn"][l], writes=[gff])
                wgv = din["w_gate"][l].rearrange("(k p) n -> p k n", p=128)
                wuv = din["w_up"][l].rearrange("(k p) n -> p k n", p=128)
                for k in range(8):
                    P.dma('pool', wg[:, k, :], wgv[:, k, :], writes=[(wg, k)])
                    P.dma('pool', wu[:, k, :], wuv[:, k, :], writes=[(wu, k)])

                def fold_gains():
                    for k in range(8):
                        P.op('dve', lambda e: e.tensor_scalar(wg[:, k, :], wg[:, k, :], gff[:, k:k + 1], None, ALU.mult), reads=[(wg, k), gff], writes=[(wg, k)])
                        P.op('pool', lambda e: e.tensor_scalar(wu[:, k, :], wu[:, k, :], gff[:, k:k + 1], None, ALU.mult), reads=[(wu, k), gff], writes=[(wu, k)])
                if 2 in phases:
                    phase2(nc, P, din, l, FTD, VD, MIXT, locals())
                if 3 in phases:
                    wd = lst.enter_context(nc.sbuf_tensor(f"wd_l{l}", [128, NCH, D], BF16))
                    wdv = din["w_down"][l].rearrange("(k p) n -> p k n", p=128)
                    for k0 in range(0, NCH, 11):
                        P.dma('pool', wd[:, k0:k0 + 11, :], wdv[:, k0:k0 + 11, :], writes=[(wd, k0)])
                    phase3(nc, P, din, l, xin, MIXT, X1, xout, locals(), wg, wu, wd, gff)
                    P.barrier()
            xin = xout
        P.barrier()
        print("program: ops", P.nops, "waits", P.nwaits, "sems", P.nsem)
    return nc


def _v(vecs, name, lo=0, hi=None):
    o, w = VOFF[name]
    hi = w if hi is None else hi
    return vecs[:, o + lo:o + hi]


def phase1(nc, P, din, l, xin, FTD, VD, C):
    idb, idf, trif, ones32, c256, mhalf = C["idb"], C["idf"], C["trif"], C["ones32"], C["c256"], C["mhalf"]
    cosT, sinT, cpos, rtot, fbias, vecs = C["cosT"], C["sinT"], C["cpos"], C["rtot"], C["fbias"], C["vecs"]
    gsub, lamn, ktm, vmem = C["gsub"], C["lamn"], C["ktm"], C["vmem"]
    epsb = C["epsb"]
    with ExitStack() as st:
        def sbuf(name, shape, dt):
            return st.enter_context(nc.sbuf_tensor(f"{name}_l{l}", shape, dt))

        def psum(name, shape, dt):
            return st.enter_context(nc.psum_tensor(f"{name}_l{l}", shape, dt))
        win = sbuf("win", [128, 8, DIN], BF16)
        wmk = sbuf("wmk", [128, 8, 512], BF16)
        wuq = sbuf("wuq", [128, 2, 384], BF16)
        wukv = sbuf("wukv", [128, 512], BF16)
        gat = sbuf("gat", [128, 8], F32)
        gme = sbuf("gme", [128, 8], F32)
        gcq = sbuf("gcq", [128, 2], F32)
        gckv = sbuf("gckv", [128, 1], F32)
        xt = [sbuf(f"xt{i}", [128, D], F32) for i in range(2)]
        junk = sbuf("junk", [128, DIN], BF16)
        xs = [sbuf(f"xs{i}", [128, D], BF16) for i in range(2)]
        xnT = [sbuf(f"xnT{i}", [128, 8, 128], BF16) for i in range(2)]
        hsb = [sbuf(f"hsb{i}", [128, DIN], F32) for i in range(2)]
        sqh = sbuf("sqh", [128, DIN], F32)
        ssx = sbuf("ssx", [128, 1], F32)
        rsx = sbuf("rsx", [128, 1], F32)
        ssA = sbuf("ssA", [128, 39], F32)
        rsA = sbuf("rsA", [128, 39], F32)
        rsAs = [sbuf(f"rsAs{i}", [128, 39], F32) for i in range(2)]
        invGA = sbuf("invGA", [128, 39], F32)
        ssB = sbuf("ssB", [128, 12], F32)
        rsB = sbuf("rsB", [128, 12], F32)
        invGB = sbuf("invGB", [128, 12], F32)
        cqn = sbuf("cqn", [128, 3, 128], BF16)
        cT = sbuf("cT", [128, 3, 128], BF16)
        qa = sbuf("qa", [128, 384], F32)
        kva = sbuf("kva", [128, 512], F32)
        sqb = sbuf("sqb", [128, 512], F32)
        sqk = sbuf("sqk", [128, 512], F32)
        t512 = sbuf("t512", [128, 512], F32)
        mqr = sbuf("mqr", [128, 512], F32)
        dqr = sbuf("dqr", [128, 512], F32)
        qar = sbuf("qar", [128, 4, 32], F32)
        krr = sbuf("krr", [128, 32], F32)
        r1 = sbuf("r1", [128, 256], F32)
        r2 = sbuf("r2", [128, 256], F32)
        r3 = sbuf("r3", [128, 256], F32)
        r4 = sbuf("r4", [128, 256], F32)
        zf = sbuf("zf", [128, 4], F32)
        ef = sbuf("ef", [128, 4], F32)
        spf = sbuf("spf", [128, 4], F32)
        qT32 = sbuf("qT32", [128, 2, 128], F32)
        kacc = sbuf("kacc", [128, 2], F32)
        kmeanT = sbuf("kmeanT", [128, 2, 16], F32)
        gate = sbuf("gate", [128, 4, 16], F32)
        max8 = sbuf("max8", [128, 4, 8], F32)
        YT = [sbuf(f"YT{i}", [128, NBLK, 128], BF16) for i in range(2)]
        FTS = [sbuf(f"FTS{i}", [128, NBLK, 128], BF16) for i in range(2)]
        VS = [sbuf(f"VS{i}", [128, 16, 65], BF16) for i in range(2)]
        lamt = sbuf("lamt", [128, 2], F32)
        fqd = sbuf("fqd", [128, 4], F32)
        fqh = sbuf("fqh", [128, 4], F32)

        pT = [psum(f"pT{i}", [128, 8, 128], BF16) for i in range(2)]
        pH = [psum(f"pH{i}", [128, 512], F32) for i in range(2)]
        pQA = psum("pQA", [128, 512], F32)
        pKVA = psum("pKVA", [128, 512], F32)
        pM = psum("pM", [128, 512], F32)
        pQ32 = psum("pQ32", [128, 2, 128], F32)

        P.dma('sp', vecs[:], din["vecs"][l].partition_broadcast(128), writes=[vecs])
        P.dma('sp', gat[:], din["g_attn"][l], writes=[gat])
        P.dma('sp', gme[:], din["g_mem"][l], writes=[gme])
        P.dma('sp', gcq[:], din["g_cq"][l], writes=[gcq])
        P.dma('sp', gckv[:], din["g_ckv"][l], writes=[gckv])
        wv = din["w_in"][l].rearrange("(k p) n -> p k n", p=128)
        for k in range(8):
            P.dma('pool', win[:, k, :], wv[:, k, :], writes=[(win, k)])
        P.dma('pool', wmk[:], din["w_memkv"][l].rearrange("(k p) n -> p k n", p=128), writes=[wmk])
        P.dma('pool', wuq[:, 0, :], din["w_uq"][l][0:128, :], writes=[wuq])
        P.dma('pool', wuq[0:64, 1, :], din["w_uq"][l][128:192, :], writes=[wuq])
        P.dma('pool', wukv[:], din["w_ukv"][l], writes=[wukv])
        P.op('dve', lambda e: e.tensor_scalar(wuq[:, 0, :], wuq[:, 0, :], gcq[:, 0:1], None, ALU.mult), reads=[wuq, gcq], writes=[wuq])
        P.op('dve', lambda e: e.tensor_scalar(wuq[0:64, 1, :], wuq[0:64, 1, :], gcq[0:64, 1:2], None, ALU.mult), reads=[wuq, gcq], writes=[wuq])
        P.op('dve', lambda e: e.tensor_scalar(wukv[:], wukv[:], gckv[:, 0:1], None, ALU.mult), reads=[wukv, gckv], writes=[wukv])

        for (a, b, G) in ((0, 1, 192), (1, 2, 128), (2, 3, 32), (3, 19, 64), (19, 35, 32), (35, 39, 64)):
            P.op('pool', lambda e: e.memset(invGA[:, a:b], 1.0 / G), writes=[invGA])
        for (a, b, G) in ((0, 4, 32), (4, 12, 64)):
            P.op('pool', lambda e: e.memset(invGB[:, a:b], 1.0 / G), writes=[invGB])
        for i in range(2):
            P.op('pool', lambda e: e.memset(YT[i][:], 0.0), writes=[YT[i]])
            P.op('pool', lambda e: e.memset(VS[i][:], 1.0), writes=[VS[i]])
            P.op('pool', lambda e: e.memset(YT[i][:, 26:30, 64:66], 1.0), writes=[YT[i]])
        P.op('pool', lambda e: e.memset(gate[:], -1e30), writes=[gate])
        P.op('pool', lambda e: e.memset(kmeanT[:], 0.0), writes=[kmeanT])
        P.op('pool', lambda e: e.memset(cqn[:], 0.0), writes=[cqn])
        P.op('pool', lambda e: e.memset(vmem[:], 1.0), writes=[vmem])

        lam_init = 0.8 - 0.6 * math.exp(-0.3 * l)
        lv = _v(vecs, "lam")
        P.op('dve', lambda e: e.tensor_tensor(r1[:, 0:32], lv[:, 0:32], lv[:, 32:64], ALU.mult), reads=[vecs], writes=[r1])
        P.op('dve', lambda e: e.tensor_tensor(r1[:, 32:64], lv[:, 64:96], lv[:, 96:128], ALU.mult), reads=[vecs, r1], writes=[r1])
        P.op('dve', lambda e: e.tensor_reduce(lamt[:], r1[:, 0:64].rearrange("p (a b) -> p a b", b=32), AX.X, ALU.add), reads=[r1], writes=[lamt])
        P.op('act', lambda e: e.activation(lamt[:], lamt[:], AF.Exp), reads=[lamt], writes=[lamt])
        P.op('dve', lambda e: e.tensor_tensor(lamn[:], lamt[:, 1:2], lamt[:, 0:1], ALU.subtract), reads=[lamt], writes=[lamn])
        P.op('dve', lambda e: e.tensor_scalar(lamn[:], lamn[:], -lam_init, None, ALU.add), reads=[lamn], writes=[lamn])
        P.op('dve', lambda e: e.tensor_scalar(gsub[:], _v(vecs, "diff_sub"), 1.0 - lam_init, None, ALU.mult), reads=[vecs], writes=[gsub])

        def rstd_from(ss, invG, rs, n):
            P.op('dve', lambda e: e.tensor_tensor(rs[:, 0:n], ss[:, 0:n], invG[:, 0:n], ALU.mult), reads=[ss, invG], writes=[rs])
            P.op('act', lambda e: e.activation(rs[:, 0:n], rs[:, 0:n], AF.Ln, bias=epsb[:, 0:1]), reads=[rs, epsb], writes=[rs])
            P.op('act', lambda e: e.activation(rs[:, 0:n], rs[:, 0:n], AF.Exp, scale=-0.5), reads=[rs], writes=[rs])

        def rope(buf, key, nvec, G, half, cs_lo, t, tmp_keys):
            v = buf.rearrange("p (n g) -> p n g", g=G)
            y1 = v[:, :, 0:half]
            y2 = v[:, :, half:2 * half]
            cb = cosT[:, t, cs_lo:cs_lo + half].unsqueeze(1).to_broadcast([128, nvec, half])
            sb_ = sinT[:, t, cs_lo:cs_lo + half].unsqueeze(1).to_broadcast([128, nvec, half])
            tv = [x[:, 0:nvec * half].rearrange("p (n g) -> p n g", g=half) for x in (r1, r2, r3, r4)]
            P.op('dve', lambda e: e.tensor_tensor(tv[0], y1, cb, ALU.mult), reads=[key, cosT], writes=[r1])
            P.op('dve', lambda e: e.tensor_tensor(tv[1], y2, sb_, ALU.mult), reads=[key, sinT], writes=[r2])
            P.op('dve', lambda e: e.tensor_tensor(tv[2], y1, sb_, ALU.mult), reads=[key, sinT], writes=[r3])
            P.op('dve', lambda e: e.tensor_tensor(tv[3], y2, cb, ALU.mult), reads=[key, cosT], writes=[r4])
            P.op('dve', lambda e: e.tensor_tensor(y1, tv[0], tv[1], ALU.subtract), reads=[r1, r2], writes=[key])
            P.op('dve', lambda e: e.tensor_tensor(y2, tv[2], tv[3], ALU.add), reads=[r3, r4], writes=[key])

        def front_chain(src_ap, i, wt, ncols, mem_mode=False, with_stats=False):
            ch = []
            ch.append(lambda: P.op('act', lambda e: e.activation(junk[:, 0:D], xt[i][:], AF.Square, accum_out=ssx[:]), reads=[xt[i]], writes=[junk, ssx], embed=False))
            ch.append(lambda: P.op('act', lambda e: e.activation(rsx[:], ssx[:], AF.Ln, bias=epsb[:, 0:1], scale=1.0 / D), reads=[ssx, epsb], writes=[rsx]))
            ch.append(lambda: P.op('act', lambda e: e.activation(rsx[:], rsx[:], AF.Exp, scale=-0.5), reads=[rsx], writes=[rsx]))
            ch.append(lambda: P.op('act', lambda e: e.activation(xs[i][:], xt[i][:], AF.Copy, scale=rsx[:, 0:1]), reads=[xt[i], rsx], writes=[xs[i]]))

            def tr():
                for k in range(8):
                    P.op('pe', lambda e: e.transpose(pT[0][:, k, :], xs[i][:, k * 128:(k + 1) * 128], idb[:]), reads=[xs[i], idb], writes=[pT[0]])
            ch.append(tr)
            gn = gme if mem_mode else gat
            for k_ in range(8):
                def ev(k=k_):
                    P.op('act', lambda e: e.activation(xnT[i][:, k, :], pT[0][:, k, :], AF.Copy, scale=gn[:, k:k + 1]), reads=[pT[0], gn], writes=[xnT[i]])
                ch.append(ev)
            nchunk = (ncols + 511) // 512
            for c_ in range(nchunk):
                def mm(c=c_):
                    c0, c1 = c * 512, min(ncols, (c + 1) * 512)
                    ph = pH[c % 2]
                    for k in range(8):
                        P.op('pe', lambda e: e.matmul(ph[:, 0:c1 - c0], xnT[i][:, k, :], wt[:, k, c0:c1], start=(k == 0), stop=(k == 7)),
                             reads=[xnT[i], (wt, k)] if not mem_mode else [xnT[i], wt], writes=[ph])

                def evh(c=c_):
                    c0, c1 = c * 512, min(ncols, (c + 1) * 512)
                    ph = pH[c % 2]
                    P.op('act', lambda e: e.copy(hsb[i][:, c0:c1], ph[:, 0:c1 - c0]), reads=[ph], writes=[hsb[i]])
                ch.append(mm)
                ch.append(evh)
            if with_stats:
                h = hsb[i]
                ch.append(lambda: P.op('act', lambda e: e.activation(sqh[:], h[:], AF.Square), reads=[h], writes=[sqh]))
                for (c0_, c1_, G_, a__) in ((C_CQ, C_CKV, 192, 0), (C_CKV, C_KR, 128, 1), (C_KR, C_FQ, 32, 2), (C_FQ, C_FV, 64, 3),
                                            (C_MQ, C_MV, 64, 11), (C_DQ, C_DV, 32, 19), (C_EQ, DIN, 64, 35)):
                    def red(c0=c0_, c1=c1_, G=G_, a_=a__):
                        n = (c1 - c0) // G
                        P.op('dve', lambda e: e.tensor_reduce(ssA[:, a_:a_ + n], sqh[:, c0:c1].rearrange("p (n g) -> p n g", g=G), AX.X, ALU.add),
                             reads=[sqh], writes=[ssA])
                    ch.append(red)
                rs_i = rsAs[i]
                ch.append(lambda: P.op('dve', lambda e: e.tensor_tensor(rs_i[:, 0:39], ssA[:, 0:39], invGA[:, 0:39], ALU.mult), reads=[ssA, invGA], writes=[rs_i]))
                ch.append(lambda: P.op('act', lambda e: e.activation(rs_i[:, 0:39], rs_i[:, 0:39], AF.Ln, bias=epsb[:, 0:1]), reads=[rs_i, epsb], writes=[rs_i]))
                ch.append(lambda: P.op('act', lambda e: e.activation(rs_i[:, 0:39], rs_i[:, 0:39], AF.Exp, scale=-0.5), reads=[rs_i], writes=[rs_i]))
            return ch

        def tile_front(src_ap, i, wt, ncols, mem_mode=False):
            P.dma('act', xt[i][:], src_ap, writes=[xt[i]])
            for f_ in front_chain(src_ap, i, wt, ncols, mem_mode):
                f_()

        for mt in range(2):
            i = mt % 2
            tile_front(din["mem"][mt * 128:(mt + 1) * 128, :], i, wmk, 512, mem_mode=True)
            h = hsb[i]
            P.op('act', lambda e: e.activation(sqh[:, 0:256], h[:, 0:256], AF.Square), reads=[h], writes=[sqh])
            P.op('dve', lambda e: e.tensor_reduce(ssA[:, 0:4], sqh[:, 0:256].rearrange("p (n g) -> p n g", g=64), AX.X, ALU.add), reads=[sqh], writes=[ssA])
            P.op('act', lambda e: e.activation(rsA[:, 0:4], ssA[:, 0:4], AF.Ln, bias=epsb[:, 0:1], scale=1.0 / 64), reads=[ssA, epsb], writes=[rsA])
            P.op('act', lambda e: e.activation(rsA[:, 0:4], rsA[:, 0:4], AF.Exp, scale=-0.5), reads=[rsA], writes=[rsA])
            kv3 = h[:, 0:256].rearrange("p (n g) -> p n g", g=64)
            t3 = t512[:, 0:256].rearrange("p (n g) -> p n g", g=64)
            P.op('dve', lambda e: e.tensor_tensor(t3, kv3, rsA[:, 0:4].unsqueeze(2).to_broadcast([128, 4, 64]), ALU.mult), reads=[h, rsA], writes=[t512])
            y3 = YT[0][:, 0:2, :].rearrange("p b (n g) -> p (b n) g", g=64)
            P.op('dve', lambda e: e.tensor_tensor(y3, t3, _v(vecs, "mem_k").unsqueeze(1).to_broadcast([128, 4, 64]), ALU.mult), reads=[t512, vecs], writes=[YT[0]])
            for b in range(2):
                P.op('pe', lambda e: e.transpose(pT[1][:, b, :], YT[0][:, b, :], idb[:]), reads=[YT[0], idb], writes=[pT[1]])
            P.op('act', lambda e: e.copy(ktm[:, :, mt * 128:(mt + 1) * 128], pT[1][:, 0:2, :]), reads=[pT[1]], writes=[ktm])
            P.op('pool', lambda e: e.tensor_copy(vmem[:, mt, :, 0:64], h[:, 256:512].rearrange("p (n g) -> p n g", g=64)), reads=[h], writes=[vmem])

        rt = {nm: [sbuf(f"rt_{nm}{q}", [128, 64], F32) for q in range(4)] for nm in ("kr", "mo", "df", "qa")}
        t512f = sbuf("t512f", [128, 512], F32)
        t512m = sbuf("t512m", [128, 512], F32)
        t512d = sbuf("t512d", [128, 512], F32)
        t512e = sbuf("t512e", [128, 256], F32)

        def rope_ops(E, buf, key, nvec, G, half, cs_lo, t, tmps):
            v = buf.rearrange("p (n g) -> p n g", g=G)
            y1 = v[:, :, 0:half]
            y2 = v[:, :, half:2 * half]
            cb = cosT[:, t, cs_lo:cs_lo + half].unsqueeze(1).to_broadcast([128, nvec, half])
            sb_ = sinT[:, t, cs_lo:cs_lo + half].unsqueeze(1).to_broadcast([128, nvec, half])
            tv = [x[:, 0:nvec * half].rearrange("p (n g) -> p n g", g=half) for x in tmps]
            return [
                lambda: P.op(E, lambda e: e.tensor_tensor(tv[0], y1, cb, ALU.mult), reads=[key, cosT], writes=[tmps[0]]),
                lambda: P.op(E, lambda e: e.tensor_tensor(tv[1], y2, sb_, ALU.mult), reads=[key, sinT], writes=[tmps[1]]),
                lambda: P.op(E, lambda e: e.tensor_tensor(tv[2], y1, sb_, ALU.mult), reads=[key, sinT], writes=[tmps[2]]),
                lambda: P.op(E, lambda e: e.tensor_tensor(tv[3], y2, cb, ALU.mult), reads=[key, cosT], writes=[tmps[3]]),
                lambda: P.op(E, lambda e: e.tensor_tensor(y1, tv[0], tv[1], ALU.subtract), reads=[tmps[0], tmps[1]], writes=[key]),
                lambda: P.op(E, lambda e: e.tensor_tensor(y2, tv[2], tv[3], ALU.add), reads=[tmps[2], tmps[3]], writes=[key]),
            ]

        def post(t, extra_chain):
            i = t % 2
            h = hsb[i]
            yt = YT[i]
            vs = VS[i]
            nb = t // 2
            rsA = rsAs[i]

            qa3 = qa[:].rearrange("p (n g) -> p n g", g=96)
            kv3 = kva[:].rearrange("p (n g) -> p n g", g=128)
            q3 = sqb[:, 0:384].rearrange("p (n g) -> p n g", g=96)
            k3 = sqk[:].rearrange("p (n g) -> p n g", g=128)
            t3q = t512[:, 0:256].rearrange("p (n g) -> p n g", g=64)
            t3k = t512[:, 256:512].rearrange("p (n g) -> p n g", g=64)

            def mla_pe():
                for b_ in range(3):
                    P.op('pe', lambda e: e.transpose(pT[1][:, b_, :], cqn[:, b_, :], idb[:]), reads=[cqn, idb], writes=[pT[1]])
                P.op('act', lambda e: e.copy(cT[:], pT[1][:, 0:3, :]), reads=[pT[1]], writes=[cT])
                P.op('pe', lambda e: e.matmul(pQA[:, 0:384], cT[:, 0, :], wuq[:, 0, :], start=True, stop=False), reads=[cT, wuq], writes=[pQA])
                P.op('pe', lambda e: e.matmul(pQA[:, 0:384], cT[0:64, 1, :], wuq[0:64, 1, :], start=False, stop=True), reads=[cT, wuq], writes=[pQA])
                P.op('pe', lambda e: e.matmul(pKVA[:], cT[:, 2, :], wukv[:], start=True, stop=True), reads=[cT, wukv], writes=[pKVA])
                P.op('act', lambda e: e.copy(qa[:], pQA[:, 0:384]), reads=[pQA], writes=[qa])
                P.op('act', lambda e: e.copy(kva[:], pKVA[:]), reads=[pKVA], writes=[kva])
                P.op('act', lambda e: e.activation(sqb[:, 0:384], qa[:], AF.Square), reads=[qa], writes=[sqb])
                P.op('act', lambda e: e.activation(sqk[:], kva[:], AF.Square), reads=[kva], writes=[sqk])
                P.op('pool', lambda e: e.tensor_copy(vs[:, 0:4, 0:64], kv3[:, :, 64:128]), reads=[kva], writes=[(vs, 0)])
            ch_mla = [
                lambda: P.op('dve', lambda e: e.tensor_scalar(cqn[:, 0, :], h[:, 0:128], rsA[:, 0:1], None, ALU.mult), reads=[h, rsA], writes=[cqn]),
                lambda: P.op('dve', lambda e: e.tensor_scalar(cqn[:, 1, 0:64], h[:, 128:192], rsA[:, 0:1], None, ALU.mult), reads=[h, rsA], writes=[cqn]),
                lambda: P.op('dve', lambda e: e.tensor_scalar(cqn[:, 2, :], h[:, C_CKV:C_KR], rsA[:, 1:2], None, ALU.mult), reads=[h, rsA], writes=[cqn]),
                mla_pe,
            ]
            ch_mla2 = [
                lambda: P.op('dve', lambda e: e.tensor_reduce(ssB[:, 0:4], q3[:, :, 0:32], AX.X, ALU.add), reads=[sqb], writes=[ssB]),
                lambda: P.op('dve', lambda e: e.tensor_reduce(ssB[:, 4:8], q3[:, :, 32:96], AX.X, ALU.add), reads=[sqb], writes=[ssB]),
                lambda: P.op('dve', lambda e: e.tensor_reduce(ssB[:, 8:12], k3[:, :, 0:64], AX.X, ALU.add), reads=[sqk], writes=[ssB]),
                lambda: rstd_from(ssB, invGB, rsB, 12),
                lambda: P.op('dve', lambda e: e.tensor_tensor(qar[:], qa3[:, :, 0:32], rsB[:, 0:4].unsqueeze(2).to_broadcast([128, 4, 32]), ALU.mult), reads=[qa, rsB], writes=[qar]),
                lambda: P.op('dve', lambda e: e.tensor_tensor(t3q, qa3[:, :, 32:96], rsB[:, 4:8].unsqueeze(2).to_broadcast([128, 4, 64]), ALU.mult), reads=[qa, rsB], writes=[(t512, 0)]),
                lambda: P.op('dve', lambda e: e.tensor_tensor(t3k, kv3[:, :, 0:64], rsB[:, 8:12].unsqueeze(2).to_broadcast([128, 4, 64]), ALU.mult), reads=[kva, rsB], writes=[(t512, 1)]),
                lambda: P.op('dve', lambda e: e.tensor_tensor(qar[:], qar[:], _v(vecs, "mla_q", 0, 32).unsqueeze(1).to_broadcast([128, 4, 32]), ALU.mult), reads=[qar, vecs], writes=[qar]),
                lambda: P.op('dve', lambda e: e.tensor_tensor(yt[:, 0:4, 32:96], t3q, _v(vecs, "mla_q", 32, 96).unsqueeze(1).to_broadcast([128, 4, 64]), ALU.mult), reads=[(t512, 0), vecs], writes=[(yt, "mq")]),
                lambda: P.op('dve', lambda e: e.tensor_tensor(yt[:, 4:8, 32:96], t3k, _v(vecs, "mla_k", 32, 96).unsqueeze(1).to_broadcast([128, 4, 64]), ALU.mult), reads=[(t512, 1), vecs], writes=[(yt, "mk")]),
            ] + rope_ops('dve', qar[:].rearrange("p n g -> p (n g)"), qar, 4, 32, 16, 0, t, rt["qa"]) + [
                lambda: P.op('dve', lambda e: e.tensor_copy(yt[:, 0:4, 0:32], qar[:]), reads=[qar], writes=[(yt, "mqr")]),
            ]
            ch_kr = [
                lambda: P.op('dve', lambda e: e.scalar_tensor_tensor(krr[:], h[:, C_KR:C_FQ], rsA[:, 2:3], _v(vecs, "mla_k", 0, 32), ALU.mult, ALU.mult),
                             reads=[h, rsA, vecs], writes=[krr]),
            ] + rope_ops('dve', krr[:], krr, 1, 32, 16, 0, t, rt["kr"]) + [
                lambda: P.op('dve', lambda e: e.tensor_copy(yt[:, 4:8, 0:32], krr[:].unsqueeze(1).to_broadcast([128, 4, 32])), reads=[krr], writes=[(yt, "kr")]),
            ]
            t3f = t512f[:].rearrange("p (n g) -> p n g", g=64)
            q0 = (t // 4) * 4

            def fox_gate_mid():
                P.op('act', lambda e: e.activation(ef[:], zf[:], AF.Exp, scale=-1.0), reads=[zf], writes=[ef])
                P.op('act', lambda e: e.activation(spf[:], ef[:], AF.Ln, bias=1.0), reads=[ef], writes=[spf])
                P.op('pe', lambda e: e.matmul(pM[:, 0:4], trif[:], spf[:], start=True, stop=True), reads=[trif, spf], writes=[pM])
                P.op('pe', lambda e: e.matmul(pM[:, 8:12], ones32[:], spf[:], start=True, stop=True), reads=[ones32, spf], writes=[pM])
            ch_fox = [
                lambda: P.op('dve', lambda e: e.tensor_tensor(zf[:], h[:, C_FF:C_FF + 4], _v(vecs, "fox_b"), ALU.add), reads=[h, vecs], writes=[zf]),
                fox_gate_mid,
                lambda: P.op('dve', lambda e: e.tensor_tensor(t3f, h[:, C_FQ:C_FV].rearrange("p (n g) -> p n g", g=64), rsA[:, 3:11].unsqueeze(2).to_broadcast([128, 8, 64]), ALU.mult), reads=[h, rsA], writes=[t512f]),
                lambda: P.op('dve', lambda e: e.tensor_tensor(yt[:, 8:12, 0:64], t3f[:, 0:4, :], _v(vecs, "fox_q").unsqueeze(1).to_broadcast([128, 4, 64]), ALU.mult), reads=[t512f, vecs], writes=[(yt, "fq")]),
                lambda: P.op('dve', lambda e: e.tensor_tensor(yt[:, 26:30, 0:64], t3f[:, 4:8, :], _v(vecs, "fox_k").unsqueeze(1).to_broadcast([128, 4, 64]), ALU.mult), reads=[t512f, vecs], writes=[(yt, "fk")]),
                lambda: P.op('dve', lambda e: e.tensor_tensor(cpos[:, t, :], pM[:, 0:4], rtot[:, t, :], ALU.add), reads=[pM, rtot], writes=[cpos]),
                lambda: P.op('dve', lambda e: e.tensor_tensor(rtot[:, t + 1, :], pM[:, 8:12], rtot[:, t, :], ALU.add), reads=[pM, rtot], writes=[rtot]),
                lambda: P.op('dve', lambda e: e.tensor_tensor(fqd[:], rtot[:, q0, :], cpos[:, t, :], ALU.subtract), reads=[rtot, cpos], writes=[fqd]),
                lambda: P.op('dve', lambda e: e.tensor_scalar(yt[:, 8:12, 64], fqd[:], 8.0, None, ALU.mult), reads=[fqd], writes=[(yt, "fh")]),
                lambda: P.op('dve', lambda e: e.tensor_copy(fqh[:], yt[:, 8:12, 64]), reads=[(yt, "fh")], writes=[fqh]),
                lambda: P.op('dve', lambda e: e.scalar_tensor_tensor(yt[:, 8:12, 65], fqd[:], 8.0, fqh[:], ALU.mult, ALU.subtract), reads=[fqd, fqh], writes=[(yt, "fl")]),
            ]
            t3m = t512m[:].rearrange("p (n g) -> p n g", g=64)
            mo, _ = VOFF["moba_q"]
            gm2 = vecs[:, mo:mo + 128].rearrange("p (a g) -> p a g", g=64).unsqueeze(2).to_broadcast([128, 2, 4, 64])

            def moba_gate_pe():
                for pr in range(2):
                    P.op('pe', lambda e: e.matmul(pM[:, 16 + pr:17 + pr], mqr[:, 256 + pr * 128:256 + (pr + 1) * 128], c256[:], start=True, stop=True),
                         reads=[mqr, c256], writes=[pM])
                if nb >= 4:
                    for pr in range(2):
                        P.op('pe', lambda e: e.transpose(pQ32[:, pr, :], mqr[:, pr * 128:(pr + 1) * 128], idf[:]), reads=[mqr, idf], writes=[pQ32])
                    P.op('act', lambda e: e.copy(qT32[:], pQ32[:]), reads=[pQ32], writes=[qT32])
                    for hh in range(4):
                        pr, r0 = hh // 2, (hh % 2) * 64
                        P.op('pe', lambda e: e.matmul(pM[:, 32 + hh * 16:32 + hh * 16 + 16], qT32[r0:r0 + 64, pr, :], kmeanT[r0:r0 + 64, pr, :], start=True, stop=True),
                             reads=[qT32, kmeanT], writes=[pM])
            ch_moba = [
                lambda: P.op('dve', lambda e: e.tensor_tensor(t3m, h[:, C_MQ:C_MV].rearrange("p (n g) -> p n g", g=64), rsA[:, 11:19].unsqueeze(2).to_broadcast([128, 8, 64]), ALU.mult), reads=[h, rsA], writes=[t512m]),
                lambda: P.op('dve', lambda e: e.tensor_tensor(mqr[:].rearrange("p (a n g) -> p a n g", a=2, g=64),
                                                              t512m[:].rearrange("p (a n g) -> p a n g", a=2, g=64), gm2, ALU.mult), reads=[t512m, vecs], writes=[mqr]),
            ] + rope_ops('dve', mqr[:], mqr, 8, 64, 8, 16, t, rt["mo"]) + [
                moba_gate_pe,
                lambda: P.op('dve', lambda e: e.tensor_copy(yt[:, 12:20, 0:64], mqr[:].rearrange("p (n g) -> p n g", g=64)), reads=[mqr], writes=[(yt, "mo")]),
            ]
            if t % 2 == 0:
                ch_moba.append(lambda: P.op('dve', lambda e: e.tensor_copy(kacc[:], pM[:, 16:18]), reads=[pM], writes=[kacc]))
            else:
                ch_moba.append(lambda: P.op('dve', lambda e: e.tensor_tensor(kmeanT[:, :, nb], pM[:, 16:18], kacc[:], ALU.add), reads=[pM, kacc], writes=[kmeanT]))
            if nb >= 4:
                ch_moba.append(lambda: P.op('dve', lambda e: e.tensor_copy(gate[:, :, 0:nb], pM[:, 32:96].rearrange("p (n g) -> p n g", g=16)[:, :, 0:nb]), reads=[pM], writes=[gate]))
                for hh_ in range(4):
                    def gsel(hh=hh_):
                        P.op('dve', lambda e: e.max(max8[:, hh, :], gate[:, hh, :]), reads=[gate], writes=[(max8, hh)])
                        P.op('dve', lambda e: e.tensor_scalar(yt[:, 12 + hh, 64:64 + nb], gate[:, hh, 0:nb], max8[:, hh, 2:3], NEG, ALU.is_lt, ALU.mult),
                             reads=[gate, (max8, hh)], writes=[(yt, "mg")])
                    ch_moba.append(gsel)
            t3d = t512d[:].rearrange("p (n g) -> p n g", g=32)
            do, _ = VOFF["diff_q"]
            gd2 = vecs[:, do:do + 64].rearrange("p (a g) -> p a g", g=32).unsqueeze(2).to_broadcast([128, 2, 8, 32])
            t3e = t512e[:].rearrange("p (n g) -> p n g", g=64)
            ch_pool = [
                lambda: P.op('pool', lambda e: e.tensor_tensor(t3d, h[:, C_DQ:C_DV].rearrange("p (n g) -> p n g", g=32), rsA[:, 19:35].unsqueeze(2).to_broadcast([128, 16, 32]), ALU.mult), reads=[h, rsA], writes=[t512d]),
                lambda: P.op('pool', lambda e: e.tensor_tensor(dqr[:].rearrange("p (a n g) -> p a n g", a=2, g=32),
                                                               t512d[:].rearrange("p (a n g) -> p a n g", a=2, g=32), gd2, ALU.mult), reads=[t512d, vecs], writes=[dqr]),
            ] + rope_ops('pool', dqr[:], dqr, 16, 32, 4, 24, t, rt["df"]) + [
                lambda: P.op('pool', lambda e: e.tensor_copy(yt[:, 20:24, :].rearrange("p b c -> p (b c)"), dqr[:]), reads=[dqr], writes=[(yt, "df")]),
                lambda: P.op('pool', lambda e: e.tensor_tensor(t3e, h[:, C_EQ:DIN].rearrange("p (n g) -> p n g", g=64), rsA[:, 35:39].unsqueeze(2).to_broadcast([128, 4, 64]), ALU.mult), reads=[h, rsA], writes=[t512e]),
                lambda: P.op('pool', lambda e: e.tensor_tensor(yt[:, 24:26, :].rearrange("p b (n g) -> p (b n) g", g=64), t3e,
                                                               _v(vecs, "mem_q").unsqueeze(1).to_broadcast([128, 4, 64]), ALU.mult), reads=[t512e, vecs], writes=[(yt, "eq")]),
                lambda: P.op('pool', lambda e: e.memset(yt[:, 16:20, 64:80], 0.0), writes=[(yt, "oh")]),
                lambda: P.op('pool', lambda e: e.memset(yt[:, 16:20, 64 + nb:65 + nb], 1.0), writes=[(yt, "oh")]),
            ]
            ch_v = []
            for (c0_, hb_) in ((C_FV, 4), (C_MV, 8), (C_DV, 12)):
                def vcopy(c0=c0_, hb=hb_):
                    P.op('act', lambda e: e.copy(vs[:, hb:hb + 4, 0:64], h[:, c0:c0 + 256].rearrange("p (n g) -> p n g", g=64)), reads=[h], writes=[(vs, hb)])
                ch_v.append(vcopy)
            ch_mla = ch_mla + [lambda: None] * 3 + ch_mla2
            chains = [extra_chain, ch_mla, ch_fox, ch_moba, ch_kr, ch_pool, ch_v]
            idx = [0] * len(chains)
            live = True
            while live:
                live = False
                for ci, ch in enumerate(chains):
                    if idx[ci] < len(ch):
                        ch[idx[ci]]()
                        idx[ci] += 1
                        live = True
            ykeys = [(yt, k_) for k_ in ("mq", "mk", "mqr", "kr", "fq", "fk", "fh", "fl", "mo", "mg", "df", "eq", "oh")] + [yt]
            fts = FTS[i]
            for grp in range(4):
                b0, b1 = grp * 8, min(NBLK, grp * 8 + 8)
                pt = pT[grp % 2]
                for b_ in range(b0, b1):
                    P.op('pe', lambda e: e.transpose(pt[:, b_ - b0, :], yt[:, b_, :], idb[:]), reads=ykeys + [idb], writes=[pt])
                if grp % 2 == 0:
                    P.op('act', lambda e: e.copy(fts[:, b0:b1, :], pt[:, 0:b1 - b0, :]), reads=[pt], writes=[fts])
                else:
                    P.op('dve', lambda e: e.tensor_copy(fts[:, b0:b1, :], pt[:, 0:b1 - b0, :]), reads=[pt], writes=[fts])
            P.dma('sp', FTD[:, :, t * 128:(t + 1) * 128].rearrange("b p t -> p b t"), fts[:], reads=[fts], writes=["FTD"])
            vkeys = [(vs, 0), (vs, 4), (vs, 8), (vs, 12), vs]
            P.dma('sp', VD[:, :, t, :].rearrange("h p c -> p h c"), vs[:], reads=vkeys, writes=["VD"])
            return ykeys, vkeys

        P.barrier()
        P.dma('act', xt[0][:], xin[0:128, :], writes=[xt[0]])
        P.dma('act', xt[1][:], xin[128:256, :], writes=[xt[1]])
        for f_ in front_chain(None, 0, win, DIN, with_stats=True):
            f_()
        for t in range(NT):
            nxt = front_chain(None, (t + 1) % 2, win, DIN, with_stats=True) if t + 1 < NT else []
            if t + 2 < NT:
                nxt = [lambda t=t: P.dma('act', xt[t % 2][:], xin[(t + 2) * 128:(t + 3) * 128, :], writes=[xt[t % 2]])] + nxt
            post(t, nxt)
        for hh in range(4):
            for Q in range(8):
                n = 4 * Q + 4
                P.op('dve', lambda e: e.tensor_scalar(fbias[:, hh, Q, 0:n], cpos[:, 0:n, hh], rtot[:, 4 * Q, hh:hh + 1], None, ALU.subtract),
                     reads=[cpos, rtot], writes=[fbias])
        P.barrier()


def phase2(nc, P, din, l, FTD, VD, MIXT, C, mid_hook=None):
    idb, trib, mhalf, fbias, vecs = C["idb"], C["trib"], C["mhalf"], C["fbias"], C["vecs"]
    gsub, lamn, ktm, vmem = C["gsub"], C["lamn"], C["ktm"], C["vmem"]
    epsb = C["epsb"]
    with ExitStack() as st:
        def sbuf(name, shape, dt):
            return st.enter_context(nc.sbuf_tensor(f"{name}_a{l}", shape, dt))

        def psum(name, shape, dt):
            return st.enter_context(nc.psum_tensor(f"{name}_a{l}", shape, dt))
        QT = [[sbuf(f"QT{i}{m}", [128, S], BF16) for m in range(2)] for i in range(2)]
        KT = [[sbuf(f"KT{i}{m}", [128, S], BF16) for m in range(2)] for i in range(2)]
        VV = [sbuf(f"VV{i}", [128, NT, 65], BF16) for i in range(2)]
        PT = [sbuf(f"PT{i}", [128, 512], BF16) for i in range(3)]
        rec = sbuf("rec", [128, 8], F32)
        osb = [sbuf(f"osb{i}", [128, 4, 64], BF16) for i in range(2)]
        oT = [sbuf(f"oT{i}", [64, 512], BF16) for i in range(2)]
        da = sbuf("da", [128, 4, 64], F32)
        db = sbuf("db", [128, 4, 64], F32)
        dsq = sbuf("dsq", [128, 4, 64], F32)
        dss = sbuf("dss", [128, 4], F32)
        pS = [psum(f"pS{i}", [128, 512], F32) for i in range(3)]
        pO = [[psum(f"pO{i}{m}", [128, 4, 128], F32) for m in range(2)] for i in range(2)]
        pTO = psum("pTO", [128, 4, 128], BF16)

        units = []
        for h in range(4):
            units.append(dict(kind="mla", rows=96, q=[(h, 0)], k=[(4 + h, 0)], v=h, scale=96 ** -0.5, out=0 * 256 + h * 64))
        for h in range(4):
            units.append(dict(kind="fox", rows=66, q=[(8 + h, 0)], k=[(26 + h, 0)], v=4 + h, scale=0.125, out=256 + h * 64, h=h))
        for h in range(4):
            units.append(dict(kind="moba", rows=80, q=[(12 + h, 0)], k=[(16 + h, 0)], v=8 + h, scale=0.125, out=512 + h * 64))
        for h in range(4):
            r0 = (h % 2) * 64
            units.append(dict(kind="diff", rows=32, q=[(20 + h // 2, r0), (20 + h // 2, r0 + 32)], k=[(22 + h // 2, r0), (22 + h // 2, r0 + 32)],
                              v=12 + h, scale=32 ** -0.5, out=768 + h * 64))
        for h in range(4):
            units.append(dict(kind="mem", rows=64, q=[(24 + h // 2, (h % 2) * 64)], k=None, v=None, scale=0.125, out=1024 + h * 64, h=h))

        def load_unit(u, i):
            rows = u["rows"]
            base = (u["h"] % 2) * 64 if u["kind"] == "mem" else 0
            for m, (blk, r0) in enumerate(u["q"]):
                P.dma('sp', QT[i][m][base:base + rows, :], FTD[blk, r0:r0 + rows, :], reads=["FTD"], writes=[QT[i][m]])
            if u["k"] is not None:
                for m, (blk, r0) in enumerate(u["k"]):
                    P.dma('sp', KT[i][m][0:rows, :], FTD[blk, r0:r0 + rows, :], reads=["FTD"], writes=[KT[i][m]])
                P.dma('sp', VV[i][:], VD[u["v"]], reads=["VD"], writes=[VV[i]])

        if DBG_UNITS is not None:
            units = [units[k] for k in DBG_UNITS]
        PT4 = PT + [sbuf("PT3", [128, 512], BF16)]
        gstep = [0]
        qglob = [0]
        load_unit(units[0], 0)
        for ui, u in enumerate(units):
            i = ui % 2
            if ui + 1 < len(units):
                load_unit(units[ui + 1], (ui + 1) % 2)
            if mid_hook is not None and ui == min(6, len(units) - 1):
                mid_hook()
            rows, kind, scale = u["rows"], u["kind"], u["scale"]
            nmap = len(u["q"])
            base = (u["h"] % 2) * 64 if kind == "mem" else 0
            steps = []
            for Q in range(8):
                nkt = 2 if kind == "mem" else 4 * Q + 4
                for j in range(nkt):
                    for m in range(nmap):
                        steps.append((Q, j, m, nkt))
            pobuf = {}
            for Q in range(8):
                pobuf[Q] = pO[qglob[0] % 2]
                qglob[0] += 1
            first = {}
            bufs = {}
            deferred = []

            def front(s):
                Q, j, m, nkt = steps[s]
                g = j - 4 * Q if (kind != "mem" and j >= 4 * Q) else None
                c0 = g * 128 if g is not None else 0
                ps = pS[gstep[0] % 3]
                pt = PT4[gstep[0] % 4]
                gstep[0] += 1
                bufs[s] = pt
                if kind == "mem":
                    lhsT = ktm[base:base + 64, u["h"] // 2, j * 128:(j + 1) * 128]
                    kkey = ktm
                else:
                    lhsT = KT[i][m][0:rows, j * 128:(j + 1) * 128]
                    kkey = KT[i][m]
                rhs = QT[i][m][base:base + rows, Q * 512 + c0:(Q + 1) * 512]
                P.op('pe', lambda e: e.matmul(ps[:, c0:512], lhsT, rhs, start=True, stop=True), reads=[kkey, QT[i][m]], writes=[ps])
                bias = fbias[:, u["h"], Q, j:j + 1] if kind == "fox" else 0.0
                P.op('act', lambda e: e.activation(pt[:, c0:512], ps[:, c0:512], AF.Exp, bias=bias, scale=scale),
                     reads=[ps, fbias] if kind == "fox" else [ps], writes=[pt])
                if g is not None:
                    P.op('pool', lambda e: e.tensor_tensor(pt[:, c0:c0 + 128], pt[:, c0:c0 + 128], trib[:], ALU.mult), reads=[pt, trib], writes=[pt])

            def back(s):
                Q, j, m, nkt = steps[s]
                g = j - 4 * Q if (kind != "mem" and j >= 4 * Q) else None
                pt = bufs.pop(s)
                po = pobuf[Q]
                if kind == "mem":
                    vap = vmem[:, j, u["h"], :]
                    vkey = vmem
                else:
                    vap = VV[i][:, j, :]
                    vkey = VV[i]
                for gp in range(g if g is not None else 0, 4):
                    st_ = first.get((Q, m), True)
                    P.op('pe', lambda e: e.matmul(po[m][:, gp, 0:65], pt[:, gp * 128:(gp + 1) * 128], vap, start=st_, stop=(j == nkt - 1 and gp == 3),
                                                  skip_group_check=True),
                         reads=[pt, vkey], writes=[po[m]])
                    first[(Q, m)] = False
                if j == nkt - 1 and m == nmap - 1:
                    evac_a(Q, s)

            def evac_a(Q, s):
                po = pobuf[Q]
                ob = osb[Q % 2]
                if kind != "diff":
                    P.op('dve', lambda e: e.reciprocal(rec[:, 0:4], po[0][:, :, 64]), reads=[po[0]], writes=[rec])
                    P.op('dve', lambda e: e.tensor_tensor(ob[:], po[0][:, :, 0:64], rec[:, 0:4].unsqueeze(2).to_broadcast([128, 4, 64]), ALU.mult),
                         reads=[po[0], rec], writes=[ob])
                    delay = 3
                else:
                    P.op('dve', lambda e: e.reciprocal(rec[:, 0:4], po[0][:, :, 64]), reads=[po[0]], writes=[rec])
                    P.op('dve', lambda e: e.reciprocal(rec[:, 4:8], po[1][:, :, 64]), reads=[po[1]], writes=[rec])
                    P.op('dve', lambda e: e.tensor_scalar(rec[:, 4:8], rec[:, 4:8], lamn[:, 0:1], None, ALU.mult), reads=[rec, lamn], writes=[rec])
                    P.op('dve', lambda e: e.tensor_tensor(da[:], po[0][:, :, 0:64], rec[:, 0:4].unsqueeze(2).to_broadcast([128, 4, 64]), ALU.mult), reads=[po[0], rec], writes=[da])
                    P.op('dve', lambda e: e.tensor_tensor(db[:], po[1][:, :, 0:64], rec[:, 4:8].unsqueeze(2).to_broadcast([128, 4, 64]), ALU.mult), reads=[po[1], rec], writes=[db])
                    P.op('dve', lambda e: e.tensor_tensor(da[:], da[:], db[:], ALU.add), reads=[da, db], writes=[da])
                    P.op('pool', lambda e: e.tensor_tensor(dsq[:], da[:], da[:], ALU.mult), reads=[da], writes=[dsq])
                    P.op('dve', lambda e: e.tensor_reduce(dss[:], dsq[:], AX.X, ALU.add), reads=[dsq], writes=[dss])
                    P.op('act', lambda e: e.activation(dss[:], dss[:], AF.Ln, bias=epsb[:, 0:1], scale=1.0 / 64), reads=[dss, epsb], writes=[dss])
                    P.op('act', lambda e: e.activation(dss[:], dss[:], AF.Exp, scale=-0.5), reads=[dss], writes=[dss])
                    P.op('dve', lambda e: e.tensor_tensor(da[:], da[:], dss[:].unsqueeze(2).to_broadcast([128, 4, 64]), ALU.mult), reads=[da, dss], writes=[da])
                    P.op('dve', lambda e: e.tensor_tensor(ob[:], da[:], gsub[:].unsqueeze(1).to_broadcast([128, 4, 64]), ALU.mult), reads=[da, gsub], writes=[ob])
                    delay = 8
                deferred.append((s + delay, Q))

            def evac_b(Q):
                ob = osb[Q % 2]
                for gp in range(4):
                    P.op('pe', lambda e: e.transpose(pTO[0:64, gp, :], ob[:, gp, :], idb[:]), reads=[ob, idb], writes=[pTO])
                ot = oT[Q % 2]
                P.op('dve', lambda e: e.tensor_copy(ot[:], pTO[0:64, :, :].rearrange("p a b -> p (a b)")), reads=[pTO], writes=[ot])
                P.dma('sp', MIXT[u["out"]:u["out"] + 64, Q * 512:(Q + 1) * 512], ot[:], reads=[ot], writes=["MIXT"])

            LAG = 3
            ns = len(steps)
            for s in range(ns + LAG):
                if s < ns:
                    front(s)
                if s - LAG >= 0:
                    back(s - LAG)
                while deferred and deferred[0][0] <= s - LAG:
                    evac_b(deferred.pop(0)[1])
            while deferred:
                evac_b(deferred.pop(0)[1])
        P.barrier()


def phase3(nc, P, din, l, xin, MIXT, X1, xout, C, wg, wu, wd, gff):
    idb, mhalf, epsb = C["idb"], C["mhalf"], C["epsb"]
    TT = 256
    ntt = S // TT
    with ExitStack() as st:
        def sbuf(name, shape, dt):
            return st.enter_context(nc.sbuf_tensor(f"{name}_o{l}", shape, dt))

        def psum(name, shape, dt):
            return st.enter_context(nc.psum_tensor(f"{name}_o{l}", shape, dt))
        wo = sbuf("wo", [128, 10, D], BF16)
        mixT = [sbuf(f"mixT{i}", [128, 10, TT], BF16) for i in range(2)]
        xr = [sbuf(f"xr{i}", [128, 2, D], F32) for i in range(2)]
        x1 = xr
        pY = [psum(f"pY{i}", [128, 512], F32) for i in range(8)]
        P.dma('pool', wo[:], din["w_o"][l].rearrange("(k p) n -> p k n", p=128), writes=[wo])

        def loads(tt, i):
            P.dma('sp', mixT[i][:], MIXT[:, tt * TT:(tt + 1) * TT].rearrange("(k p) t -> p k t", p=128), reads=["MIXT"], writes=[mixT[i]])
            P.dma('sp', xr[i][:], xin[tt * TT:(tt + 1) * TT, :].rearrange("(g p) d -> p g d", p=128), reads=["xin"], writes=[xr[i]])
        loads(0, 0)
        for tt in range(ntt):
            i = tt % 2
            if tt + 1 < ntt:
                loads(tt + 1, (tt + 1) % 2)
            for g in range(2):
                for hf in range(2):
                    py = pY[i * 4 + g * 2 + hf]
                    for k in range(10):
                        P.op('pe', lambda e: e.matmul(py[:], mixT[i][:, k, g * 128:(g + 1) * 128], wo[:, k, hf * 512:(hf + 1) * 512], start=(k == 0), stop=(k == 9)),
                             reads=[mixT[i], wo], writes=[py])
                    P.op('dve', lambda e: e.tensor_tensor(x1[i][:, g, hf * 512:(hf + 1) * 512], py[:], xr[i][:, g, hf * 512:(hf + 1) * 512], ALU.add),
                         reads=[py, xr[i]], writes=[x1[i]])
            P.dma('sp', X1[tt * TT:(tt + 1) * TT, :].rearrange("(g p) d -> p g d", p=128), x1[i][:], reads=[x1[i]], writes=["X1"])
        P.barrier()

    with ExitStack() as st:
        def sbuf(name, shape, dt):
            return st.enter_context(nc.sbuf_tensor(f"{name}_f{l}", shape, dt))

        def psum(name, shape, dt):
            return st.enter_context(nc.psum_tensor(f"{name}_f{l}", shape, dt))
        cw = sbuf("cw", [128, NCH, 3], F32)
        cb = sbuf("cb", [128, NCH], F32)
        halo = sbuf("halo", [128, NCH, 2], F32)
        x1 = [sbuf(f"x1{i}", [128, 2, D], F32) for i in range(2)]
        junk = sbuf("junk", [128, D], BF16)
        xs2 = [sbuf(f"xs{i}", [128, 2, D], BF16) for i in range(2)]
        xn2T2 = [sbuf(f"xn2T{i}", [128, 8, TT], BF16) for i in range(2)]
        ss2 = [sbuf(f"ss{i}", [128, 2], F32) for i in range(2)]
        rs2 = [sbuf(f"rs{i}", [128, 2], F32) for i in range(2)]
        gsb = [sbuf(f"gsb{i}", [128, TT + 2], F32) for i in range(2)]
        acc = [sbuf(f"acc{i}", [128, TT], F32) for i in range(2)]
        sg = [sbuf(f"sg{i}", [128, TT], F32) for i in range(2)]
        hT = sbuf("hT", [128, NCH, TT], BF16)
        pY = [psum(f"pY{i}", [128, 512], F32) for i in range(2)]
        pT = [psum(f"pT{i}", [128, 8, 128], BF16) for i in range(2)]
        pG = [psum(f"pG{i}", [128, 512], F32) for i in range(2)]
        pU = [psum(f"pU{i}", [128, 512], F32) for i in range(2)]

        P.dma('sp', cw[:], din["convw"][l], writes=[cw])
        P.dma('sp', cb[:], din["convb"][l], writes=[cb])
        P.op('pool', lambda e: e.memset(halo[:], 0.0), writes=[halo])
        wdk = [(wd, 0), (wd, 11)]

        def loads2(tt, i):
            P.dma('sp', x1[i][:], X1[tt * TT:(tt + 1) * TT, :].rearrange("(g p) d -> p g d", p=128), reads=["X1"], writes=[x1[i]])
        def norm_part(i):
            xs, ss, rs = xs2[i], ss2[i], rs2[i]
            for g in range(2):
                P.op('act', lambda e: e.activation(junk[:], x1[i][:, g, :], AF.Square, accum_out=ss[:, g:g + 1]), reads=[x1[i]], writes=[junk, ss], embed=False)
                P.op('act', lambda e: e.activation(rs[:, g:g + 1], ss[:, g:g + 1], AF.Ln, bias=epsb[:, 0:1], scale=1.0 / D), reads=[ss, epsb], writes=[rs])
                P.op('act', lambda e: e.activation(rs[:, g:g + 1], rs[:, g:g + 1], AF.Exp, scale=-0.5), reads=[rs], writes=[rs])
                P.op('dve', lambda e: e.tensor_scalar(xs[:, g, :], x1[i][:, g, :], rs[:, g:g + 1], None, ALU.mult), reads=[x1[i], rs], writes=[(xs, g)])

        def transpose_part(i):
            xs, xn2T = xs2[i], xn2T2[i]
            for g in range(2):
                for k in range(8):
                    P.op('pe', lambda e: e.transpose(pT[g][:, k, :], xs[:, g, k * 128:(k + 1) * 128], idb[:]), reads=[(xs, g), idb], writes=[pT[g]])
                for k in range(8):
                    P.op('act', lambda e: e.activation(xn2T[:, k, g * 128:(g + 1) * 128], pT[g][:, k, :], AF.Copy, scale=gff[:, k:k + 1]), reads=[pT[g], gff], writes=[xn2T])

        loads2(0, 0)
        norm_part(0)
        transpose_part(0)
        for tt in range(ntt):
            i = tt % 2
            xn2T = xn2T2[i]
            if tt + 1 < ntt:
                loads2(tt + 1, (tt + 1) % 2)
            for c in range(NCH):
                j = c % 2
                for k in range(8):
                    P.op('pe', lambda e: e.matmul(pG[j][:, 0:TT], wg[:, k, c * 128:(c + 1) * 128], xn2T[:, k, :], start=(k == 0), stop=(k == 7)),
                         reads=[(wg, k), xn2T], writes=[pG[j]])
                for k in range(8):
                    P.op('pe', lambda e: e.matmul(pU[j][:, 0:TT], wu[:, k, c * 128:(c + 1) * 128], xn2T[:, k, :], start=(k == 0), stop=(k == 7)),
                         reads=[(wu, k), xn2T], writes=[pU[j]])
                P.op('pool', lambda e: e.tensor_copy(gsb[j][:, 0:2], halo[:, c, :]), reads=[halo], writes=[gsb[j]])
                P.op('act', lambda e: e.copy(gsb[j][:, 2:TT + 2], pG[j][:, 0:TT]), reads=[pG[j]], writes=[gsb[j]])
                P.op('pool', lambda e: e.tensor_copy(halo[:, c, :], gsb[j][:, TT:TT + 2]), reads=[gsb[j]], writes=[halo])
                P.op('dve', lambda e: e.tensor_scalar(acc[j][:], gsb[j][:, 2:TT + 2], cw[:, c, 2:3], cb[:, c:c + 1], ALU.mult, ALU.add), reads=[gsb[j], cw, cb], writes=[acc[j]])
                P.op('dve', lambda e: e.scalar_tensor_tensor(acc[j][:], gsb[j][:, 1:TT + 1], cw[:, c, 1:2], acc[j][:], ALU.mult, ALU.add), reads=[gsb[j], cw, acc[j]], writes=[acc[j]])
                P.op('dve', lambda e: e.scalar_tensor_tensor(acc[j][:], gsb[j][:, 0:TT], cw[:, c, 0:1], acc[j][:], ALU.mult, ALU.add), reads=[gsb[j], cw, acc[j]], writes=[acc[j]])
                P.op('act', lambda e: e.activation(sg[j][:], acc[j][:], AF.Silu), reads=[acc[j]], writes=[sg[j]])
                P.op('dve', lambda e: e.tensor_tensor(hT[:, c, :], pU[j][:, 0:TT], sg[j][:], ALU.mult), reads=[pU[j], sg[j]], writes=[(hT, c)])
                if c == 10 and tt + 1 < ntt:
                    norm_part((tt + 1) % 2)
            if tt + 1 < ntt:
                transpose_part((tt + 1) % 2)
            for g in range(2):
                for hf in range(2):
                    py = pY[hf]
                    for c in range(NCH):
                        P.op('pe', lambda e: e.matmul(py[:], hT[:, c, g * 128:(g + 1) * 128], wd[:, c, hf * 512:(hf + 1) * 512], start=(c == 0), stop=(c == NCH - 1)),
                             reads=[(hT, c), wdk[0 if c < 11 else 1]], writes=[py])
                    P.op('dve', lambda e: e.tensor_tensor(x1[i][:, g, hf * 512:(hf + 1) * 512], py[:], x1[i][:, g, hf * 512:(hf + 1) * 512], ALU.add),
                         reads=[py, x1[i]], writes=[x1[i]])
            P.dma('sp', xout[tt * TT:(tt + 1) * TT, :].rearrange("(g p) d -> p g d", p=128), x1[i][:], reads=[x1[i]], writes=["xout"])
        P.barrier()


def _host_inputs(inputs):
    f = lambda a: np.ascontiguousarray(np.asarray(a, dtype=np.float32))
    ident = np.eye(128, dtype=np.float32)
    tri = np.triu(np.ones((128, 128), dtype=np.float32))
    theta = 500000.0
    invs = []
    for rot in (32, 16, 8):
        invs.append(theta ** (-np.arange(0, rot, 2, dtype=np.float32) / rot))
    invf = np.concatenate(invs).astype(np.float32).reshape(1, NFREQ)

    def pk(v, nk):
        v = f(v)
        return np.ascontiguousarray(v.reshape(L, nk, 128).transpose(0, 2, 1))
    g_cq = np.zeros((L, 256), np.float32)
    g_cq[:, :192] = f(inputs["mla_cq_norm"])
    vecs = np.zeros((L, 1, 1024), np.float32)

    def put(name, arr):
        o, w = VOFF[name]
        vecs[:, 0, o:o + w] = f(arr).reshape(L, w)
    put("mla_q", inputs["mla_q_norm"]); put("mla_k", inputs["mla_k_norm"]); put("fox_b", inputs["fox_b_f"])
    put("fox_q", inputs["fox_q_norm"]); put("fox_k", inputs["fox_k_norm"]); put("moba_q", inputs["moba_q_norm"])
    put("moba_k", inputs["moba_k_norm"]); put("lam", inputs["diff_lambda"]); put("diff_q", inputs["diff_q_norm"])
    put("diff_k", inputs["diff_k_norm"]); put("diff_sub", inputs["diff_sub_norm"]); put("mem_q", inputs["mem_q_norm"])
    put("mem_k", inputs["mem_k_norm"])
    convw = np.ascontiguousarray(f(inputs["ffn_conv_w"]).reshape(L, 3, NCH, 128).transpose(0, 3, 2, 1))
    convb = np.ascontiguousarray(f(inputs["ffn_conv_b"]).reshape(L, NCH, 128).transpose(0, 2, 1))
    shared = {
        "ident": ident, "tri": tri, "invf": invf,
        "w_in": f(inputs["w_in"]), "w_uq": f(inputs["mla_w_uq"]), "w_ukv": f(inputs["mla_w_ukv"]),
        "w_memkv": f(inputs["mem_w_kv"]), "w_o": f(inputs["w_o"]), "w_gate": f(inputs["ffn_w_gate"]),
        "w_up": f(inputs["ffn_w_up"]), "w_down": f(inputs["ffn_w_down"]),
        "g_attn": pk(inputs["attn_norm"], 8), "g_ffn": pk(inputs["ffn_norm"], 8), "g_mem": pk(inputs["mem_norm"], 8),
        "g_cq": pk(g_cq, 2), "g_ckv": pk(inputs["mla_ckv_norm"], 1),
        "vecs": vecs, "convw": convw, "convb": convb,
    }
    x = f(inputs["x"]); mem = f(inputs["mem"]); pos = np.asarray(inputs["positions"]).astype(np.int32)
    maps = []
    for b in range(x.shape[0]):
        m = dict(shared)
        m["x"] = x[b]
        m["mem"] = mem[b]
        m["pos"] = np.ascontiguousarray(pos[b].reshape(NT, 128).T)
        maps.append(m)
    return maps


_NC_CACHE = {}


def kernel(**inputs):
    maps = _host_inputs(inputs)
    if "nc" not in _NC_CACHE:
        _NC_CACHE["nc"] = build()
    nc = _NC_CACHE["nc"]
    res = run_bass_kernel_spmd(nc, maps, core_ids=list(range(8)))
    return np.stack([np.asarray(r["out"], dtype=np.float32) for r in res.results], axis=0)
```
